# Optimizing a Trainium2 kernel written in Bass

```python
import jax, jax.numpy as jnp
from jax import lax
import numpy as np

D_MODEL = 2048
BATCH = 8
SEQ = 2048
DEPTH = 1

HEAD_DIM = 64
ATTN_WIDTH = D_MODEL // 2
ATTN_Q_HEADS = ATTN_WIDTH // HEAD_DIM
ATTN_KV_HEADS = 4
KV_WIDTH = ATTN_KV_HEADS * HEAD_DIM
WINDOW = 128
BLOCK = 128
RWKV_WIDTH = D_MODEL - ATTN_WIDTH
RWKV_HEADS = RWKV_WIDTH // HEAD_DIM
DECAY_LORA = 64
ICLR_LORA = 64
MIX_WIDTH = ATTN_WIDTH + RWKV_WIDTH
ATTN_SIZES = [ATTN_WIDTH, KV_WIDTH, KV_WIDTH, ATTN_WIDTH]
RWKV_SHIFT_SIZES = [RWKV_WIDTH, RWKV_WIDTH, RWKV_WIDTH, DECAY_LORA, ICLR_LORA]
RWKV_SHIFT_COLS = sum(RWKV_SHIFT_SIZES)
ATTN_COLS = sum(ATTN_SIZES)
RWKV_COLS = RWKV_SHIFT_COLS + RWKV_WIDTH
IN_COLS = ATTN_COLS + RWKV_COLS
RMS_EPS = 1e-6
GN_EPS = 64e-5
NEG_BIG = -1e30

kernel_name = "hymba_swa_sink_rwkv7_hybrid"


def _split(x, sizes):
    offs = np.cumsum(sizes)[:-1].tolist()
    return jnp.split(x, offs, axis=-1)


def _rms_norm(x, g):
    xf = x.astype(jnp.float32)
    y = xf * lax.rsqrt(jnp.mean(xf * xf, axis=-1, keepdims=True) + RMS_EPS)
    return (y * g.astype(jnp.float32)).astype(x.dtype)


def _sliding_window_attention(q, k, v, sinks):
    B, T = q.shape[0], q.shape[1]
    nb = T // BLOCK
    G = ATTN_Q_HEADS // ATTN_KV_HEADS
    qb = q.reshape(B, nb, BLOCK, ATTN_KV_HEADS, G, HEAD_DIM)
    kb = k.reshape(B, nb, BLOCK, ATTN_KV_HEADS, HEAD_DIM)
    vb = v.reshape(B, nb, BLOCK, ATTN_KV_HEADS, HEAD_DIM)
    pad = ((0, 0), (1, 0), (0, 0), (0, 0), (0, 0))
    kcat = jnp.concatenate([jnp.pad(kb, pad)[:, :-1], kb], axis=2)
    vcat = jnp.concatenate([jnp.pad(vb, pad)[:, :-1], vb], axis=2)
    scale = HEAD_DIM ** -0.5
    s = jnp.einsum('bnqhgd,bnshd->bnhgqs', qb, kcat).astype(jnp.float32) * scale
    qi = jnp.arange(BLOCK)[:, None]
    si = jnp.arange(2 * BLOCK)[None, :]
    rel = si - BLOCK - qi
    in_win = (rel <= 0) & (rel > -WINDOW)
    has_prev = (jnp.arange(nb)[:, None, None] > 0) | (si[None] >= BLOCK)
    valid = in_win[None] & has_prev
    s = jnp.where(valid[None, :, None, None], s, NEG_BIG)
    sink = sinks.astype(jnp.float32).reshape(ATTN_KV_HEADS, G)[None, None, :, :, None, None]
    sink = jnp.broadcast_to(sink, s.shape[:-1] + (1,))
    p = jax.nn.softmax(jnp.concatenate([s, sink], axis=-1), axis=-1)[..., :-1]
    o = jnp.einsum('bnhgqs,bnshd->bnqhgd', p.astype(v.dtype), vcat)
    return o.reshape(B, T, ATTN_WIDTH)


def _rwkv7_step(S, inp):
    r_t, w_t, k_t, v_t, a_t, b_t = inp
    sa = jnp.einsum('bhvk,bhk->bhv', S, a_t)
    S = S * w_t[:, :, None, :] + sa[..., None] * b_t[:, :, None, :] + v_t[..., None] * k_t[:, :, None, :]
    y = jnp.einsum('bhvk,bhk->bhv', S, r_t)
    return S, y


def _rwkv7_time_mix(p, mu, w0, w_up, a0, a_up, k_k, k_a, r_k, ln_w, ln_b):
    B, T = p.shape[0], p.shape[1]
    prev = jnp.pad(p, ((0, 0), (1, 0), (0, 0)))[:, :-1]
    p = p + (prev - p) * mu
    r, k, v, wd, ad = _split(p, RWKV_SHIFT_SIZES)
    w = -jax.nn.softplus(-(w0 + jnp.tanh(wd) @ w_up)) - 0.5
    decay = jnp.exp(-jnp.exp(w.astype(jnp.float32)))
    a = jax.nn.sigmoid(a0 + ad @ a_up)
    hs = (B, T, RWKV_HEADS, HEAD_DIM)
    kk = (k * k_k).reshape(hs).astype(jnp.float32)
    kk = kk * lax.rsqrt(jnp.maximum(jnp.sum(kk * kk, -1, keepdims=True), 1e-24))
    k = k * (1 + (a - 1) * k_a)
    rh = r.reshape(hs).astype(jnp.float32)
    kh = k.reshape(hs).astype(jnp.float32)
    vh = v.reshape(hs).astype(jnp.float32)
    ah = a.reshape(hs).astype(jnp.float32)
    xs = tuple(jnp.moveaxis(t, 1, 0) for t in (rh, decay.reshape(hs), kh, vh, -kk, kk * ah))
    S0 = jnp.zeros((B, RWKV_HEADS, HEAD_DIM, HEAD_DIM), jnp.float32)
    _, ys = lax.scan(_rwkv7_step, S0, xs)
    y = jnp.moveaxis(ys, 0, 1)
    mean = jnp.mean(y, -1, keepdims=True)
    var = jnp.mean(jnp.square(y - mean), -1, keepdims=True)
    y = ((y - mean) * lax.rsqrt(var + GN_EPS)).reshape(B, T, RWKV_WIDTH)
    y = y * ln_w.astype(jnp.float32) + ln_b.astype(jnp.float32)
    bonus = jnp.sum(rh * kh * r_k.astype(jnp.float32), -1, keepdims=True) * vh
    y = y + bonus.reshape(B, T, RWKV_WIDTH)
    return y.astype(p.dtype)


def setup_inputs(seed: int = 0) -> dict:
    key = jax.random.key(seed)
    ks = jax.random.split(key, 20)
    f = jnp.float32
    L, D = DEPTH, D_MODEL
    return {
        "x": jax.random.normal(ks[0], (BATCH, SEQ, D), f),
        "c": jax.random.normal(ks[1], (BATCH, D), f),
        "w_ada": jax.random.normal(ks[2], (L, D, 3 * D), f) * (0.5 * D ** -0.5),
        "b_ada": jax.random.normal(ks[3], (L, 3 * D), f) * 0.02,
        "pre_norm_g": 1.0 + 0.05 * jax.random.normal(ks[4], (L, D), f),
        "post_norm_g": 1.0 + 0.05 * jax.random.normal(ks[5], (L, D), f),
        "w_in": jax.random.normal(ks[6], (L, D, IN_COLS), f) * D ** -0.5,
        "w_out": jax.random.normal(ks[7], (L, MIX_WIDTH, D), f) * MIX_WIDTH ** -0.5,
        "attn_sinks": jax.random.normal(ks[8], (L, ATTN_Q_HEADS), f),
        "rwkv_mu": jax.random.uniform(ks[9], (L, RWKV_SHIFT_COLS), f),
        "rwkv_w0": jax.random.uniform(ks[10], (L, RWKV_WIDTH), f, -6.0, 0.0),
        "rwkv_w_up": jax.random.normal(ks[11], (L, DECAY_LORA, RWKV_WIDTH), f) * 0.1,
        "rwkv_a0": jax.random.normal(ks[12], (L, RWKV_WIDTH), f) * 0.1,
        "rwkv_a_up": jax.random.normal(ks[13], (L, ICLR_LORA, RWKV_WIDTH), f) * 0.1,
        "rwkv_k_k": 0.85 + 0.1 * jax.random.normal(ks[14], (L, RWKV_WIDTH), f),
        "rwkv_k_a": 1.0 + 0.1 * jax.random.normal(ks[15], (L, RWKV_WIDTH), f),
        "rwkv_r_k": jax.random.normal(ks[16], (L, RWKV_HEADS, HEAD_DIM), f) * 0.1,
        "rwkv_ln_w": 1.0 + 0.05 * jax.random.normal(ks[17], (L, RWKV_WIDTH), f),
        "rwkv_ln_b": 0.02 * jax.random.normal(ks[18], (L, RWKV_WIDTH), f),
    }


def reference(x, c, w_ada, b_ada, pre_norm_g, post_norm_g, w_in, w_out, attn_sinks,
              rwkv_mu, rwkv_w0, rwkv_w_up, rwkv_a0, rwkv_a_up, rwkv_k_k, rwkv_k_a,
              rwkv_r_k, rwkv_ln_w, rwkv_ln_b):
    B, T = x.shape[0], x.shape[1]
    for l in range(DEPTH):
        mod = jax.nn.silu(c) @ w_ada[l] + b_ada[l]
        shift, scale, gate = jnp.split(mod, 3, axis=-1)
        h = _rms_norm(x, pre_norm_g[l]) * (1 + scale[:, None]) + shift[:, None]
        proj = h @ w_in[l]
        p_attn, p_rwkv = proj[..., :ATTN_COLS], proj[..., ATTN_COLS:]
        q, ka, va, ga = _split(p_attn, ATTN_SIZES)
        y_attn = _sliding_window_attention(
            q.reshape(B, T, ATTN_Q_HEADS, HEAD_DIM),
            ka.reshape(B, T, ATTN_KV_HEADS, HEAD_DIM),
            va.reshape(B, T, ATTN_KV_HEADS, HEAD_DIM),
            attn_sinks[l]) * jax.nn.silu(ga)
        p_shift, gr = p_rwkv[..., :RWKV_SHIFT_COLS], p_rwkv[..., RWKV_SHIFT_COLS:]
        y_rwkv = _rwkv7_time_mix(p_shift, rwkv_mu[l], rwkv_w0[l], rwkv_w_up[l], rwkv_a0[l],
                                 rwkv_a_up[l], rwkv_k_k[l], rwkv_k_a[l], rwkv_r_k[l],
                                 rwkv_ln_w[l], rwkv_ln_b[l]) * jax.nn.silu(gr)
        mix = jnp.concatenate([y_attn, y_rwkv], axis=-1) @ w_out[l]
        x = x + gate[:, None] * _rms_norm(mix, post_norm_g[l])
    return x
```

```python
import numpy as np
import concourse.bass as bass
import concourse.mybir as mybir
from concourse.bass_utils import run_bass_kernel_spmd
from contextlib import ExitStack

F32 = mybir.dt.float32
BF16 = mybir.dt.bfloat16
AF = mybir.ActivationFunctionType
ALU = mybir.AluOpType

T = 2048
D = 2048
NT = 16
RMS_EPS = 1e-6
GN_EPS = 64e-5
C0 = float(np.exp(-0.5))
NBLK = 55
DEBUG = False
POOL_COPY = False
DVE_EVAC = True

PV_PREG = 0
PV_MU = 16
PV_W0 = 41
PV_A0 = 49
PV_KK = 57
PV_KA = 65
PV_RK = 73
PV_LNW = 81
PV_LNB = 89
PV_N = 97

CB_IDENT = 0
CB_MASKA = 128
CB_MASKA0 = 384
CB_MST = 640
CB_MS = 704
CB_MIT = 768
CB_I64 = 832
CB_OL0 = 896
CB_OL1 = 1024
CB_N = 1152
CF_BONES = 0
CF_BONES64 = 128
CF_RESET = 256
CF_ONES = 768
CF_N = 896


class Sched:
    EPOCH = 12000
    RECYCLE = False

    def __init__(self, nc, es):
        self.nc = nc
        self.es = es
        self.engs = {'pe': nc.tensor, 'act': nc.scalar, 'dve': nc.vector, 'pool': nc.gpsimd, 'sp': nc.sync}
        self.sems = {e: [] for e in self.engs}
        self.cnt = {e: 0 for e in self.engs}
        self.waited = {}
        self.lastw = {}
        self.readers = {}
        self.dsem = {}
        self.dpool = []
        self.nsem = 0
        self.ninstr = 0

    def _newsem(self, name):
        self.nsem += 1
        return self.es.enter_context(self.nc.semaphore(name))

    def _cursem(self, e):
        if not self.sems[e] or self.cnt[e] >= self.EPOCH:
            self.sems[e].append(self._newsem("s_%s_%d" % (e, len(self.sems[e]))))
            self.cnt[e] = 0
        return len(self.sems[e]) - 1

    def _wait(self, e, tok):
        if tok is None:
            return
        if tok[0] == 'e':
            _, te, ep, c = tok
            if te == e and e == 'pe':
                return
            key = (e, 'e', te, ep)
            if self.waited.get(key, 0) >= c:
                return
            self.waited[key] = c
            self.engs[e].wait_ge(self.sems[te][ep], c)
        else:
            _, k, c = tok
            key = (e, 'd', k)
            if self.waited.get(key, 0) >= c:
                return
            self.waited[key] = c
            self.engs[e].wait_ge(self.dsem[k][0], c)

    def _deps(self, e, reads, writes):
        for r in reads:
            self._wait(e, self.lastw.get(r))
        for w in writes:
            self._wait(e, self.lastw.get(w))
            for t in self.readers.get(w, {}).values():
                self._wait(e, t)

    def _commit(self, tok, reads, writes):
        for w in writes:
            self.lastw[w] = tok
            self.readers[w] = {}
        for r in reads:
            if r in writes:
                continue
            d = self.readers.setdefault(r, {})
            k = (tok[1],) if tok[0] == 'e' else ('d', tok[1])
            d[k] = tok

    @staticmethod
    def _pe_writes(e, writes):
        if e != 'pe':
            return writes
        out = []
        seen = set()
        for w in writes:
            if isinstance(w, tuple) and len(w) == 3 and w[0] == "ps":
                if w[1] not in seen:
                    seen.add(w[1])
                    out += [("ps", w[1], s_) for s_ in range(8)]
            else:
                out.append(w)
        return out

    def op(self, e, fn, reads=(), writes=()):
        writes = self._pe_writes(e, writes)
        self._deps(e, reads, writes)
        ep = self._cursem(e)
        ins = fn(self.engs[e])
        ins.then_inc(self.sems[e][ep], 1)
        self.cnt[e] += 1
        self.ninstr += 1
        tok = ('e', e, ep, self.cnt[e])
        self._commit(tok, reads, writes)
        return tok

    def multi(self, e, fns, reads=(), writes=()):
        writes = self._pe_writes(e, writes)
        self._deps(e, reads, writes)
        ep = self._cursem(e)
        ins = None
        for fn in fns:
            ins = fn(self.engs[e])
            self.ninstr += 1
        ins.then_inc(self.sems[e][ep], 1)
        self.cnt[e] += 1
        tok = ('e', e, ep, self.cnt[e])
        self._commit(tok, reads, writes)
        return tok

    def dma(self, q, out, in_, reads=(), writes=(), key=None, serialize=True):
        if key not in self.dsem:
            if self.dpool:
                self.dsem[key] = self.dpool.pop()
            else:
                self.dsem[key] = [self._newsem("d%d" % self.nsem), 0]
        if serialize and self.dsem[key][1] > 0:
            self._wait(q, ('d', key, self.dsem[key][1]))
        self._deps(q, reads, writes)
        ins = self.engs[q].dma_start(out=out, in_=in_)
        ins.then_inc(self.dsem[key][0], 16)
        self.dsem[key][1] += 16
        self.ninstr += 1
        tok = ('d', key, self.dsem[key][1])
        self._commit(tok, reads, writes)
        return tok

    def barrier(self):
        last = {}
        for e in self.engs:
            if self.sems[e]:
                last[e] = ('e', e, len(self.sems[e]) - 1, self.cnt[e])
        for e in self.engs:
            for te, tok in last.items():
                if te != e and tok[3] > 0:
                    self._wait(e, tok)
            for k, (s, c) in self.dsem.items():
                if c > 0:
                    self._wait(e, ('d', k, c))
        self.lastw = {}
        self.readers = {}
        if self.RECYCLE:
            for k in list(self.dsem.keys()):
                self.dpool.append(self.dsem.pop(k))
                self.waited = {w: v for w, v in self.waited.items() if not (w[1] == 'd' and w[2] == k)}


def build_program():
    nc = bass.Bass("TRN2", target_bir_lowering=False)
    dt = nc.dram_tensor
    x_d = dt("x", [T, D], F32, kind="ExternalInput").ap()
    cvec_d = dt("cvec", [128, 16], F32, kind="ExternalInput").ap()
    wada_d = dt("wada", [12, 128, 16, 512], F32, kind="ExternalInput").ap()
    bada_d = dt("bada", [1, 6144], F32, kind="ExternalInput").ap()
    win_d = dt("win", [NBLK, 128, 16, 128], F32, kind="ExternalInput").ap()
    wout_d = dt("wout", [4, 128, 16, 512], F32, kind="ExternalInput").ap()
    pvec_d = dt("pvec", [128, PV_N], F32, kind="ExternalInput").ap()
    lora_d = dt("lora", [128, 1024], F32, kind="ExternalInput").ap()
    postg_d = dt("postg", [1, 2048], F32, kind="ExternalInput").ap()
    sinks_d = dt("sinksrep", [1, 1024], F32, kind="ExternalInput").ap()
    sinkc_d = dt("sinkcol", [128, 8], F32, kind="ExternalInput").ap()
    cb_d = dt("constb", [128, CB_N], F32, kind="ExternalInput").ap()
    cf_d = dt("constf", [128, CF_N], F32, kind="ExternalInput").ap()
    out_d = dt("out", [T, D], F32, kind="ExternalOutput").ap()
    scrA = dt("scrA", [28, 128, T], BF16, kind="Internal").ap()
    scrR = dt("scrR", [25, 128, T], F32, kind="Internal").ap()
    scrM = dt("scrM", [16, 128, T], BF16, kind="Internal").ap()

    es = ExitStack()
    S = Sched(nc, es)
    sb = lambda name, shape, dtype: es.enter_context(nc.sbuf_tensor(name, shape, dtype))

    cb = sb("cb", [128, CB_N], BF16)
    cf = sb("cf", [128, CF_N], F32)
    pv = sb("pv", [128, PV_N], F32)
    omm = sb("omm", [128, 25], F32)
    omka = sb("omka", [128, 8], F32)
    epsc = sb("epsc", [128, 2], F32)
    negb = sb("negb", [128, 16], F32)
    lora = sb("lora_sb", [128, 1024], BF16)
    gpb = sb("gpb", [128, 2048], F32)
    gs = sb("gs", [128, 16], F32)
    shiftc = sb("shiftc", [128, 16], F32)
    sinkl = sb("sinkl", [1, 1024], F32)
    sinkc = sb("sinkc", [128, 8], F32)
    Hst = sb("Hst", [128, 8, 64], F32)
    small = sb("small", [128, 64], F32)
    one11 = cf[0:1, CF_ONES:CF_ONES + 1]
    ones_row = cf[0:1, CF_ONES:CF_ONES + 128]

    ARENA_W = 42400
    VP_OFF = 27700
    arena = sb("arena", [128, ARENA_W], F32)
    ps = es.enter_context(nc.psum_tensor("ps", [128, 8, 512], F32))

    def carve(off, shape, dtype):
        n = int(np.prod(shape[1:]))
        if dtype == BF16:
            assert n % 2 == 0
            a = arena[:, off:off + n // 2].bitcast(BF16)
            w = n // 2
        else:
            a = arena[:, off:off + n]
            w = n
        if len(shape) == 3:
            a = a.rearrange("p (a b) -> p a b", a=shape[1])
        elif len(shape) == 4:
            a = a.rearrange("p (a b c) -> p a b c", a=shape[1], b=shape[2])
        return a, off + w

    def PSU(bank, lo, hi):
        return [("ps", bank, s) for s in range(lo // 64, (hi - 1) // 64 + 1)]

    Vp, _vpend = carve(VP_OFF, [128, 17, 4, 192], BF16)
    assert _vpend <= ARENA_W

    S.dma('pool', cb[:], cb_d, writes=["cb"], key="c0")
    S.dma('sp', cf[:], cf_d, writes=["cf"], key="c1")
    S.dma('sp', pv[:], pvec_d, writes=["pv"], key="c2")
    S.dma('pool', lora[:], lora_d, writes=["lora"], key="c3")
    S.dma('sp', sinkl[:], sinks_d, writes=["sinkl"], key="c4")
    S.op('act', lambda e: e.activation(out=sinkl[:], in_=sinkl[:], func=AF.Exp), reads=["sinkl"], writes=["sinkl"])
    S.dma('sp', sinkc[:], sinkc_d, writes=["sinkc"], key="c6")
    S.op('act', lambda e: e.activation(out=sinkc[:], in_=sinkc[:], func=AF.Exp), reads=["sinkc"], writes=["sinkc"])
    S.op('dve', lambda e: e.tensor_scalar(out=omm[:], in0=pv[:, PV_MU:PV_MU + 25], scalar1=-1.0, scalar2=1.0,
                                          op0=ALU.mult, op1=ALU.add), reads=["pv"], writes=["omm"])
    S.op('dve', lambda e: e.tensor_scalar(out=omka[:], in0=pv[:, PV_KA:PV_KA + 8], scalar1=-1.0, scalar2=1.0,
                                          op0=ALU.mult, op1=ALU.add), reads=["pv"], writes=["omka"])
    S.op('dve', lambda e: e.tensor_scalar(out=negb[:, 0:8], in0=pv[:, PV_W0:PV_W0 + 8], scalar1=-1.0, scalar2=None, op0=ALU.mult),
         reads=["pv"], writes=["negb"])
    S.op('dve', lambda e: e.tensor_scalar(out=negb[:, 8:16], in0=pv[:, PV_A0:PV_A0 + 8], scalar1=-1.0, scalar2=None, op0=ALU.mult),
         reads=["pv"], writes=["negb"])
    S.op('pool', lambda e: e.memset(epsc[:, 0:1], RMS_EPS), writes=["epsc"])
    S.op('pool', lambda e: e.memset(epsc[:, 1:2], GN_EPS), writes=["epsc"])
    S.op('pool', lambda e: e.memset(Hst[:].rearrange("p a b -> p (a b)"), 0.0), writes=["H"])

    ident = cb[:, CB_IDENT:CB_IDENT + 128]

    o = 30000
    wa, o = carve(o, [128, 2, 16, 512], BF16)
    csb, o = carve(o, [128, 16], F32)
    scb, o = carve(o, [128, 16], BF16)
    brow, o = carve(o, [128, 2, 512], F32)
    mrow, o = carve(o, [128, 2, 512], F32)
    pgrow, o = carve(o, [128, 2, 512], F32)
    S.dma('sp', csb, cvec_d, writes=["csb"], key="c5")
    S.op('act', lambda e: e.activation(out=scb, in_=csb, func=AF.Silu), reads=["csb"], writes=["scb"])
    assert o <= ARENA_W, o

    def phase0():
      for nb in range(12):
          u = nb % 2
          S.dma('pool', wa[:, u], wada_d[nb], writes=[("wa", u)], key=("wa", u))
          S.dma('sp', brow[0:1, u, :], bada_d[0:1, nb * 512:(nb + 1) * 512], writes=[("brow", u)], key=("br", u))
          fns = []
          for kc in range(16):
              fns.append(lambda e, kc=kc: e.matmul(ps[0:1, u, :], scb[:, kc:kc + 1], wa[:, u, kc, :],
                                                   start=(kc == 0), stop=(kc == 15)))
          S.multi('pe', fns, reads=[("wa", u), "scb"], writes=PSU(u, 0, 512))
          S.op('dve', lambda e: e.tensor_tensor(out=mrow[0:1, u, :], in0=ps[0:1, u, :], in1=brow[0:1, u, :], op=ALU.add),
               reads=PSU(u, 0, 512) + [("brow", u)], writes=[("mrow", u)])
          if nb < 8:
              fns = []
              for i in range(4):
                  col = nb * 4 + i
                  fns.append(lambda e, i=i, col=col: e.matmul(ps[:, 2, col:col + 1], mrow[0:1, u, i * 128:(i + 1) * 128],
                                                              one11, start=True, stop=True))
              S.multi('pe', fns, reads=[("mrow", u), "cf"], writes=PSU(2, 0, 64))
          else:
              gb = nb - 8
              S.dma('sp', pgrow[0:1, u, :], postg_d[0:1, gb * 512:(gb + 1) * 512], writes=[("pgrow", u)], key=("pg", u))
              S.op('dve', lambda e: e.tensor_tensor(out=mrow[0:1, u, :], in0=mrow[0:1, u, :], in1=pgrow[0:1, u, :], op=ALU.mult),
                   reads=[("mrow", u), ("pgrow", u)], writes=[("mrow", u)])
              S.op('pe', lambda e: e.matmul(ps[:, 3, :], ones_row, mrow[0:1, u, :], start=True, stop=True),
                   reads=[("mrow", u), "cf"], writes=PSU(3, 0, 512))
              S.op('act', lambda e: e.activation(out=gpb[:, gb * 512:(gb + 1) * 512], in_=ps[:, 3, :], func=AF.Copy),
                   reads=PSU(3, 0, 512), writes=["gpb"])
          if nb == 7:
              S.op('dve', lambda e: e.tensor_copy(out=shiftc[:], in_=ps[:, 2, 0:16]), reads=PSU(2, 0, 64), writes=["shiftc"])
              S.op('dve', lambda e: e.scalar_tensor_tensor(out=gs[:], in0=ps[:, 2, 16:32], scalar=1.0, in1=pv[:, PV_PREG:PV_PREG + 16],
                                                           op0=ALU.add, op1=ALU.mult), reads=PSU(2, 0, 64) + ["pv"], writes=["gs"])
          yield

    o = 0
    hT, o = carve(o, [128, 16, T], BF16)
    o_after_hT = o
    xt, o = carve(o, [128, 2, D], F32)
    xn, o = carve(o, [128, 2, 4, D], BF16)
    junk, o = carve(o, [128, D], BF16)

    def n_stage1(q):
        qu = q % 2
        for t4 in range(4):
            tt = q * 4 + t4
            u = tt % 2
            S.dma('sp', xt[:, u, :], x_d[tt * 128:(tt + 1) * 128, :], writes=[("xt", u)], key=("xt", u))
            S.op('act', lambda e: e.activation(out=junk, in_=xt[:, u, :], func=AF.Square, scale=float(D) ** -0.5,
                                               accum_out=small[:, tt:tt + 1]),
                 reads=[("xt", u)], writes=["junk", ("ssq", tt)])
            S.op('act', lambda e: e.activation(out=small[:, 16 + tt:17 + tt], in_=small[:, tt:tt + 1], func=AF.Sqrt, bias=epsc[:, 0:1]),
                 reads=[("ssq", tt), "epsc"], writes=[("rs", tt)])
            S.op('dve', lambda e: e.reciprocal(out=small[:, 16 + tt:17 + tt], in_=small[:, 16 + tt:17 + tt]),
                 reads=[("rs", tt)], writes=[("rs", tt)])
            S.op('dve', lambda e: e.tensor_scalar(out=xn[:, qu, t4, :], in0=xt[:, u, :], scalar1=small[:, 16 + tt:17 + tt], scalar2=None,
                                                  op0=ALU.mult), reads=[("xt", u), ("rs", tt)], writes=[("xn", qu, t4)])

    def n_stage2(q):
        qu = q % 2
        for kc in range(16):
            bank = 4 + (kc // 2) % 4
            half = kc % 2
            pst = ps[:, bank, :].bitcast(BF16)[:, half * 512:(half + 1) * 512]
            fns = []
            for t4 in range(4):
                fns.append(lambda e, t4=t4, pst=pst: e.transpose(pst[:, t4 * 128:(t4 + 1) * 128],
                                                                 xn[:, qu, t4, kc * 128:(kc + 1) * 128], ident))
            S.multi('pe', fns, reads=[("xn", qu, t4) for t4 in range(4)] + ["cb"], writes=PSU(bank, half * 256, half * 256 + 256))
            dst = hT[:, kc, q * 512:(q + 1) * 512]
            wr = [("hT", q * 4 + t4) for t4 in range(4)]
            if kc % 2 == 0:
                S.op('act', lambda e, pst=pst, dst=dst: e.activation(out=dst, in_=pst, func=AF.Identity,
                                                                    bias=shiftc[:, kc:kc + 1], scale=gs[:, kc:kc + 1]),
                     reads=PSU(bank, half * 256, half * 256 + 256) + ["gs", "shiftc"], writes=wr)
            else:
                S.op('dve', lambda e, pst=pst, dst=dst: e.tensor_scalar(out=dst, in0=pst, scalar1=gs[:, kc:kc + 1],
                                                                       scalar2=shiftc[:, kc:kc + 1], op0=ALU.mult, op1=ALU.add),
                     reads=PSU(bank, half * 256, half * 256 + 256) + ["gs", "shiftc"], writes=wr)

    p0 = phase0()

    def p0_step(n):
        for _ in range(n):
            try:
                next(p0)
            except StopIteration:
                return

    p0_step(8)
    n_stage1(0)
    n_stage1(1)
    for q in range(4):
        n_stage2(q)
        p0_step(1)
        if q + 2 < 4:
            n_stage1(q + 2)
    p0_step(12)
    S.barrier()

    o = o_after_hT
    wbuf, o = carve(o, [128, 3, 16, 128], BF16)
    stgb, o = carve(o, [128, 2, T], BF16)
    stgf, o = carve(o, [128, 2, T], F32)
    amu, o = carve(o, [128, T + 1], F32)
    S.op('pool', lambda e: e.memset(amu[:, 0:1], 0.0), writes=["amu0"])
    assert o <= VP_OFF, o
    S.op('pool', lambda e: e.memset(Vp.rearrange("p a b c -> p (a b c)"), 0.0), writes=[("Vp", i) for i in range(17)] + ["Vp"])
    hT_all = [("hT", tt) for tt in range(NT)]

    def blk_kind(blk):
        if blk < 4: return ('copy', blk, None)
        if blk < 6: return ('v', blk - 4, None)
        if blk < 14: return ('q', 4 + (blk - 6), None)
        if blk < 22: return ('silu', 12 + (blk - 14), None)
        if blk == 22: return ('lerp', 0, 0)
        j, r = divmod(blk - 23, 4)
        if r < 3: return ('lerp', 1 + 3 * j + r, 1 + 3 * j + r)
        return ('silu', 20 + j, None)

    for blk in range(min(2, NBLK)):
        S.dma('pool', wbuf[:, blk % 3], win_d[blk], writes=[("wbuf", blk % 3)], key=("wb", blk % 3))
    def phaseI(blks):
      bankrot = 0
      nb_i = 0
      nf_i = 0
      for blk in blks:
          wu = blk % 3
          if blk + 2 < NBLK:
              yield S.dma('pool', wbuf[:, (blk + 2) % 3], win_d[blk + 2], writes=[("wbuf", (blk + 2) % 3)], key=("wb", (blk + 2) % 3))
          kind, sidx, mucol = blk_kind(blk)
          if kind == 'v':
              for tt in range(NT):
                  bank = bankrot % 4; bankrot += 1
                  fns = [lambda e, kc=kc, bank=bank, tt=tt: e.matmul(ps[:, bank, 0:128], hT[:, kc, tt * 128:(tt + 1) * 128],
                                                                     wbuf[:, wu, kc, :], start=(kc == 0), stop=(kc == 15))
                         for kc in range(16)]
                  yield S.multi('pe', fns, reads=[("hT", tt), ("wbuf", wu)], writes=PSU(bank, 0, 128))
                  dst = Vp[:, 1 + tt, 2 * sidx:2 * sidx + 2, 64:128]
                  src = ps[:, bank, 0:128].rearrange("p (a b) -> p a b", a=2)
                  eng = 'act' if tt % 2 == 0 else 'dve'
                  if eng == 'act':
                      yield S.op('act', lambda e, dst=dst, src=src: e.activation(out=dst, in_=src, func=AF.Copy),
                           reads=PSU(bank, 0, 128), writes=[("Vp", 1 + tt)])
                  else:
                      yield S.op('dve', lambda e, dst=dst, src=src: e.tensor_copy(out=dst, in_=src),
                           reads=PSU(bank, 0, 128), writes=[("Vp", 1 + tt)])
              continue
          if kind == 'lerp':
              su = nf_i % 2; nf_i += 1
              stg = stgf[:, su, :]
              stgname = ("stgf", su)
          else:
              su = nb_i % 2; nb_i += 1
              stg = stgb[:, su, :]
              stgname = ("stgb", su)
          for wn in range(4):
              bank = bankrot % 4; bankrot += 1
              tok = slice(wn * 512, (wn + 1) * 512)
              for pc4 in range(2):
                  fns = [lambda e, kc=kc, bank=bank, tok=tok: e.matmul(ps[:, bank, :], wbuf[:, wu, kc, :], hT[:, kc, tok],
                                                                       start=(kc == 0), stop=(kc == 15)) for kc in range(pc4 * 8, pc4 * 8 + 8)]
                  yield S.multi('pe', fns, reads=hT_all[wn * 4:(wn + 1) * 4] + [("wbuf", wu)], writes=PSU(bank, 0, 512))
              pin = ps[:, bank, :]
              if kind == 'copy':
                  yield S.op('act', lambda e, pin=pin, tok=tok: e.activation(out=stg[:, tok], in_=pin, func=AF.Copy),
                       reads=PSU(bank, 0, 512), writes=[stgname])
              elif kind == 'q':
                  yield S.op('dve', lambda e, pin=pin, tok=tok: e.tensor_scalar(out=stg[:, tok], in0=pin, scalar1=0.125, scalar2=None,
                                                                         op0=ALU.mult), reads=PSU(bank, 0, 512), writes=[stgname])
              elif kind == 'silu':
                  yield S.op('act', lambda e, pin=pin, tok=tok: e.activation(out=stg[:, tok], in_=pin, func=AF.Silu),
                       reads=PSU(bank, 0, 512), writes=[stgname])
              else:
                  mc = pv[:, PV_MU + mucol:PV_MU + mucol + 1]
                  oc = omm[:, mucol:mucol + 1]
                  yield S.op('act', lambda e, pin=pin, wn=wn, mc=mc: e.activation(out=amu[:, 1 + wn * 512:1 + (wn + 1) * 512], in_=pin,
                                                                            func=AF.Copy, scale=mc),
                       reads=PSU(bank, 0, 512) + ["pv"], writes=[("amu", wn)])
                  rd = [("amu", wn)] + ([("amu", wn - 1)] if wn > 0 else ["amu0"])
                  yield S.op('dve', lambda e, pin=pin, wn=wn, oc=oc, tok=tok: e.scalar_tensor_tensor(
                      out=stg[:, tok], in0=pin, scalar=oc, in1=amu[:, wn * 512:(wn + 1) * 512], op0=ALU.mult, op1=ALU.add),
                      reads=PSU(bank, 0, 512) + rd + ["omm"], writes=[stgname])
          if kind == 'lerp':
              yield S.dma('sp', scrR[sidx], stg, reads=[stgname], writes=[("scrR", sidx)], key=("spf", su))
          else:
              yield S.dma('sp', scrA[sidx], stg, reads=[stgname], writes=[("scrA", sidx)], key=("spb", su))

    o = _vpend
    qT, o = carve(o, [128, 1, 2, T], BF16)
    kTg, o = carve(o, [128, 1, T + 128], BF16)
    gaT, o = carve(o, [128, 1, 2, T], BF16)
    PT, o = carve(o, [128, 2, 2, 512], BF16)
    rden, o = carve(o, [128, 2, 256], F32)
    ynum, o = carve(o, [128, 2, 256], F32)
    mixst, o = carve(o, [128, 2, 2, 128], BF16)
    assert o <= ARENA_W, o
    S.op('pool', lambda e: e.memset(kTg[:, :, 0:128], 0.0), writes=[("kTg", 0)])
    maskA = cb[:, CB_MASKA:CB_MASKA + 256]
    maskA0 = cb[:, CB_MASKA0:CB_MASKA0 + 256]
    OL = [cb[:, CB_OL0:CB_OL0 + 128], cb[:, CB_OL1:CB_OL1 + 128]]
    def a_load(g):
        gu = 0
        for pi in range(2):
            yield S.dma('sp', qT[:, gu, pi, :], scrA[4 + 2 * g + pi], reads=[("scrA", 4 + 2 * g + pi)], writes=[("qT", gu)],
                  key=("aq", gu), serialize=(pi == 0))
        yield S.dma('sp', kTg[:, gu, 128:], scrA[g], reads=[("scrA", g)], writes=[("kTg", gu)], key=("ak", gu))

    def a_stage1(unit):
        g, tt = divmod(unit, NT)
        gu = 0
        u = unit % 2
        if tt == 0:
            yield from a_load(g)
        for e_ in range(2):
            bank = 4 + e_
            rows = slice(64 * e_, 64 * e_ + 64)
            fns = []
            scv = ps[:, bank, :].rearrange("p (a b c) -> p a b c", a=2, b=2)
            for pc in range(2):
                fns.append(lambda e, rows=rows, pc=pc, scv=scv: e.matmul(
                    scv[:, :, pc, :], kTg[rows, gu, (tt + pc) * 128:(tt + pc + 1) * 128],
                    qT[rows, gu, :, tt * 128:(tt + 1) * 128], start=True, stop=True))
            yield S.multi('pe', fns, reads=[("kTg", gu), ("qT", gu)], writes=PSU(bank, 0, 512))
            yield S.op('act', lambda e, bank=bank, e_=e_: e.activation(out=PT[:, u, e_, :], in_=ps[:, bank, :], func=AF.Exp),
                 reads=PSU(bank, 0, 512), writes=[("PT", u, e_)])
        mk = maskA0 if tt == 0 else maskA
        ptv = PT[:, u].rearrange("p e (a b) -> p (e a) b", a=2)
        yield S.op('dve', lambda e, ptv=ptv, mk=mk: e.tensor_tensor(out=ptv, in0=ptv, in1=mk.unsqueeze(1).broadcast_to([128, 4, 256]),
                                                              op=ALU.mult),
             reads=[("PT", u, 0), ("PT", u, 1), "cb"], writes=[("PT", u, 0), ("PT", u, 1)])

    def a_stage2(unit):
        g, tt = divmod(unit, NT)
        gu = 0
        u = unit % 2
        nbank = 6 + u
        if tt == 0:
            for pi in range(2):
                yield S.dma('sp', gaT[:, gu, pi, :], scrA[12 + 2 * g + pi], reads=[("scrA", 12 + 2 * g + pi)], writes=[("gaT", gu)],
                            key=("ag", gu), serialize=(pi == 0))
        fns = []
        seq = [(e_, pc) for e_ in range(2) for pc in range(2)]
        numv = ps[:, nbank, 0:256].rearrange("p (a b) -> p a b", a=2)
        denv = ps[:, nbank, 256:512].rearrange("p (a b) -> p a b", a=2)
        ptq = lambda e_, pc: PT[:, u, e_, :].rearrange("p (a b c) -> p a b c", a=2, b=2)[:, :, pc, :]
        for i, (e_, pc) in enumerate(seq):
            vl = Vp[:, tt + pc, g, 64:192] if e_ == 0 else Vp[:, tt + pc, g, 0:128]
            fns.append(lambda e, e_=e_, pc=pc, vl=vl, i=i: e.matmul(numv, vl, ptq(e_, pc), start=(i == 0), stop=(i == 3)))
        for i, (e_, pc) in enumerate(seq):
            fns.append(lambda e, e_=e_, pc=pc, i=i: e.matmul(denv, OL[e_], ptq(e_, pc), start=(i == 0), stop=(i == 3)))
        yield S.multi('pe', fns, reads=[("PT", u, 0), ("PT", u, 1), ("Vp", tt), ("Vp", tt + 1), "Vp", "cb", "cf", "sinkl"],
                writes=PSU(nbank, 0, 512))
        for pi in range(2):
            pr = 2 * g + pi
            yield S.op('act', lambda e, pi=pi, pr=pr: e.activation(out=rden[:, u, pi * 128:(pi + 1) * 128],
                                                                 in_=ps[:, nbank, 256 + pi * 128:256 + (pi + 1) * 128],
                                                                 func=AF.Ln, bias=sinkc[:, pr:pr + 1]),
                       reads=PSU(nbank, 256, 512) + ["sinkc"], writes=[("rden", u)])
        yield S.op('act', lambda e: e.activation(out=rden[:, u, :], in_=rden[:, u, :], func=AF.Exp, scale=-1.0), reads=[("rden", u)],
             writes=[("rden", u)])
        yield S.op('dve', lambda e: e.tensor_tensor(out=ynum[:, u, :], in0=ps[:, nbank, 0:256], in1=rden[:, u, :], op=ALU.mult),
             reads=PSU(nbank, 0, 256) + [("rden", u)], writes=[("ynum", u)])
        yield S.op('pool', lambda e: e.tensor_tensor(out=mixst[:, u, :, :],
                                               in0=ynum[:, u, :].rearrange("p (a b) -> p a b", a=2),
                                               in1=gaT[:, gu, :, tt * 128:(tt + 1) * 128], op=ALU.mult),
             reads=[("ynum", u), ("gaT", gu)], writes=[("mixst", u)])
        yield S.dma('sp', scrM[2 * g:2 * g + 2, :, tt * 128:(tt + 1) * 128].rearrange("c p t -> p c t"), mixst[:, u, :, :],
              reads=[("mixst", u)], writes=[("scrM", g, tt)], key=("am", u))

    NU = 4 * NT

    def phaseA():
        yield from a_stage1(0)
        for unit in range(NU):
            if unit + 1 < NU:
                yield from a_stage1(unit + 1)
            yield from a_stage2(unit)

    for _ in phaseI(range(0, 22)):
        pass
    gens = [phaseI(range(22, NBLK)), phaseA()]
    while gens:
        for gen in list(gens):
            try:
                next(gen)
            except StopIteration:
                gens.remove(gen)
    S.barrier()

    WR = 256
    NWR = T // WR
    CW = WR // 64
    o = 0
    Hba, o = carve(o, [128, 8, 64], BF16)
    Hbb, o = carve(o, [128, 8, 64], BF16)
    Hb = [Hba, Hbb]
    S.op('pool', lambda e: e.memset(Hba.rearrange("p a b -> p (a b)"), 0.0), writes=[("Hb", 0, 0), ("Hb", 0, 1)])
    S.op('pool', lambda e: e.memset(Hbb.rearrange("p a b -> p (a b)"), 0.0), writes=[("Hb", 1, 0), ("Hb", 1, 1)])
    bones = cf[:, CF_BONES:CF_BONES + 128]
    bones64 = cf[:, CF_BONES64:CF_BONES64 + 128]
    resetm = cf[:, CF_RESET:CF_RESET + WR]
    mST = cb[:, CB_MST:CB_MST + 64]
    mS = cb[:, CB_MS:CB_MS + 64]
    mIT = cb[:, CB_MIT:CB_MIT + 64]
    I64 = cb[:, CB_I64:CB_I64 + 64]
    bc4 = lambda m: m.unsqueeze(1).broadcast_to([128, 4, 64])
    v4 = lambda a: a.rearrange("p (a b) -> p a b", a=4)

    class LB:
        pass
    lanes = []
    for g in range(2):
        L = LB()
        L.tw, o = carve(o, [128, WR], BF16)
        L.wdl, o = carve(o, [128, WR], F32)
        L.rkv, o = carve(o, [128, 2, 3, WR], F32)
        L.tmp = []
        for i in range(12):
            t_, o = carve(o, [128, WR], F32)
            L.tmp.append(t_)
        L.tb = []
        for i in range(3):
            t_, o = carve(o, [128, WR], BF16)
            L.tb.append(t_)
        L.grs, L.RT, L.AT, L.BT, L.KT, L.VT, L.BCt, L.KCt, L.BON, L.Wend = [], [], [], [], [], [], [], [], [], []
        for wp in range(2):
            for lst, shp, dt_ in ((L.grs, [128, 4, WR], BF16), (L.RT, [128, 4, WR], BF16), (L.AT, [128, 4, WR], BF16),
                                  (L.BT, [128, 4, WR], BF16), (L.KT, [128, 4, WR], BF16), (L.VT, [128, CW, 4, 64], BF16),
                                  (L.BCt, [128, CW, 4, 64], BF16), (L.KCt, [128, CW, 4, 64], BF16),
                                  (L.BON, [128, 4, WR], F32), (L.Wend, [128, 4, CW], F32)):
                t_, o = carve(o, shp, dt_)
                lst.append(t_)
        L.Yw, o = carve(o, [128, 4, WR], F32)
        L.mixr, o = carve(o, [128, 4, WR], BF16)
        L.ptmp = []
        for i in range(3):
            t_, o = carve(o, [128, WR], F32)
            L.ptmp.append(t_)
        L.NNs = []
        for i in range(2):
            t_, o = carve(o, [128, 2, 256], BF16)
            L.NNs.append(t_)
        L.PTm = []
        for i in range(2):
            t_, o = carve(o, [128, 256], BF16)
            L.PTm.append(t_)
        L.G2 = []
        for i in range(2):
            t_, o = carve(o, [128, 2, 256], BF16)
            L.G2.append(t_)
        L.G3 = []
        for i in range(2):
            t_, o = carve(o, [128, 256], BF16)
            L.G3.append(t_)
        L.Xb, o = carve(o, [128, 256], BF16)
        L.Ub, o = carve(o, [128, 256], BF16)
        lanes.append(L)
    assert o <= ARENA_W, o
    LO, HI = slice(0, 256), slice(256, 512)

    def prep_stream(grp):
        L = lanes[grp]
        N_ = lambda n, *a: (n, grp) + a
        P0, P1 = 4 * grp, 4 * grp + 1
        tw, wdl, rkv = L.tw, L.wdl, L.rkv
        sw, a_, cs, W_, Wi, Wp, WC, kkr, sq, kkn, kmod, ba = L.tmp
        vlb, bCb, kCb = L.tb
        for wn in range(NWR):
            wp = wn % 2
            while chain_done[grp] < wn - 1:
                yield None
            grs, RT, AT, BT, KT, VT, BCt, KCt, BON, Wend = (L.grs[wp], L.RT[wp], L.AT[wp], L.BT[wp], L.KT[wp], L.VT[wp],
                                                            L.BCt[wp], L.KCt[wp], L.BON[wp], L.Wend[wp])
            wtok = slice(wn * WR, (wn + 1) * WR)
            yield S.dma('sp', wdl, scrR[0][:, wtok], reads=[("scrR", 0)], writes=[N_("wdl")], key=N_("rw"))
            onec = cf[0:64, CF_ONES:CF_ONES + 1]
            yield S.op('act', lambda e: e.activation(out=sw[0:64, :], in_=wdl[0:64, :], func=AF.Exp, scale=2.0), reads=[N_("wdl")], writes=[N_("sw")])
            yield S.op('act', lambda e: e.activation(out=sw[0:64, :], in_=sw[0:64, :], func=AF.Ln, bias=onec), reads=[N_("sw"), "cf"], writes=[N_("sw")])
            yield S.op('act', lambda e: e.activation(out=sw[0:64, :], in_=sw[0:64, :], func=AF.Exp, scale=-1.0), reads=[N_("sw")], writes=[N_("sw")])
            yield S.op('dve', lambda e: e.tensor_scalar(out=tw[0:64, :], in0=sw[0:64, :], scalar1=-2.0, scalar2=1.0, op0=ALU.mult, op1=ALU.add),
                       reads=[N_("sw")], writes=[N_("tw")])
            yield S.op('dve', lambda e: e.tensor_copy(out=tw[64:128, :], in_=wdl[64:128, :]), reads=[N_("wdl")], writes=[N_("tw1")])
            for jj in range(4):
                j = grp * 4 + jj
                ru = jj % 2
                rl, kl, vl_ = rkv[:, ru, 0, :], rkv[:, ru, 1, :], rkv[:, ru, 2, :]
                RKV = N_("rkv", ru)
                for r_ in range(3):
                    yield S.dma('sp', rkv[:, ru, r_, :], scrR[1 + 3 * j + r_][:, wtok], reads=[("scrR", 1 + 3 * j + r_)],
                                writes=[RKV], key=N_("rr", ru), serialize=(r_ == 0))
                yield S.dma('sp', grs[:, jj, :], scrA[20 + j][:, wtok], reads=[("scrA", 20 + j)], writes=[N_("grs", wp, jj)], key=N_("rg", jj))
                col = lambda base: pv[:, base + j:base + j + 1]
                yield S.op('pe', lambda e: e.matmul(ps[:, P0, LO], lora[0:64, j * 128:(j + 1) * 128], tw[0:64, :], start=True, stop=True),
                           reads=["lora", N_("tw")], writes=PSU(P0, 0, 256))
                yield S.op('pe', lambda e: e.matmul(ps[:, P1, LO], lora[64:128, j * 128:(j + 1) * 128], tw[64:128, :], start=True, stop=True),
                           reads=["lora", N_("tw1")], writes=PSU(P1, 0, 256))
                one128 = cf[:, CF_ONES:CF_ONES + 1]
                for dst_, dn_, bank_, nb_ in ((sw, "sw", P0, negb[:, j:j + 1]), (a_, "a_", P1, negb[:, 8 + j:9 + j])):
                    yield S.op('act', lambda e, dst_=dst_, bank_=bank_, nb_=nb_: e.activation(out=dst_, in_=ps[:, bank_, LO], func=AF.Exp, scale=-1.0, bias=nb_),
                               reads=PSU(bank_, 0, 256) + ["negb"], writes=[N_(dn_)])
                    yield S.op('act', lambda e, dst_=dst_: e.activation(out=dst_, in_=dst_, func=AF.Ln, bias=one128), reads=[N_(dn_), "cf"], writes=[N_(dn_)])
                    yield S.op('act', lambda e, dst_=dst_: e.activation(out=dst_, in_=dst_, func=AF.Exp, scale=-1.0), reads=[N_(dn_)], writes=[N_(dn_)])
                yield S.op('dve', lambda e: e.tensor_tensor_scan(out=cs, data0=resetm, data1=sw, initial=0.0, op0=ALU.mult, op1=ALU.add),
                           reads=[N_("sw"), "cf"], writes=[N_("cs")])
                yield S.op('act', lambda e: e.activation(out=sq, in_=kl, func=AF.Square, scale=col(PV_KK)),
                           reads=[RKV, "pv"], writes=[N_("sq")])
                yield S.op('pe', lambda e: e.matmul(ps[:, P0, HI], bones, sq, start=True, stop=True), reads=[N_("sq"), "cf"], writes=PSU(P0, 256, 512))
                yield S.op('act', lambda e: e.activation(out=W_, in_=cs, func=AF.Exp, scale=-C0), reads=[N_("cs")], writes=[N_("W_")])
                yield S.op('act', lambda e: e.activation(out=Wi, in_=cs, func=AF.Exp, scale=C0), reads=[N_("cs")], writes=[N_("Wi")])
                yield S.op('dve', lambda e: e.tensor_tensor(out=Wp, in0=cs, in1=sw, op=ALU.subtract), reads=[N_("cs"), N_("sw")], writes=[N_("Wp")])
                yield S.op('act', lambda e: e.activation(out=Wp, in_=Wp, func=AF.Exp, scale=-C0), reads=[N_("Wp")], writes=[N_("Wp")])
                cs3 = cs.rearrange("p (a b) -> p a b", a=CW)
                yield S.op('dve', lambda e: e.tensor_tensor(out=WC.rearrange("p (a b) -> p a b", a=CW),
                                                            in0=cs3[:, :, 63:64].broadcast_to([128, CW, 64]), in1=cs3, op=ALU.subtract),
                           reads=[N_("cs")], writes=[N_("WC")])
                yield S.op('act', lambda e: e.activation(out=WC, in_=WC, func=AF.Exp, scale=-C0), reads=[N_("WC")], writes=[N_("WC")])
                yield S.op('act', lambda e: e.activation(out=Wend[:, jj, :], in_=cs3[:, :, 63], func=AF.Exp, scale=-C0),
                           reads=[N_("cs")], writes=[N_("Wend", wp, jj)])
                yield S.op('dve', lambda e: e.tensor_scalar(out=sq, in0=ps[:, P0, HI], scalar1=1e-24, scalar2=None, op0=ALU.max),
                           reads=PSU(P0, 256, 512), writes=[N_("sq")])
                yield S.op('act', lambda e: e.activation(out=sq, in_=sq, func=AF.Ln), reads=[N_("sq")], writes=[N_("sq")])
                yield S.op('act', lambda e: e.activation(out=sq, in_=sq, func=AF.Exp, scale=-0.5), reads=[N_("sq")], writes=[N_("sq")])
                yield S.op('dve', lambda e: e.scalar_tensor_tensor(out=kkn, in0=kl, scalar=col(PV_KK), in1=sq, op0=ALU.mult, op1=ALU.mult),
                           reads=[RKV, "pv", N_("sq")], writes=[N_("kkn")])
                yield S.op('act', lambda e: e.activation(out=kmod, in_=a_, func=AF.Identity, scale=col(PV_KA), bias=omka[:, j:j + 1]),
                           reads=[N_("a_"), "pv", "omka"], writes=[N_("kmod")])
                yield S.op('pool', lambda e: e.tensor_tensor(out=kmod, in0=kmod, in1=kl, op=ALU.mult),
                           reads=[N_("kmod"), RKV], writes=[N_("kmod")])
                yield S.op('pool', lambda e: e.tensor_tensor(out=RT[:, jj, :], in0=rl, in1=W_, op=ALU.mult),
                           reads=[RKV, N_("W_")], writes=[N_("RT", wp, jj)])
                yield S.op('dve', lambda e: e.scalar_tensor_tensor(out=AT[:, jj, :], in0=kkn, scalar=-1.0, in1=Wp, op0=ALU.mult, op1=ALU.mult),
                           reads=[N_("kkn"), N_("Wp")], writes=[N_("AT", wp, jj)])
                yield S.op('pool', lambda e: e.tensor_tensor(out=ba, in0=kkn, in1=a_, op=ALU.mult), reads=[N_("kkn"), N_("a_")], writes=[N_("ba")])
                yield S.op('pool', lambda e: e.tensor_tensor(out=BT[:, jj, :], in0=ba, in1=Wi, op=ALU.mult), reads=[N_("ba"), N_("Wi")], writes=[N_("BT", wp, jj)])
                yield S.op('pool', lambda e: e.tensor_tensor(out=bCb, in0=ba, in1=WC, op=ALU.mult), reads=[N_("ba"), N_("WC")], writes=[N_("bCb")])
                yield S.op('pool', lambda e: e.tensor_tensor(out=KT[:, jj, :], in0=kmod, in1=Wi, op=ALU.mult), reads=[N_("kmod"), N_("Wi")], writes=[N_("KT", wp, jj)])
                yield S.op('pool', lambda e: e.tensor_tensor(out=kCb, in0=kmod, in1=WC, op=ALU.mult), reads=[N_("kmod"), N_("WC")], writes=[N_("kCb")])
                yield S.op('dve', lambda e: e.scalar_tensor_tensor(out=kkr, in0=rl, scalar=col(PV_RK), in1=kmod, op0=ALU.mult, op1=ALU.mult),
                           reads=[RKV, "pv", N_("kmod")], writes=[N_("kkr")])
                yield S.op('pe', lambda e: e.matmul(ps[:, P1, HI], bones, kkr, start=True, stop=True), reads=[N_("kkr"), "cf"], writes=PSU(P1, 256, 512))
                yield S.op('dve', lambda e: e.tensor_tensor(out=BON[:, jj, :], in0=ps[:, P1, HI], in1=vl_, op=ALU.mult),
                           reads=PSU(P1, 256, 512) + [RKV], writes=[N_("BON", wp, jj)])
                if POOL_COPY:
                    yield S.op('pool', lambda e: e.tensor_copy(out=vlb, in_=vl_), reads=[RKV], writes=[N_("vlb")])
                else:
                    yield S.op('act', lambda e: e.activation(out=vlb, in_=vl_, func=AF.Copy), reads=[RKV], writes=[N_("vlb")])
                for si, (src, sname, dst, dname, bank, hh) in enumerate([(vlb, "vlb", VT, "VT", P0, 0), (bCb, "bCb", BCt, "BCt", P1, 0),
                                                                          (kCb, "kCb", KCt, "KCt", P0, 256)]):
                    fns = []
                    for c in range(CW):
                        for e_ in range(2):
                            rows = slice(64 * e_, 64 * e_ + 64)
                            fns.append(lambda e, src=src, bank=bank, rows=rows, c=c, hh=hh: e.matmul(
                                ps[rows, bank, hh + c * 64:hh + (c + 1) * 64], src[rows, c * 64:(c + 1) * 64], I64[rows, :], start=True, stop=True))
                    yield S.multi('pe', fns, reads=[N_(sname), "cb"], writes=PSU(bank, hh, hh + 256))
                    pv3 = ps[:, bank, hh:hh + 256].rearrange("p (a b) -> p a b", a=CW)
                    if si == 1:
                        yield S.op('dve', lambda e, dst=dst, pv3=pv3: e.tensor_copy(out=dst[:, :, jj, :], in_=pv3),
                                   reads=PSU(bank, hh, hh + 256), writes=[N_(dname, wp, jj)])
                    else:
                        yield S.op('act', lambda e, dst=dst, pv3=pv3: e.activation(out=dst[:, :, jj, :], in_=pv3, func=AF.Copy),
                                   reads=PSU(bank, hh, hh + 256), writes=[N_(dname, wp, jj)])
            prep_done[grp] = wn + 1

    def chain_stream(grp):
        L = lanes[grp]
        N_ = lambda n, *a: (n, grp) + a
        C0, C1 = 4 * grp + 2, 4 * grp + 3
        Yw, mixr = L.Yw, L.mixr
        NNs, PTm, G2, G3, Xb, Ub = L.NNs, L.PTm, L.G2, L.G3, L.Xb, L.Ub
        hbpar = 0
        for wn in range(NWR):
            wp = wn % 2
            grs, RT, AT, BT, KT, VT, BCt, KCt, BON, Wend = (L.grs[wp], L.RT[wp], L.AT[wp], L.BT[wp], L.KT[wp], L.VT[wp],
                                                            L.BCt[wp], L.KCt[wp], L.BON[wp], L.Wend[wp])
            allj = lambda n: [N_(n, wp, jj) for jj in range(4)]
            wtok = slice(wn * WR, (wn + 1) * WR)
            while prep_done[grp] < wn + 1:
                yield None
            for c in range(CW):
                cu = c % 2
                tokc = slice(c * 64, (c + 1) * 64)
                fa, fb, fc = [], [], []
                for jj in range(4):
                    for e_ in range(2):
                        rows = slice(64 * e_, 64 * e_ + 64)
                        cs_ = slice(jj * 64, (jj + 1) * 64)
                        cs2 = slice(256 + jj * 64, 256 + (jj + 1) * 64)
                        fa.append(lambda e, rows=rows, cs_=cs_, jj=jj: e.matmul(ps[rows, C0, cs_], BT[rows, jj, tokc], AT[rows, jj, tokc], start=True, stop=True))
                        fa.append(lambda e, rows=rows, cs2=cs2, jj=jj: e.matmul(ps[rows, C0, cs2], AT[rows, jj, tokc], BT[rows, jj, tokc], start=True, stop=True))
                        fb.append(lambda e, rows=rows, cs_=cs_, jj=jj: e.matmul(ps[rows, C1, cs_], KT[rows, jj, tokc], AT[rows, jj, tokc], start=True, stop=True))
                        fb.append(lambda e, rows=rows, cs2=cs2, jj=jj: e.matmul(ps[rows, C1, cs2], BT[rows, jj, tokc], RT[rows, jj, tokc], start=True, stop=True))
                        fc.append(lambda e, rows=rows, cs_=cs_, jj=jj: e.matmul(ps[rows, C0, cs_], KT[rows, jj, tokc], RT[rows, jj, tokc], start=True, stop=True))
                yield S.multi('pe', fa, reads=allj("BT") + allj("AT"), writes=PSU(C0, 0, 512))
                yield S.multi('pe', fb, reads=allj("BT") + allj("AT") + allj("KT") + allj("RT"), writes=PSU(C1, 0, 512))
                N0 = NNs[0]
                yield S.op('dve', lambda e: e.tensor_tensor(out=v4(N0[:, 0, :]), in0=v4(ps[:, C0, LO]), in1=bc4(mST), op=ALU.mult),
                           reads=PSU(C0, 0, 256) + ["cb"], writes=[N_("NN", 0)])
                yield S.op('dve', lambda e: e.tensor_tensor(out=v4(N0[:, 1, :]), in0=v4(ps[:, C0, HI]), in1=bc4(mS), op=ALU.mult),
                           reads=PSU(C0, 256, 512) + ["cb"], writes=[N_("NN", 0)])
                yield S.multi('pe', fc, reads=allj("KT") + allj("RT"), writes=PSU(C0, 0, 256))
                yield S.op('pool', lambda e: e.tensor_tensor(out=v4(PTm[0]), in0=v4(N0[:, 0, :]), in1=bc4(I64), op=ALU.add),
                           reads=[N_("NN", 0), "cb"], writes=[N_("PTm", 0)])
                yield S.op('dve', lambda e: e.tensor_tensor(out=v4(G2[cu][:, 0, :]), in0=v4(ps[:, C1, LO]), in1=bc4(mST), op=ALU.mult),
                           reads=PSU(C1, 0, 256) + ["cb"], writes=[N_("G2", cu)])
                yield S.op('dve', lambda e: e.tensor_tensor(out=v4(G2[cu][:, 1, :]), in0=v4(ps[:, C1, HI]), in1=bc4(mIT), op=ALU.mult),
                           reads=PSU(C1, 256, 512) + ["cb"], writes=[N_("G2", cu)])
                yield S.op('dve', lambda e: e.tensor_tensor(out=v4(G3[cu]), in0=v4(ps[:, C0, LO]), in1=bc4(mIT), op=ALU.mult),
                           reads=PSU(C0, 0, 256) + ["cb"], writes=[N_("G3", cu)])
                for l in range(5):
                    a0_, a1_ = NNs[l % 2], NNs[(l + 1) % 2]
                    fns = []
                    for jj in range(4):
                        for e_ in range(2):
                            rows = slice(64 * e_, 64 * e_ + 64)
                            cs_ = slice(jj * 64, (jj + 1) * 64)
                            cs2 = slice(256 + jj * 64, 256 + (jj + 1) * 64)
                            fns.append(lambda e, rows=rows, cs_=cs_, cs2=cs2, a0_=a0_: e.matmul(
                                ps[rows, C1, cs2], a0_[rows, 0, cs_], a0_[rows, 1, cs_], start=True, stop=True))
                            fns.append(lambda e, rows=rows, cs_=cs_, a0_=a0_: e.matmul(
                                ps[rows, C1, cs_], a0_[rows, 1, cs_], a0_[rows, 0, cs_], start=True, stop=True))
                    yield S.multi('pe', fns, reads=[N_("NN", l % 2)], writes=PSU(C1, 0, 512))
                    if l % 2 == 0 or not DVE_EVAC:
                        yield S.op('act', lambda e, a1_=a1_: e.activation(out=a1_.rearrange("p a b -> p (a b)"), in_=ps[:, C1, :], func=AF.Copy),
                                   reads=PSU(C1, 0, 512), writes=[N_("NN", (l + 1) % 2)])
                    else:
                        yield S.op('dve', lambda e, a1_=a1_: e.tensor_copy(out=a1_.rearrange("p a b -> p (a b)"), in_=ps[:, C1, :]),
                                   reads=PSU(C1, 0, 512), writes=[N_("NN", (l + 1) % 2)])
                    fns = []
                    for jj in range(4):
                        for e_ in range(2):
                            rows = slice(64 * e_, 64 * e_ + 64)
                            cs_ = slice(jj * 64, (jj + 1) * 64)
                            cs2 = slice(256 + jj * 64, 256 + (jj + 1) * 64)
                            fns.append(lambda e, rows=rows, cs_=cs_, cs2=cs2, a1_=a1_, l=l: e.matmul(
                                ps[rows, C0, cs2], a1_[rows, 1, cs_], PTm[l % 2][rows, cs_], start=True, stop=True))
                    yield S.multi('pe', fns, reads=[N_("NN", (l + 1) % 2), N_("PTm", l % 2)], writes=PSU(C0, 256, 512))
                    yield S.op('dve', lambda e, l=l: e.tensor_tensor(out=PTm[(l + 1) % 2], in0=ps[:, C0, HI], in1=PTm[l % 2], op=ALU.add),
                               reads=PSU(C0, 256, 512) + [N_("PTm", l % 2)], writes=[N_("PTm", (l + 1) % 2)])
                TT = PTm[1]
                hb = Hb[hbpar]
                hbn = Hb[1 - hbpar]
                fns = []
                for jj in range(4):
                    j = grp * 4 + jj
                    for e_ in range(2):
                        rows = slice(64 * e_, 64 * e_ + 64)
                        cs_ = slice(jj * 64, (jj + 1) * 64)
                        fns.append(lambda e, rows=rows, cs_=cs_, jj=jj, j=j: e.matmul(ps[rows, C1, cs_], AT[rows, jj, tokc], hb[rows, j, :], start=True, stop=False))
                        fns.append(lambda e, rows=rows, cs_=cs_, jj=jj: e.matmul(ps[rows, C1, cs_], G2[cu][rows, 0, cs_], VT[rows, c, jj, :], start=False, stop=True))
                yield S.multi('pe', fns, reads=allj("AT") + [("Hb", hbpar, grp), N_("G2", cu)] + allj("VT"), writes=PSU(C1, 0, 256))
                yield S.op('act', lambda e: e.activation(out=Xb, in_=ps[:, C1, LO], func=AF.Copy), reads=PSU(C1, 0, 256), writes=[N_("Xb")])
                fns = []
                for jj in range(4):
                    for e_ in range(2):
                        rows = slice(64 * e_, 64 * e_ + 64)
                        cs_ = slice(jj * 64, (jj + 1) * 64)
                        cs2 = slice(256 + jj * 64, 256 + (jj + 1) * 64)
                        fns.append(lambda e, rows=rows, cs_=cs_, cs2=cs2: e.matmul(ps[rows, C1, cs2], TT[rows, cs_], Xb[rows, cs_], start=True, stop=True))
                yield S.multi('pe', fns, reads=[N_("PTm", 1), N_("Xb")], writes=PSU(C1, 256, 512))
                yield S.op('act', lambda e: e.activation(out=Ub, in_=ps[:, C1, HI], func=AF.Copy), reads=PSU(C1, 256, 512), writes=[N_("Ub")])
                fns = []
                for jj in range(4):
                    j = grp * 4 + jj
                    for e_ in range(2):
                        rows = slice(64 * e_, 64 * e_ + 64)
                        cs_ = slice(jj * 64, (jj + 1) * 64)
                        cs2 = slice(256 + jj * 64, 256 + (jj + 1) * 64)
                        fns.append(lambda e, rows=rows, cs2=cs2, jj=jj, j=j: e.matmul(ps[rows, C0, cs2], hb[rows, j, :], RT[rows, jj, tokc], start=True, stop=False))
                        fns.append(lambda e, rows=rows, cs_=cs_, cs2=cs2: e.matmul(ps[rows, C0, cs2], Ub[rows, cs_], G2[cu][rows, 1, cs_], start=False, stop=False))
                        fns.append(lambda e, rows=rows, cs_=cs_, cs2=cs2, jj=jj: e.matmul(ps[rows, C0, cs2], VT[rows, c, jj, :], G3[cu][rows, cs_], start=False, stop=True))
                        fns.append(lambda e, rows=rows, cs_=cs_, jj=jj: e.matmul(ps[rows, C0, cs_], BCt[rows, c, jj, :], Ub[rows, cs_], start=True, stop=False))
                        fns.append(lambda e, rows=rows, cs_=cs_, jj=jj: e.matmul(ps[rows, C0, cs_], KCt[rows, c, jj, :], VT[rows, c, jj, :], start=False, stop=True))
                yield S.multi('pe', fns, reads=[("Hb", hbpar, grp), N_("Ub"), N_("G2", cu), N_("G3", cu)] + allj("RT") + allj("VT") + allj("BCt") + allj("KCt"),
                              writes=PSU(C0, 0, 512))
                H4 = Hst[:, grp * 4:(grp + 1) * 4, :]
                yield S.op('dve', lambda e: e.tensor_tensor(out=H4, in0=H4, in1=Wend[:, :, c:c + 1].broadcast_to([128, 4, 64]), op=ALU.mult),
                           reads=[N_("H")] + allj("Wend"), writes=[N_("H")])
                yield S.op('dve', lambda e: e.tensor_tensor(out=H4, in0=v4(ps[:, C0, LO]), in1=H4, op=ALU.add),
                           reads=PSU(C0, 0, 256) + [N_("H")], writes=[N_("H")])
                if POOL_COPY:
                    yield S.op('pool', lambda e: e.tensor_copy(out=hbn[:, grp * 4:(grp + 1) * 4, :], in_=H4),
                               reads=[N_("H")], writes=[("Hb", 1 - hbpar, grp)])
                else:
                    yield S.op('act', lambda e: e.activation(out=hbn[:, grp * 4:(grp + 1) * 4, :], in_=H4, func=AF.Copy),
                               reads=[N_("H")], writes=[("Hb", 1 - hbpar, grp)])
                yield S.op('act', lambda e: e.activation(out=Yw[:, :, tokc], in_=v4(ps[:, C0, HI]), func=AF.Copy),
                           reads=PSU(C0, 256, 512), writes=[N_("Yw")])
                hbpar = 1 - hbpar
            for jj in range(4):
                j = grp * 4 + jj
                col = lambda base, j=j: pv[:, base + j:base + j + 1]
                yc, sq_, yn = L.ptmp
                yield S.op('pe', lambda e: e.matmul(ps[:, C1, LO], bones64, Yw[:, jj, :], start=True, stop=True), reads=[N_("Yw"), "cf"], writes=PSU(C1, 0, 256))
                yield S.op('dve', lambda e: e.tensor_tensor(out=yc, in0=Yw[:, jj, :], in1=ps[:, C1, LO], op=ALU.subtract),
                           reads=[N_("Yw")] + PSU(C1, 0, 256), writes=[N_("yc")])
                yield S.op('pool', lambda e: e.tensor_tensor(out=sq_, in0=yc, in1=yc, op=ALU.mult), reads=[N_("yc")], writes=[N_("sq_")])
                yield S.op('pe', lambda e: e.matmul(ps[:, C1, HI], bones64, sq_, start=True, stop=True), reads=[N_("sq_"), "cf"], writes=PSU(C1, 256, 512))
                yield S.op('act', lambda e: e.activation(out=sq_, in_=ps[:, C1, HI], func=AF.Ln, bias=epsc[:, 1:2]),
                           reads=PSU(C1, 256, 512) + ["epsc"], writes=[N_("sq_")])
                yield S.op('act', lambda e: e.activation(out=sq_, in_=sq_, func=AF.Exp, scale=-0.5), reads=[N_("sq_")], writes=[N_("sq_")])
                yield S.op('pool', lambda e: e.tensor_tensor(out=yn, in0=yc, in1=sq_, op=ALU.mult), reads=[N_("yc"), N_("sq_")], writes=[N_("yn")])
                yield S.op('act', lambda e: e.activation(out=yn, in_=yn, func=AF.Identity, bias=col(PV_LNB), scale=col(PV_LNW)),
                           reads=[N_("yn"), "pv"], writes=[N_("yn")])
                yield S.op('pool', lambda e: e.tensor_tensor(out=yn, in0=yn, in1=BON[:, jj, :], op=ALU.add), reads=[N_("yn"), N_("BON", wp, jj)], writes=[N_("yn")])
                yield S.op('pool', lambda e: e.tensor_tensor(out=mixr[:, jj, :], in0=yn, in1=grs[:, jj, :], op=ALU.mult),
                           reads=[N_("yn"), N_("grs", wp, jj)], writes=[N_("mixr", jj)])
                yield S.dma('pool', scrM[8 + j][:, wtok], mixr[:, jj, :], reads=[N_("mixr", jj)], writes=[("scrM", 8 + j, wn)], key=N_("rm", jj))
            chain_done[grp] = wn + 1

    prep_done = [0, 0]
    chain_done = [0, 0]
    gens = [prep_stream(0), prep_stream(1), chain_stream(0), chain_stream(1)]
    while gens:
        for gen in list(gens):
            try:
                next(gen)
            except StopIteration:
                gens.remove(gen)
    S.barrier()

    o = 0
    wo, o = carve(o, [128, 16, 2048], BF16)
    mt, o = carve(o, [128, 2, 16, 128], BF16)
    xt2, o = carve(o, [128, 2, D], F32)
    ot, o = carve(o, [128, 2, D], F32)
    assert o <= ARENA_W, o
    for nb in range(4):
        S.dma('pool', wo[:, :, nb * 512:(nb + 1) * 512], wout_d[nb], writes=[("wo", nb)], key=("wo", nb))
    allM = []
    def o_load(tt):
        u = tt % 2
        tsl = slice(tt * 128, (tt + 1) * 128)
        S.dma('sp', mt[:, u], scrM[:, :, tsl].rearrange("c p t -> p c t"), reads=allM, writes=[("mt", u)], key=("om", u))
        S.dma('sp', xt2[:, u, :], x_d[tsl, :], writes=[("xt2", u)], key=("ox", u))

    o_load(0)
    for tt in range(NT):
        u = tt % 2
        tsl = slice(tt * 128, (tt + 1) * 128)
        if tt + 1 < NT:
            o_load(tt + 1)
        for nb in range(4):
            bank = 4 * u + nb
            fns = [lambda e, kc=kc, bank=bank, nb=nb: e.matmul(ps[:, bank, :], mt[:, u, kc, :], wo[:, kc, nb * 512:(nb + 1) * 512],
                                                               start=(kc == 0), stop=(kc == 15)) for kc in range(16)]
            S.multi('pe', fns, reads=[("mt", u), ("wo", nb)], writes=PSU(bank, 0, 512))
        pso = ps[:, 4 * u:4 * u + 4, :]
        allb = [r for b in range(4 * u, 4 * u + 4) for r in PSU(b, 0, 512)]
        S.op('act', lambda e: e.activation(out=ot[:, u, :].rearrange("p (a b) -> p a b", a=4), in_=pso, func=AF.Square,
                                           scale=float(D) ** -0.5, accum_out=small[:, 32 + tt:33 + tt]),
             reads=allb, writes=[("ot", u), ("ss2", tt)])
        S.op('act', lambda e: e.activation(out=small[:, 48 + tt:49 + tt], in_=small[:, 32 + tt:33 + tt], func=AF.Sqrt, bias=epsc[:, 0:1]),
             reads=[("ss2", tt), "epsc"], writes=[("rs2", tt)])
        S.op('dve', lambda e: e.reciprocal(out=small[:, 48 + tt:49 + tt], in_=small[:, 48 + tt:49 + tt]),
             reads=[("rs2", tt)], writes=[("rs2", tt)])
        S.op('dve', lambda e: e.scalar_tensor_tensor(out=ot[:, u, :].rearrange("p (a b) -> p a b", a=4), in0=pso,
                                                     scalar=small[:, 48 + tt:49 + tt], in1=gpb[:].rearrange("p (a b) -> p a b", a=4),
                                                     op0=ALU.mult, op1=ALU.mult),
             reads=allb + [("rs2", tt), "gpb"], writes=[("ot", u)])
        S.op('pool', lambda e: e.tensor_tensor(out=ot[:, u, :], in0=ot[:, u, :], in1=xt2[:, u, :], op=ALU.add),
             reads=[("ot", u), ("xt2", u)], writes=[("ot", u)])
        S.dma('pool', out_d[tsl, :], ot[:, u, :], reads=[("ot", u)], writes=[("out", tt)], key=("oo", u))
    S.barrier()
    es.close()
    return nc, S


_CACHE = {}


def _host_consts():
    cbm = np.zeros((128, CB_N), np.float32)
    cbm[:, CB_IDENT:CB_IDENT + 128] = np.eye(128, dtype=np.float32)
    s = np.arange(128)[:, None]
    q = np.arange(128)[None, :]
    prev = (s > q).astype(np.float32)
    cur = (s <= q).astype(np.float32)
    cbm[:, CB_MASKA:CB_MASKA + 128] = prev
    cbm[:, CB_MASKA + 128:CB_MASKA + 256] = cur
    cbm[:, CB_MASKA0 + 128:CB_MASKA0 + 256] = cur
    p = (np.arange(128) % 64)[:, None]
    f = np.arange(64)[None, :]
    cbm[:, CB_MST:CB_MST + 64] = (p < f)
    cbm[:, CB_MS:CB_MS + 64] = (f < p)
    cbm[:, CB_MIT:CB_MIT + 64] = (p <= f)
    cbm[:, CB_I64:CB_I64 + 64] = (p == f)
    cbm[:, CB_OL0:CB_OL0 + 64] = 1.0
    cbm[:, CB_OL1 + 64:CB_OL1 + 128] = 1.0
    cfm = np.zeros((128, CF_N), np.float32)
    blk = (np.arange(128)[:, None] // 64 == np.arange(128)[None, :] // 64).astype(np.float32)
    cfm[:, CF_BONES:CF_BONES + 128] = blk
    cfm[:, CF_BONES64:CF_BONES64 + 128] = blk / 64.0
    rm = np.ones(512, np.float32)
    rm[::64] = 0.0
    cfm[:, CF_RESET:CF_RESET + 512] = rm[None, :]
    cfm[:, CF_ONES:CF_ONES + 128] = 1.0
    return cbm, cfm


def _colidx():
    idx = []
    for g in range(4):
        kc = np.arange(1024 + g * 64, 1024 + (g + 1) * 64)
        idx += [kc, kc]
    idx.append(np.arange(1280, 1536))
    idx.append(np.arange(0, 1024))
    idx.append(np.arange(1536, 2560))
    R0 = 2560
    idx.append(np.arange(R0 + 3072, R0 + 3200))
    for j in range(8):
        for base in (0, 1024, 2048, 3200):
            idx.append(np.arange(R0 + base + j * 128, R0 + base + (j + 1) * 128))
    idx = np.concatenate(idx)
    assert idx.size == NBLK * 128
    return idx


def kernel(x, c, w_ada, b_ada, pre_norm_g, post_norm_g, w_in, w_out, attn_sinks,
           rwkv_mu, rwkv_w0, rwkv_w_up, rwkv_a0, rwkv_a_up, rwkv_k_k, rwkv_k_a,
           rwkv_r_k, rwkv_ln_w, rwkv_ln_b):
    f = lambda a: np.ascontiguousarray(np.asarray(a, dtype=np.float32))
    x = f(x); c = f(c)
    B = x.shape[0]
    if 'nc' not in _CACHE:
        _CACHE['nc'] = build_program()
    nc, S = _CACHE['nc']
    wada_h = f(np.asarray(w_ada[0]).reshape(16, 128, 12, 512).transpose(2, 1, 0, 3))
    win_h = f(np.asarray(w_in[0])[:, _colidx()].reshape(16, 128, NBLK, 128).transpose(2, 1, 0, 3))
    wout_h = f(np.asarray(w_out[0]).reshape(16, 128, 4, 512).transpose(2, 1, 0, 3))
    col = lambda v: np.asarray(v, np.float32).reshape(-1, 128).T
    pvec = np.zeros((128, PV_N), np.float32)
    pvec[:, PV_PREG:PV_PREG + 16] = col(pre_norm_g[0])
    mu = np.asarray(rwkv_mu[0], np.float32)
    pvec[:, PV_MU] = mu[3072:3200]
    for j in range(8):
        for r_ in range(3):
            pvec[:, PV_MU + 1 + 3 * j + r_] = mu[r_ * 1024 + j * 128:r_ * 1024 + (j + 1) * 128]
    pvec[:, PV_W0:PV_W0 + 8] = col(rwkv_w0[0])
    pvec[:, PV_A0:PV_A0 + 8] = col(rwkv_a0[0])
    pvec[:, PV_KK:PV_KK + 8] = col(rwkv_k_k[0])
    pvec[:, PV_KA:PV_KA + 8] = col(rwkv_k_a[0])
    pvec[:, PV_RK:PV_RK + 8] = col(np.asarray(rwkv_r_k[0]).reshape(-1))
    pvec[:, PV_LNW:PV_LNW + 8] = col(rwkv_ln_w[0])
    pvec[:, PV_LNB:PV_LNB + 8] = col(rwkv_ln_b[0])
    lora_h = f(np.concatenate([np.asarray(rwkv_w_up[0]), np.asarray(rwkv_a_up[0])], axis=0))
    sinks_h = f(np.repeat(np.asarray(attn_sinks[0], np.float32), 64)[None, :])
    sinkc_h = f(np.asarray(attn_sinks[0], np.float32).reshape(8, 2)[:, np.arange(128) // 64].T)
    cbm, cfm = _host_consts()
    bada_h = f(np.asarray(b_ada[0])[None, :])
    postg_h = f(np.asarray(post_norm_g[0])[None, :])
    in_maps = []
    for b in range(B):
        in_maps.append({
            "x": x[b], "cvec": f(c[b].reshape(16, 128).T), "wada": wada_h, "bada": bada_h, "win": win_h,
            "wout": wout_h, "pvec": pvec, "lora": lora_h, "postg": postg_h, "sinksrep": sinks_h, "sinkcol": sinkc_h,
            "constb": cbm, "constf": cfm,
        })
    res = run_bass_kernel_spmd(nc, in_maps, core_ids=list(range(B)))
    return np.stack([np.asarray(r["out"], dtype=np.float32) for r in res.results], axis=0)
```

```python
import numpy as np
import concourse.bass as bass
import concourse.mybir as mybir
from concourse.bass_utils import run_bass_kernel_spmd
from contextlib import ExitStack

F32 = mybir.dt.float32
BF16 = mybir.dt.bfloat16
AF = mybir.ActivationFunctionType
ALU = mybir.AluOpType

T = 2048
D = 2048
NT = 16
RMS_EPS = 1e-6
GN_EPS = 64e-5
C0 = float(np.exp(-0.5))
NBLK = 55
DEBUG = False
POOL_COPY = False
DVE_EVAC = True

PV_PREG = 0
PV_MU = 16
PV_W0 = 41
PV_A0 = 49
PV_KK = 57
PV_KA = 65
PV_RK = 73
PV_LNW = 81
PV_LNB = 89
PV_N = 97

CB_IDENT = 0
CB_MASKA = 128
CB_MASKA0 = 384
CB_MST = 640
CB_MS = 704
CB_MIT = 768
CB_I64 = 832
CB_OL0 = 896
CB_OL1 = 1024
CB_N = 1152
CF_BONES = 0
CF_BONES64 = 128
CF_RESET = 256
CF_ONES = 768
CF_N = 896


class Sched:
    EPOCH = 12000
    RECYCLE = False

    def __init__(self, nc, es):
        self.nc = nc
        self.es = es
        self.engs = {'pe': nc.tensor, 'act': nc.scalar, 'dve': nc.vector, 'pool': nc.gpsimd, 'sp': nc.sync}
        self.sems = {e: [] for e in self.engs}
        self.cnt = {e: 0 for e in self.engs}
        self.waited = {}
        self.lastw = {}
        self.readers = {}
        self.dsem = {}
        self.dpool = []
        self.nsem = 0
        self.ninstr = 0

    def _newsem(self, name):
        self.nsem += 1
        return self.es.enter_context(self.nc.semaphore(name))

    def _cursem(self, e):
        if not self.sems[e] or self.cnt[e] >= self.EPOCH:
            self.sems[e].append(self._newsem("s_%s_%d" % (e, len(self.sems[e]))))
            self.cnt[e] = 0
        return len(self.sems[e]) - 1

    def _wait(self, e, tok):
        if tok is None:
            return
        if tok[0] == 'e':
            _, te, ep, c = tok
            if te == e and e == 'pe':
                return
            key = (e, 'e', te, ep)
            if self.waited.get(key, 0) >= c:
                return
            self.waited[key] = c
            self.engs[e].wait_ge(self.sems[te][ep], c)
        else:
            _, k, c = tok
            key = (e, 'd', k)
            if self.waited.get(key, 0) >= c:
                return
            self.waited[key] = c
            self.engs[e].wait_ge(self.dsem[k][0], c)

    def _deps(self, e, reads, writes):
        for r in reads:
            self._wait(e, self.lastw.get(r))
        for w in writes:
            self._wait(e, self.lastw.get(w))
            for t in self.readers.get(w, {}).values():
                self._wait(e, t)

    def _commit(self, tok, reads, writes):
        for w in writes:
            self.lastw[w] = tok
            self.readers[w] = {}
        for r in reads:
            if r in writes:
                continue
            d = self.readers.setdefault(r, {})
            k = (tok[1],) if tok[0] == 'e' else ('d', tok[1])
            d[k] = tok

    @staticmethod
    def _pe_writes(e, writes):
        if e != 'pe':
            return writes
        out = []
        seen = set()
        for w in writes:
            if isinstance(w, tuple) and len(w) == 3 and w[0] == "ps":
                if w[1] not in seen:
                    seen.add(w[1])
                    out += [("ps", w[1], s_) for s_ in range(8)]
            else:
                out.append(w)
        return out

    def op(self, e, fn, reads=(), writes=()):
        writes = self._pe_writes(e, writes)
        self._deps(e, reads, writes)
        ep = self._cursem(e)
        ins = fn(self.engs[e])
        ins.then_inc(self.sems[e][ep], 1)
        self.cnt[e] += 1
        self.ninstr += 1
        tok = ('e', e, ep, self.cnt[e])
        self._commit(tok, reads, writes)
        return tok

    def multi(self, e, fns, reads=(), writes=()):
        writes = self._pe_writes(e, writes)
        self._deps(e, reads, writes)
        ep = self._cursem(e)
        ins = None
        for fn in fns:
            ins = fn(self.engs[e])
            self.ninstr += 1
        ins.then_inc(self.sems[e][ep], 1)
        self.cnt[e] += 1
        tok = ('e', e, ep, self.cnt[e])
        self._commit(tok, reads, writes)
        return tok

    def dma(self, q, out, in_, reads=(), writes=(), key=None, serialize=True):
        if key not in self.dsem:
            if self.dpool:
                self.dsem[key] = self.dpool.pop()
            else:
                self.dsem[key] = [self._newsem("d%d" % self.nsem), 0]
        if serialize and self.dsem[key][1] > 0:
            self._wait(q, ('d', key, self.dsem[key][1]))
        self._deps(q, reads, writes)
        ins = self.engs[q].dma_start(out=out, in_=in_)
        ins.then_inc(self.dsem[key][0], 16)
        self.dsem[key][1] += 16
        self.ninstr += 1
        tok = ('d', key, self.dsem[key][1])
        self._commit(tok, reads, writes)
        return tok

    def barrier(self):
        last = {}
        for e in self.engs:
            if self.sems[e]:
                last[e] = ('e', e, len(self.sems[e]) - 1, self.cnt[e])
        for e in self.engs:
            for te, tok in last.items():
                if te != e and tok[3] > 0:
                    self._wait(e, tok)
            for k, (s, c) in self.dsem.items():
                if c > 0:
                    self._wait(e, ('d', k, c))
        self.lastw = {}
        self.readers = {}
        if self.RECYCLE:
            for k in list(self.dsem.keys()):
                self.dpool.append(self.dsem.pop(k))
                self.waited = {w: v for w, v in self.waited.items() if not (w[1] == 'd' and w[2] == k)}


def build_program():
    nc = bass.Bass("TRN2", target_bir_lowering=False)
    dt = nc.dram_tensor
    x_d = dt("x", [T, D], F32, kind="ExternalInput").ap()
    cvec_d = dt("cvec", [128, 16], F32, kind="ExternalInput").ap()
    wada_d = dt("wada", [12, 128, 16, 512], F32, kind="ExternalInput").ap()
    bada_d = dt("bada", [1, 6144], F32, kind="ExternalInput").ap()
    win_d = dt("win", [NBLK, 128, 16, 128], F32, kind="ExternalInput").ap()
    wout_d = dt("wout", [4, 128, 16, 512], F32, kind="ExternalInput").ap()
    pvec_d = dt("pvec", [128, PV_N], F32, kind="ExternalInput").ap()
    lora_d = dt("lora", [128, 1024], F32, kind="ExternalInput").ap()
    postg_d = dt("postg", [1, 2048], F32, kind="ExternalInput").ap()
    sinks_d = dt("sinksrep", [1, 1024], F32, kind="ExternalInput").ap()
    cb_d = dt("constb", [128, CB_N], F32, kind="ExternalInput").ap()
    cf_d = dt("constf", [128, CF_N], F32, kind="ExternalInput").ap()
    out_d = dt("out", [T, D], F32, kind="ExternalOutput").ap()
    scrA = dt("scrA", [28, 128, T], BF16, kind="Internal").ap()
    scrR = dt("scrR", [25, 128, T], F32, kind="Internal").ap()
    scrM = dt("scrM", [16, 128, T], BF16, kind="Internal").ap()

    es = ExitStack()
    S = Sched(nc, es)
    sb = lambda name, shape, dtype: es.enter_context(nc.sbuf_tensor(name, shape, dtype))

    cb = sb("cb", [128, CB_N], BF16)
    cf = sb("cf", [128, CF_N], F32)
    pv = sb("pv", [128, PV_N], F32)
    omm = sb("omm", [128, 25], F32)
    omka = sb("omka", [128, 8], F32)
    epsc = sb("epsc", [128, 2], F32)
    lora = sb("lora_sb", [128, 1024], BF16)
    gpb = sb("gpb", [128, 2048], F32)
    gs = sb("gs", [128, 16], F32)
    shiftc = sb("shiftc", [128, 16], F32)
    sinkl = sb("sinkl", [1, 1024], F32)
    Hst = sb("Hst", [128, 8, 64], F32)
    small = sb("small", [128, 64], F32)
    one11 = cf[0:1, CF_ONES:CF_ONES + 1]
    ones_row = cf[0:1, CF_ONES:CF_ONES + 128]

    ARENA_W = 42400
    VP_OFF = 27700
    arena = sb("arena", [128, ARENA_W], F32)
    ps = es.enter_context(nc.psum_tensor("ps", [128, 8, 512], F32))

    def carve(off, shape, dtype):
        n = int(np.prod(shape[1:]))
        if dtype == BF16:
            assert n % 2 == 0
            a = arena[:, off:off + n // 2].bitcast(BF16)
            w = n // 2
        else:
            a = arena[:, off:off + n]
            w = n
        if len(shape) == 3:
            a = a.rearrange("p (a b) -> p a b", a=shape[1])
        elif len(shape) == 4:
            a = a.rearrange("p (a b c) -> p a b c", a=shape[1], b=shape[2])
        return a, off + w

    def PSU(bank, lo, hi):
        return [("ps", bank, s) for s in range(lo // 64, (hi - 1) // 64 + 1)]

    Vp, _vpend = carve(VP_OFF, [128, 17, 4, 192], BF16)
    assert _vpend <= ARENA_W

    S.dma('pool', cb[:], cb_d, writes=["cb"], key="c0")
    S.dma('sp', cf[:], cf_d, writes=["cf"], key="c1")
    S.dma('sp', pv[:], pvec_d, writes=["pv"], key="c2")
    S.dma('pool', lora[:], lora_d, writes=["lora"], key="c3")
    S.dma('sp', sinkl[:], sinks_d, writes=["sinkl"], key="c4")
    S.op('act', lambda e: e.activation(out=sinkl[:], in_=sinkl[:], func=AF.Exp), reads=["sinkl"], writes=["sinkl"])
    S.op('dve', lambda e: e.tensor_scalar(out=omm[:], in0=pv[:, PV_MU:PV_MU + 25], scalar1=-1.0, scalar2=1.0,
                                          op0=ALU.mult, op1=ALU.add), reads=["pv"], writes=["omm"])
    S.op('dve', lambda e: e.tensor_scalar(out=omka[:], in0=pv[:, PV_KA:PV_KA + 8], scalar1=-1.0, scalar2=1.0,
                                          op0=ALU.mult, op1=ALU.add), reads=["pv"], writes=["omka"])
    S.op('pool', lambda e: e.memset(epsc[:, 0:1], RMS_EPS), writes=["epsc"])
    S.op('pool', lambda e: e.memset(epsc[:, 1:2], GN_EPS), writes=["epsc"])
    S.op('pool', lambda e: e.memset(Hst[:].rearrange("p a b -> p (a b)"), 0.0), writes=["H"])

    ident = cb[:, CB_IDENT:CB_IDENT + 128]

    o = 30000
    wa, o = carve(o, [128, 2, 16, 512], BF16)
    csb, o = carve(o, [128, 16], F32)
    scb, o = carve(o, [128, 16], BF16)
    brow, o = carve(o, [128, 2, 512], F32)
    mrow, o = carve(o, [128, 2, 512], F32)
    pgrow, o = carve(o, [128, 2, 512], F32)
    S.dma('sp', csb, cvec_d, writes=["csb"], key="c5")
    S.op('act', lambda e: e.activation(out=scb, in_=csb, func=AF.Silu), reads=["csb"], writes=["scb"])
    assert o <= ARENA_W, o

    def phase0():
      for nb in range(12):
          u = nb % 2
          S.dma('pool', wa[:, u], wada_d[nb], writes=[("wa", u)], key=("wa", u))
          S.dma('sp', brow[0:1, u, :], bada_d[0:1, nb * 512:(nb + 1) * 512], writes=[("brow", u)], key=("br", u))
          fns = []
          for kc in range(16):
              fns.append(lambda e, kc=kc: e.matmul(ps[0:1, u, :], scb[:, kc:kc + 1], wa[:, u, kc, :],
                                                   start=(kc == 0), stop=(kc == 15)))
          S.multi('pe', fns, reads=[("wa", u), "scb"], writes=PSU(u, 0, 512))
          S.op('dve', lambda e: e.tensor_tensor(out=mrow[0:1, u, :], in0=ps[0:1, u, :], in1=brow[0:1, u, :], op=ALU.add),
               reads=PSU(u, 0, 512) + [("brow", u)], writes=[("mrow", u)])
          if nb < 8:
              fns = []
              for i in range(4):
                  col = nb * 4 + i
                  fns.append(lambda e, i=i, col=col: e.matmul(ps[:, 2, col:col + 1], mrow[0:1, u, i * 128:(i + 1) * 128],
                                                              one11, start=True, stop=True))
              S.multi('pe', fns, reads=[("mrow", u), "cf"], writes=PSU(2, 0, 64))
          else:
              gb = nb - 8
              S.dma('sp', pgrow[0:1, u, :], postg_d[0:1, gb * 512:(gb + 1) * 512], writes=[("pgrow", u)], key=("pg", u))
              S.op('dve', lambda e: e.tensor_tensor(out=mrow[0:1, u, :], in0=mrow[0:1, u, :], in1=pgrow[0:1, u, :], op=ALU.mult),
                   reads=[("mrow", u), ("pgrow", u)], writes=[("mrow", u)])
              S.op('pe', lambda e: e.matmul(ps[:, 3, :], ones_row, mrow[0:1, u, :], start=True, stop=True),
                   reads=[("mrow", u), "cf"], writes=PSU(3, 0, 512))
              S.op('act', lambda e: e.activation(out=gpb[:, gb * 512:(gb + 1) * 512], in_=ps[:, 3, :], func=AF.Copy),
                   reads=PSU(3, 0, 512), writes=["gpb"])
          if nb == 7:
              S.op('dve', lambda e: e.tensor_copy(out=shiftc[:], in_=ps[:, 2, 0:16]), reads=PSU(2, 0, 64), writes=["shiftc"])
              S.op('dve', lambda e: e.scalar_tensor_tensor(out=gs[:], in0=ps[:, 2, 16:32], scalar=1.0, in1=pv[:, PV_PREG:PV_PREG + 16],
                                                           op0=ALU.add, op1=ALU.mult), reads=PSU(2, 0, 64) + ["pv"], writes=["gs"])
          yield

    o = 0
    hT, o = carve(o, [128, 16, T], BF16)
    o_after_hT = o
    xt, o = carve(o, [128, 2, D], F32)
    xn, o = carve(o, [128, 2, 4, D], BF16)
    junk, o = carve(o, [128, D], BF16)

    def n_stage1(q):
        qu = q % 2
        for t4 in range(4):
            tt = q * 4 + t4
            u = tt % 2
            S.dma('sp', xt[:, u, :], x_d[tt * 128:(tt + 1) * 128, :], writes=[("xt", u)], key=("xt", u))
            S.op('act', lambda e: e.activation(out=junk, in_=xt[:, u, :], func=AF.Square, scale=float(D) ** -0.5,
                                               accum_out=small[:, tt:tt + 1]),
                 reads=[("xt", u)], writes=["junk", ("ssq", tt)])
            S.op('act', lambda e: e.activation(out=small[:, 16 + tt:17 + tt], in_=small[:, tt:tt + 1], func=AF.Sqrt, bias=epsc[:, 0:1]),
                 reads=[("ssq", tt), "epsc"], writes=[("rs", tt)])
            S.op('dve', lambda e: e.reciprocal(out=small[:, 16 + tt:17 + tt], in_=small[:, 16 + tt:17 + tt]),
                 reads=[("rs", tt)], writes=[("rs", tt)])
            S.op('dve', lambda e: e.tensor_scalar(out=xn[:, qu, t4, :], in0=xt[:, u, :], scalar1=small[:, 16 + tt:17 + tt], scalar2=None,
                                                  op0=ALU.mult), reads=[("xt", u), ("rs", tt)], writes=[("xn", qu, t4)])

    def n_stage2(q):
        qu = q % 2
        for kc in range(16):
            bank = 4 + (kc // 2) % 4
            half = kc % 2
            pst = ps[:, bank, :].bitcast(BF16)[:, half * 512:(half + 1) * 512]
            fns = []
            for t4 in range(4):
                fns.append(lambda e, t4=t4, pst=pst: e.transpose(pst[:, t4 * 128:(t4 + 1) * 128],
                                                                 xn[:, qu, t4, kc * 128:(kc + 1) * 128], ident))
            S.multi('pe', fns, reads=[("xn", qu, t4) for t4 in range(4)] + ["cb"], writes=PSU(bank, half * 256, half * 256 + 256))
            dst = hT[:, kc, q * 512:(q + 1) * 512]
            wr = [("hT", q * 4 + t4) for t4 in range(4)]
            if kc % 2 == 0:
                S.op('act', lambda e, pst=pst, dst=dst: e.activation(out=dst, in_=pst, func=AF.Identity,
                                                                    bias=shiftc[:, kc:kc + 1], scale=gs[:, kc:kc + 1]),
                     reads=PSU(bank, half * 256, half * 256 + 256) + ["gs", "shiftc"], writes=wr)
            else:
                S.op('dve', lambda e, pst=pst, dst=dst: e.tensor_scalar(out=dst, in0=pst, scalar1=gs[:, kc:kc + 1],
                                                                       scalar2=shiftc[:, kc:kc + 1], op0=ALU.mult, op1=ALU.add),
                     reads=PSU(bank, half * 256, half * 256 + 256) + ["gs", "shiftc"], writes=wr)

    p0 = phase0()

    def p0_step(n):
        for _ in range(n):
            try:
                next(p0)
            except StopIteration:
                return

    p0_step(8)
    n_stage1(0)
    n_stage1(1)
    for q in range(4):
        n_stage2(q)
        p0_step(1)
        if q + 2 < 4:
            n_stage1(q + 2)
    p0_step(12)
    S.barrier()

    o = o_after_hT
    wbuf, o = carve(o, [128, 3, 16, 128], BF16)
    stgb, o = carve(o, [128, 2, T], BF16)
    stgf, o = carve(o, [128, 2, T], F32)
    amu, o = carve(o, [128, T + 1], F32)
    S.op('pool', lambda e: e.memset(amu[:, 0:1], 0.0), writes=["amu0"])
    assert o <= VP_OFF, o
    S.op('pool', lambda e: e.memset(Vp.rearrange("p a b c -> p (a b c)"), 0.0), writes=[("Vp", i) for i in range(17)] + ["Vp"])
    hT_all = [("hT", tt) for tt in range(NT)]

    def blk_kind(blk):
        if blk < 2: return ('copy', blk, None)
        if blk < 6: return ('v', blk - 4, None)
        if blk < 14: return ('q', 4 + (blk - 6), None)
        if blk < 22: return ('silu', 12 + (blk - 14), None)
        if blk == 22: return ('lerp', 0, 0)
        j, r = divmod(blk - 23, 4)
        if r < 3: return ('lerp', 1 + 3 * j + r, 1 + 3 * j + r)
        return ('silu', 20 + j, None)

    SEQ = [0, 1] + list(range(4, NBLK))
    for p_ in range(2):
        S.dma('pool', wbuf[:, p_ % 3], win_d[SEQ[p_]], writes=[("wbuf", p_ % 3)], key=("wb", p_ % 3))
    def phaseI(blks):
      bankrot = 0
      nb_i = 0
      nf_i = 0
      for p_ in blks:
          blk = SEQ[p_]
          wu = p_ % 3
          if p_ + 2 < len(SEQ):
              yield S.dma('pool', wbuf[:, (p_ + 2) % 3], win_d[SEQ[p_ + 2]], writes=[("wbuf", (p_ + 2) % 3)], key=("wb", (p_ + 2) % 3))
          kind, sidx, mucol = blk_kind(blk)
          if kind == 'v':
              for tt in range(NT):
                  bank = bankrot % 4; bankrot += 1
                  fns = [lambda e, kc=kc, bank=bank, tt=tt: e.matmul(ps[:, bank, 0:128], hT[:, kc, tt * 128:(tt + 1) * 128],
                                                                     wbuf[:, wu, kc, :], start=(kc == 0), stop=(kc == 15))
                         for kc in range(16)]
                  yield S.multi('pe', fns, reads=[("hT", tt), ("wbuf", wu)], writes=PSU(bank, 0, 128))
                  dst = Vp[:, 1 + tt, 2 * sidx:2 * sidx + 2, 64:128]
                  src = ps[:, bank, 0:128].rearrange("p (a b) -> p a b", a=2)
                  eng = 'act' if tt % 2 == 0 else 'dve'
                  if eng == 'act':
                      yield S.op('act', lambda e, dst=dst, src=src: e.activation(out=dst, in_=src, func=AF.Copy),
                           reads=PSU(bank, 0, 128), writes=[("Vp", 1 + tt)])
                  else:
                      yield S.op('dve', lambda e, dst=dst, src=src: e.tensor_copy(out=dst, in_=src),
                           reads=PSU(bank, 0, 128), writes=[("Vp", 1 + tt)])
              continue
          if kind == 'lerp':
              su = nf_i % 2; nf_i += 1
              stg = stgf[:, su, :]
              stgname = ("stgf", su)
          else:
              su = nb_i % 2; nb_i += 1
              stg = stgb[:, su, :]
              stgname = ("stgb", su)
          for wn in range(4):
              bank = bankrot % 4; bankrot += 1
              tok = slice(wn * 512, (wn + 1) * 512)
              for pc4 in range(2):
                  fns = [lambda e, kc=kc, bank=bank, tok=tok: e.matmul(ps[:, bank, :], wbuf[:, wu, kc, :], hT[:, kc, tok],
                                                                       start=(kc == 0), stop=(kc == 15)) for kc in range(pc4 * 8, pc4 * 8 + 8)]
                  yield S.multi('pe', fns, reads=hT_all[wn * 4:(wn + 1) * 4] + [("wbuf", wu)], writes=PSU(bank, 0, 512))
              pin = ps[:, bank, :]
              if kind == 'copy':
                  yield S.op('act', lambda e, pin=pin, tok=tok: e.activation(out=stg[:, tok], in_=pin, func=AF.Copy),
                       reads=PSU(bank, 0, 512), writes=[stgname])
              elif kind == 'q':
                  yield S.op('dve', lambda e, pin=pin, tok=tok: e.tensor_scalar(out=stg[:, tok], in0=pin, scalar1=0.125, scalar2=None,
                                                                         op0=ALU.mult), reads=PSU(bank, 0, 512), writes=[stgname])
              elif kind == 'silu':
                  yield S.op('act', lambda e, pin=pin, tok=tok: e.activation(out=stg[:, tok], in_=pin, func=AF.Silu),
                       reads=PSU(bank, 0, 512), writes=[stgname])
              else:
                  mc = pv[:, PV_MU + mucol:PV_MU + mucol + 1]
                  oc = omm[:, mucol:mucol + 1]
                  yield S.op('act', lambda e, pin=pin, wn=wn, mc=mc: e.activation(out=amu[:, 1 + wn * 512:1 + (wn + 1) * 512], in_=pin,
                                                                            func=AF.Copy, scale=mc),
                       reads=PSU(bank, 0, 512) + ["pv"], writes=[("amu", wn)])
                  rd = [("amu", wn)] + ([("amu", wn - 1)] if wn > 0 else ["amu0"])
                  yield S.op('dve', lambda e, pin=pin, wn=wn, oc=oc, tok=tok: e.scalar_tensor_tensor(
                      out=stg[:, tok], in0=pin, scalar=oc, in1=amu[:, wn * 512:(wn + 1) * 512], op0=ALU.mult, op1=ALU.add),
                      reads=PSU(bank, 0, 512) + rd + ["omm"], writes=[stgname])
          if kind == 'lerp':
              yield S.dma('sp', scrR[sidx], stg, reads=[stgname], writes=[("scrR", sidx)], key=("spf", su))
          else:
              yield S.dma('sp', scrA[sidx], stg, reads=[stgname], writes=[("scrA", sidx)], key=("spb", su))

    o = _vpend
    qT, o = carve(o, [128, 1, 2, T], BF16)
    kTg, o = carve(o, [128, 1, T + 128], BF16)
    gaT, o = carve(o, [128, 1, 2, T], BF16)
    PT, o = carve(o, [128, 2, 2, 512], BF16)
    rden, o = carve(o, [128, 2, 256], F32)
    ynum, o = carve(o, [128, 2, 256], F32)
    mixst, o = carve(o, [128, 2, 2, 128], BF16)
    assert o <= ARENA_W, o
    S.op('pool', lambda e: e.memset(kTg[:, :, 0:128], 0.0), writes=[("kTg", 0)])
    maskA = cb[:, CB_MASKA:CB_MASKA + 256]
    maskA0 = cb[:, CB_MASKA0:CB_MASKA0 + 256]
    OL = [cb[:, CB_OL0:CB_OL0 + 128], cb[:, CB_OL1:CB_OL1 + 128]]
    def a_load(g):
        gu = 0
        for pi in range(2):
            yield S.dma('sp', qT[:, gu, pi, :], scrA[4 + 2 * g + pi], reads=[("scrA", 4 + 2 * g + pi)], writes=[("qT", gu)],
                  key=("aq", gu), serialize=(pi == 0))
        ksrc = scrA[g // 2][64 * (g % 2):64 * (g % 2) + 64, :]
        yield S.dma('sp', kTg[0:64, gu, 128:], ksrc, reads=[("scrA", g // 2)], writes=[("kTg", gu)], key=("ak", gu))
        yield S.dma('sp', kTg[64:128, gu, 128:], ksrc, reads=[("scrA", g // 2)], writes=[("kTg", gu)], key=("ak", gu), serialize=False)

    def a_stage1(unit):
        g, tt = divmod(unit, NT)
        gu = 0
        u = unit % 2
        if tt == 0:
            yield from a_load(g)
        for e_ in range(2):
            bank = 4 + e_
            rows = slice(64 * e_, 64 * e_ + 64)
            fns = []
            for pi in range(2):
                for pc in range(2):
                    col = (pi * 2 + pc) * 128
                    fns.append(lambda e, bank=bank, rows=rows, pi=pi, pc=pc, col=col: e.matmul(
                        ps[:, bank, col:col + 128], kTg[rows, gu, (tt + pc) * 128:(tt + pc + 1) * 128],
                        qT[rows, gu, pi, tt * 128:(tt + 1) * 128], start=True, stop=True))
            yield S.multi('pe', fns, reads=[("kTg", gu), ("qT", gu)], writes=PSU(bank, 0, 512))
            yield S.op('act', lambda e, bank=bank, e_=e_: e.activation(out=PT[:, u, e_, :], in_=ps[:, bank, :], func=AF.Exp),
                 reads=PSU(bank, 0, 512), writes=[("PT", u, e_)])
        mk = maskA0 if tt == 0 else maskA
        ptv = PT[:, u].rearrange("p e (a b) -> p (e a) b", a=2)
        yield S.op('dve', lambda e, ptv=ptv, mk=mk: e.tensor_tensor(out=ptv, in0=ptv, in1=mk.unsqueeze(1).broadcast_to([128, 4, 256]),
                                                              op=ALU.mult),
             reads=[("PT", u, 0), ("PT", u, 1), "cb"], writes=[("PT", u, 0), ("PT", u, 1)])

    def a_stage2(unit):
        g, tt = divmod(unit, NT)
        gu = 0
        u = unit % 2
        nbank = 6 + u
        if tt == 0:
            for pi in range(2):
                yield S.dma('sp', gaT[:, gu, pi, :], scrA[12 + 2 * g + pi], reads=[("scrA", 12 + 2 * g + pi)], writes=[("gaT", gu)],
                            key=("ag", gu), serialize=(pi == 0))
        fns = []
        for pi in range(2):
            seq = [(e_, pc) for e_ in range(2) for pc in range(2)]
            for i, (e_, pc) in enumerate(seq):
                vl = Vp[:, tt + pc, g, 64:192] if e_ == 0 else Vp[:, tt + pc, g, 0:128]
                col = (pi * 2 + pc) * 128
                fns.append(lambda e, pi=pi, e_=e_, vl=vl, col=col, i=i: e.matmul(
                    ps[:, nbank, pi * 128:(pi + 1) * 128], vl, PT[:, u, e_, col:col + 128], start=(i == 0), stop=(i == 3)))
            for i, (e_, pc) in enumerate(seq):
                col = (pi * 2 + pc) * 128
                fns.append(lambda e, pi=pi, e_=e_, col=col, i=i: e.matmul(
                    ps[:, nbank, 256 + pi * 128:256 + (pi + 1) * 128], OL[e_], PT[:, u, e_, col:col + 128],
                    start=(i == 0), stop=False))
            pr = 2 * g + pi
            fns.append(lambda e, pi=pi, pr=pr: e.matmul(ps[:, nbank, 256 + pi * 128:256 + (pi + 1) * 128],
                                                       sinkl[0:1, pr * 128:(pr + 1) * 128], ones_row, start=False, stop=True))
        yield S.multi('pe', fns, reads=[("PT", u, 0), ("PT", u, 1), ("Vp", tt), ("Vp", tt + 1), "Vp", "cb", "cf", "sinkl"],
                writes=PSU(nbank, 0, 512))
        yield S.op('act', lambda e: e.activation(out=rden[:, u, :], in_=ps[:, nbank, 256:512], func=AF.Ln), reads=PSU(nbank, 256, 512),
             writes=[("rden", u)])
        yield S.op('act', lambda e: e.activation(out=rden[:, u, :], in_=rden[:, u, :], func=AF.Exp, scale=-1.0), reads=[("rden", u)],
             writes=[("rden", u)])
        yield S.op('dve', lambda e: e.tensor_tensor(out=ynum[:, u, :], in0=ps[:, nbank, 0:256], in1=rden[:, u, :], op=ALU.mult),
             reads=PSU(nbank, 0, 256) + [("rden", u)], writes=[("ynum", u)])
        yield S.op('pool', lambda e: e.tensor_tensor(out=mixst[:, u, :, :],
                                               in0=ynum[:, u, :].rearrange("p (a b) -> p a b", a=2),
                                               in1=gaT[:, gu, :, tt * 128:(tt + 1) * 128], op=ALU.mult),
             reads=[("ynum", u), ("gaT", gu)], writes=[("mixst", u)])
        yield S.dma('sp', scrM[2 * g:2 * g + 2, :, tt * 128:(tt + 1) * 128].rearrange("c p t -> p c t"), mixst[:, u, :, :],
              reads=[("mixst", u)], writes=[("scrM", g, tt)], key=("am", u))

    NU = 4 * NT

    def phaseA():
        yield from a_stage1(0)
        for unit in range(NU):
            if unit + 1 < NU:
                yield from a_stage1(unit + 1)
            yield from a_stage2(unit)

    for _ in phaseI(range(0, 20)):
        pass
    gens = [phaseI(range(20, len(SEQ))), phaseA()]
    while gens:
        for gen in list(gens):
            try:
                next(gen)
            except StopIteration:
                gens.remove(gen)
    S.barrier()

    WR = 256
    NWR = T // WR
    CW = WR // 64
    o = 0
    Hba, o = carve(o, [128, 8, 64], BF16)
    Hbb, o = carve(o, [128, 8, 64], BF16)
    Hb = [Hba, Hbb]
    S.op('pool', lambda e: e.memset(Hba.rearrange("p a b -> p (a b)"), 0.0), writes=[("Hb", 0, 0), ("Hb", 0, 1)])
    S.op('pool', lambda e: e.memset(Hbb.rearrange("p a b -> p (a b)"), 0.0), writes=[("Hb", 1, 0), ("Hb", 1, 1)])
    bones = cf[:, CF_BONES:CF_BONES + 128]
    bones64 = cf[:, CF_BONES64:CF_BONES64 + 128]
    resetm = cf[:, CF_RESET:CF_RESET + WR]
    mST = cb[:, CB_MST:CB_MST + 64]
    mS = cb[:, CB_MS:CB_MS + 64]
    mIT = cb[:, CB_MIT:CB_MIT + 64]
    I64 = cb[:, CB_I64:CB_I64 + 64]
    bc4 = lambda m: m.unsqueeze(1).broadcast_to([128, 4, 64])
    v4 = lambda a: a.rearrange("p (a b) -> p a b", a=4)

    class LB:
        pass
    lanes = []
    for g in range(2):
        L = LB()
        L.tw, o = carve(o, [128, WR], BF16)
        L.wdl, o = carve(o, [128, WR], F32)
        L.rkv, o = carve(o, [128, 2, 3, WR], F32)
        L.tmp = []
        for i in range(12):
            t_, o = carve(o, [128, WR], F32)
            L.tmp.append(t_)
        L.tb = []
        for i in range(3):
            t_, o = carve(o, [128, WR], BF16)
            L.tb.append(t_)
        L.grs, L.RT, L.AT, L.BT, L.KT, L.VT, L.BCt, L.KCt, L.BON, L.Wend = [], [], [], [], [], [], [], [], [], []
        for wp in range(2):
            for lst, shp, dt_ in ((L.grs, [128, 4, WR], BF16), (L.RT, [128, 4, WR], BF16), (L.AT, [128, 4, WR], BF16),
                                  (L.BT, [128, 4, WR], BF16), (L.KT, [128, 4, WR], BF16), (L.VT, [128, CW, 4, 64], BF16),
                                  (L.BCt, [128, CW, 4, 64], BF16), (L.KCt, [128, CW, 4, 64], BF16),
                                  (L.BON, [128, 4, WR], F32), (L.Wend, [128, 4, CW], F32)):
                t_, o = carve(o, shp, dt_)
                lst.append(t_)
        L.Yw, o = carve(o, [128, 4, WR], F32)
        L.mixr, o = carve(o, [128, 4, WR], BF16)
        L.ptmp = []
        for i in range(3):
            t_, o = carve(o, [128, WR], F32)
            L.ptmp.append(t_)
        L.NNs = []
        for i in range(2):
            t_, o = carve(o, [128, 2, 256], BF16)
            L.NNs.append(t_)
        L.PTm = []
        for i in range(2):
            t_, o = carve(o, [128, 256], BF16)
            L.PTm.append(t_)
        L.G2 = []
        for i in range(2):
            t_, o = carve(o, [128, 2, 256], BF16)
            L.G2.append(t_)
        L.G3 = []
        for i in range(2):
            t_, o = carve(o, [128, 256], BF16)
            L.G3.append(t_)
        L.Xb, o = carve(o, [128, 256], BF16)
        L.Ub, o = carve(o, [128, 256], BF16)
        lanes.append(L)
    assert o <= ARENA_W, o
    LO, HI = slice(0, 256), slice(256, 512)

    def prep_stream(grp):
        L = lanes[grp]
        N_ = lambda n, *a: (n, grp) + a
        P0, P1 = 4 * grp, 4 * grp + 1
        tw, wdl, rkv = L.tw, L.wdl, L.rkv
        sw, a_, cs, W_, Wi, Wp, WC, kkr, sq, kkn, kmod, ba = L.tmp
        vlb, bCb, kCb = L.tb
        for wn in range(NWR):
            wp = wn % 2
            while chain_done[grp] < wn - 1:
                yield None
            grs, RT, AT, BT, KT, VT, BCt, KCt, BON, Wend = (L.grs[wp], L.RT[wp], L.AT[wp], L.BT[wp], L.KT[wp], L.VT[wp],
                                                            L.BCt[wp], L.KCt[wp], L.BON[wp], L.Wend[wp])
            wtok = slice(wn * WR, (wn + 1) * WR)
            yield S.dma('sp', wdl, scrR[0][:, wtok], reads=[("scrR", 0)], writes=[N_("wdl")], key=N_("rw"))
            yield S.op('act', lambda e: e.activation(out=tw[0:64, :], in_=wdl[0:64, :], func=AF.Tanh), reads=[N_("wdl")], writes=[N_("tw")])
            yield S.op('dve', lambda e: e.tensor_copy(out=tw[64:128, :], in_=wdl[64:128, :]), reads=[N_("wdl")], writes=[N_("tw1")])
            for jj in range(4):
                j = grp * 4 + jj
                ru = jj % 2
                rl, kl, vl_ = rkv[:, ru, 0, :], rkv[:, ru, 1, :], rkv[:, ru, 2, :]
                RKV = N_("rkv", ru)
                for r_ in range(3):
                    yield S.dma('sp', rkv[:, ru, r_, :], scrR[1 + 3 * j + r_][:, wtok], reads=[("scrR", 1 + 3 * j + r_)],
                                writes=[RKV], key=N_("rr", ru), serialize=(r_ == 0))
                yield S.dma('sp', grs[:, jj, :], scrA[20 + j][:, wtok], reads=[("scrA", 20 + j)], writes=[N_("grs", wp, jj)], key=N_("rg", jj))
                col = lambda base: pv[:, base + j:base + j + 1]
                yield S.op('pe', lambda e: e.matmul(ps[:, P0, LO], lora[0:64, j * 128:(j + 1) * 128], tw[0:64, :], start=True, stop=True),
                           reads=["lora", N_("tw")], writes=PSU(P0, 0, 256))
                yield S.op('pe', lambda e: e.matmul(ps[:, P1, LO], lora[64:128, j * 128:(j + 1) * 128], tw[64:128, :], start=True, stop=True),
                           reads=["lora", N_("tw1")], writes=PSU(P1, 0, 256))
                yield S.op('act', lambda e: e.activation(out=sw, in_=ps[:, P0, LO], func=AF.Sigmoid, bias=col(PV_W0)),
                           reads=PSU(P0, 0, 256) + ["pv"], writes=[N_("sw")])
                yield S.op('act', lambda e: e.activation(out=a_, in_=ps[:, P1, LO], func=AF.Sigmoid, bias=col(PV_A0)),
                           reads=PSU(P1, 0, 256) + ["pv"], writes=[N_("a_")])
                yield S.op('dve', lambda e: e.tensor_tensor_scan(out=cs, data0=resetm, data1=sw, initial=0.0, op0=ALU.mult, op1=ALU.add),
                           reads=[N_("sw"), "cf"], writes=[N_("cs")])
                yield S.op('act', lambda e: e.activation(out=sq, in_=kl, func=AF.Square, scale=col(PV_KK)),
                           reads=[RKV, "pv"], writes=[N_("sq")])
                yield S.op('pe', lambda e: e.matmul(ps[:, P0, HI], bones, sq, start=True, stop=True), reads=[N_("sq"), "cf"], writes=PSU(P0, 256, 512))
                yield S.op('act', lambda e: e.activation(out=W_, in_=cs, func=AF.Exp, scale=-C0), reads=[N_("cs")], writes=[N_("W_")])
                yield S.op('act', lambda e: e.activation(out=Wi, in_=cs, func=AF.Exp, scale=C0), reads=[N_("cs")], writes=[N_("Wi")])
                yield S.op('dve', lambda e: e.tensor_tensor(out=Wp, in0=cs, in1=sw, op=ALU.subtract), reads=[N_("cs"), N_("sw")], writes=[N_("Wp")])
                yield S.op('act', lambda e: e.activation(out=Wp, in_=Wp, func=AF.Exp, scale=-C0), reads=[N_("Wp")], writes=[N_("Wp")])
                cs3 = cs.rearrange("p (a b) -> p a b", a=CW)
                yield S.op('dve', lambda e: e.tensor_tensor(out=WC.rearrange("p (a b) -> p a b", a=CW),
                                                            in0=cs3[:, :, 63:64].broadcast_to([128, CW, 64]), in1=cs3, op=ALU.subtract),
                           reads=[N_("cs")], writes=[N_("WC")])
                yield S.op('act', lambda e: e.activation(out=WC, in_=WC, func=AF.Exp, scale=-C0), reads=[N_("WC")], writes=[N_("WC")])
                yield S.op('act', lambda e: e.activation(out=Wend[:, jj, :], in_=cs3[:, :, 63], func=AF.Exp, scale=-C0),
                           reads=[N_("cs")], writes=[N_("Wend", wp, jj)])
                yield S.op('dve', lambda e: e.tensor_scalar(out=sq, in0=ps[:, P0, HI], scalar1=1e-24, scalar2=None, op0=ALU.max),
                           reads=PSU(P0, 256, 512), writes=[N_("sq")])
                yield S.op('act', lambda e: e.activation(out=sq, in_=sq, func=AF.Ln), reads=[N_("sq")], writes=[N_("sq")])
                yield S.op('act', lambda e: e.activation(out=sq, in_=sq, func=AF.Exp, scale=-0.5), reads=[N_("sq")], writes=[N_("sq")])
                yield S.op('dve', lambda e: e.scalar_tensor_tensor(out=kkn, in0=kl, scalar=col(PV_KK), in1=sq, op0=ALU.mult, op1=ALU.mult),
                           reads=[RKV, "pv", N_("sq")], writes=[N_("kkn")])
                yield S.op('act', lambda e: e.activation(out=kmod, in_=a_, func=AF.Identity, scale=col(PV_KA), bias=omka[:, j:j + 1]),
                           reads=[N_("a_"), "pv", "omka"], writes=[N_("kmod")])
                yield S.op('pool', lambda e: e.tensor_tensor(out=kmod, in0=kmod, in1=kl, op=ALU.mult),
                           reads=[N_("kmod"), RKV], writes=[N_("kmod")])
                yield S.op('pool', lambda e: e.tensor_tensor(out=RT[:, jj, :], in0=rl, in1=W_, op=ALU.mult),
                           reads=[RKV, N_("W_")], writes=[N_("RT", wp, jj)])
                yield S.op('dve', lambda e: e.scalar_tensor_tensor(out=AT[:, jj, :], in0=kkn, scalar=-1.0, in1=Wp, op0=ALU.mult, op1=ALU.mult),
                           reads=[N_("kkn"), N_("Wp")], writes=[N_("AT", wp, jj)])
                yield S.op('pool', lambda e: e.tensor_tensor(out=ba, in0=kkn, in1=a_, op=ALU.mult), reads=[N_("kkn"), N_("a_")], writes=[N_("ba")])
                yield S.op('pool', lambda e: e.tensor_tensor(out=BT[:, jj, :], in0=ba, in1=Wi, op=ALU.mult), reads=[N_("ba"), N_("Wi")], writes=[N_("BT", wp, jj)])
                yield S.op('pool', lambda e: e.tensor_tensor(out=bCb, in0=ba, in1=WC, op=ALU.mult), reads=[N_("ba"), N_("WC")], writes=[N_("bCb")])
                yield S.op('pool', lambda e: e.tensor_tensor(out=KT[:, jj, :], in0=kmod, in1=Wi, op=ALU.mult), reads=[N_("kmod"), N_("Wi")], writes=[N_("KT", wp, jj)])
                yield S.op('pool', lambda e: e.tensor_tensor(out=kCb, in0=kmod, in1=WC, op=ALU.mult), reads=[N_("kmod"), N_("WC")], writes=[N_("kCb")])
                yield S.op('dve', lambda e: e.scalar_tensor_tensor(out=kkr, in0=rl, scalar=col(PV_RK), in1=kmod, op0=ALU.mult, op1=ALU.mult),
                           reads=[RKV, "pv", N_("kmod")], writes=[N_("kkr")])
                yield S.op('pe', lambda e: e.matmul(ps[:, P1, HI], bones, kkr, start=True, stop=True), reads=[N_("kkr"), "cf"], writes=PSU(P1, 256, 512))
                yield S.op('dve', lambda e: e.tensor_tensor(out=BON[:, jj, :], in0=ps[:, P1, HI], in1=vl_, op=ALU.mult),
                           reads=PSU(P1, 256, 512) + [RKV], writes=[N_("BON", wp, jj)])
                if POOL_COPY:
                    yield S.op('pool', lambda e: e.tensor_copy(out=vlb, in_=vl_), reads=[RKV], writes=[N_("vlb")])
                else:
                    yield S.op('act', lambda e: e.activation(out=vlb, in_=vl_, func=AF.Copy), reads=[RKV], writes=[N_("vlb")])
                for si, (src, sname, dst, dname, bank, hh) in enumerate([(vlb, "vlb", VT, "VT", P0, 0), (bCb, "bCb", BCt, "BCt", P1, 0),
                                                                          (kCb, "kCb", KCt, "KCt", P0, 256)]):
                    fns = []
                    for c in range(CW):
                        for e_ in range(2):
                            rows = slice(64 * e_, 64 * e_ + 64)
                            fns.append(lambda e, src=src, bank=bank, rows=rows, c=c, hh=hh: e.matmul(
                                ps[rows, bank, hh + c * 64:hh + (c + 1) * 64], src[rows, c * 64:(c + 1) * 64], I64[rows, :], start=True, stop=True))
                    yield S.multi('pe', fns, reads=[N_(sname), "cb"], writes=PSU(bank, hh, hh + 256))
                    pv3 = ps[:, bank, hh:hh + 256].rearrange("p (a b) -> p a b", a=CW)
                    if si == 1:
                        yield S.op('dve', lambda e, dst=dst, pv3=pv3: e.tensor_copy(out=dst[:, :, jj, :], in_=pv3),
                                   reads=PSU(bank, hh, hh + 256), writes=[N_(dname, wp, jj)])
                    else:
                        yield S.op('act', lambda e, dst=dst, pv3=pv3: e.activation(out=dst[:, :, jj, :], in_=pv3, func=AF.Copy),
                                   reads=PSU(bank, hh, hh + 256), writes=[N_(dname, wp, jj)])
            prep_done[grp] = wn + 1

    def chain_stream(grp):
        L = lanes[grp]
        N_ = lambda n, *a: (n, grp) + a
        C0, C1 = 4 * grp + 2, 4 * grp + 3
        Yw, mixr = L.Yw, L.mixr
        NNs, PTm, G2, G3, Xb, Ub = L.NNs, L.PTm, L.G2, L.G3, L.Xb, L.Ub
        hbpar = 0
        for wn in range(NWR):
            wp = wn % 2
            grs, RT, AT, BT, KT, VT, BCt, KCt, BON, Wend = (L.grs[wp], L.RT[wp], L.AT[wp], L.BT[wp], L.KT[wp], L.VT[wp],
                                                            L.BCt[wp], L.KCt[wp], L.BON[wp], L.Wend[wp])
            allj = lambda n: [N_(n, wp, jj) for jj in range(4)]
            wtok = slice(wn * WR, (wn + 1) * WR)
            while prep_done[grp] < wn + 1:
                yield None
            for c in range(CW):
                cu = c % 2
                tokc = slice(c * 64, (c + 1) * 64)
                fa, fb, fc = [], [], []
                for jj in range(4):
                    for e_ in range(2):
                        rows = slice(64 * e_, 64 * e_ + 64)
                        cs_ = slice(jj * 64, (jj + 1) * 64)
                        cs2 = slice(256 + jj * 64, 256 + (jj + 1) * 64)
                        fa.append(lambda e, rows=rows, cs_=cs_, jj=jj: e.matmul(ps[rows, C0, cs_], BT[rows, jj, tokc], AT[rows, jj, tokc], start=True, stop=True))
                        fa.append(lambda e, rows=rows, cs2=cs2, jj=jj: e.matmul(ps[rows, C0, cs2], AT[rows, jj, tokc], BT[rows, jj, tokc], start=True, stop=True))
                        fb.append(lambda e, rows=rows, cs_=cs_, jj=jj: e.matmul(ps[rows, C1, cs_], KT[rows, jj, tokc], AT[rows, jj, tokc], start=True, stop=True))
                        fb.append(lambda e, rows=rows, cs2=cs2, jj=jj: e.matmul(ps[rows, C1, cs2], BT[rows, jj, tokc], RT[rows, jj, tokc], start=True, stop=True))
                        fc.append(lambda e, rows=rows, cs_=cs_, jj=jj: e.matmul(ps[rows, C0, cs_], KT[rows, jj, tokc], RT[rows, jj, tokc], start=True, stop=True))
                yield S.multi('pe', fa, reads=allj("BT") + allj("AT"), writes=PSU(C0, 0, 512))
                yield S.multi('pe', fb, reads=allj("BT") + allj("AT") + allj("KT") + allj("RT"), writes=PSU(C1, 0, 512))
                N0 = NNs[0]
                yield S.op('dve', lambda e: e.tensor_tensor(out=v4(N0[:, 0, :]), in0=v4(ps[:, C0, LO]), in1=bc4(mST), op=ALU.mult),
                           reads=PSU(C0, 0, 256) + ["cb"], writes=[N_("NN", 0)])
                yield S.op('dve', lambda e: e.tensor_tensor(out=v4(N0[:, 1, :]), in0=v4(ps[:, C0, HI]), in1=bc4(mS), op=ALU.mult),
                           reads=PSU(C0, 256, 512) + ["cb"], writes=[N_("NN", 0)])
                yield S.multi('pe', fc, reads=allj("KT") + allj("RT"), writes=PSU(C0, 0, 256))
                yield S.op('pool', lambda e: e.tensor_tensor(out=v4(PTm[0]), in0=v4(N0[:, 0, :]), in1=bc4(I64), op=ALU.add),
                           reads=[N_("NN", 0), "cb"], writes=[N_("PTm", 0)])
                yield S.op('dve', lambda e: e.tensor_tensor(out=v4(G2[cu][:, 0, :]), in0=v4(ps[:, C1, LO]), in1=bc4(mST), op=ALU.mult),
                           reads=PSU(C1, 0, 256) + ["cb"], writes=[N_("G2", cu)])
                yield S.op('dve', lambda e: e.tensor_tensor(out=v4(G2[cu][:, 1, :]), in0=v4(ps[:, C1, HI]), in1=bc4(mIT), op=ALU.mult),
                           reads=PSU(C1, 256, 512) + ["cb"], writes=[N_("G2", cu)])
                yield S.op('dve', lambda e: e.tensor_tensor(out=v4(G3[cu]), in0=v4(ps[:, C0, LO]), in1=bc4(mIT), op=ALU.mult),
                           reads=PSU(C0, 0, 256) + ["cb"], writes=[N_("G3", cu)])
                for l in range(5):
                    a0_, a1_ = NNs[l % 2], NNs[(l + 1) % 2]
                    fns = []
                    for jj in range(4):
                        for e_ in range(2):
                            rows = slice(64 * e_, 64 * e_ + 64)
                            cs_ = slice(jj * 64, (jj + 1) * 64)
                            cs2 = slice(256 + jj * 64, 256 + (jj + 1) * 64)
                            fns.append(lambda e, rows=rows, cs_=cs_, cs2=cs2, a0_=a0_: e.matmul(
                                ps[rows, C1, cs2], a0_[rows, 0, cs_], a0_[rows, 1, cs_], start=True, stop=True))
                            fns.append(lambda e, rows=rows, cs_=cs_, a0_=a0_: e.matmul(
                                ps[rows, C1, cs_], a0_[rows, 1, cs_], a0_[rows, 0, cs_], start=True, stop=True))
                    yield S.multi('pe', fns, reads=[N_("NN", l % 2)], writes=PSU(C1, 0, 512))
                    if l % 2 == 0 or not DVE_EVAC:
                        yield S.op('act', lambda e, a1_=a1_: e.activation(out=a1_.rearrange("p a b -> p (a b)"), in_=ps[:, C1, :], func=AF.Copy),
                                   reads=PSU(C1, 0, 512), writes=[N_("NN", (l + 1) % 2)])
                    else:
                        yield S.op('dve', lambda e, a1_=a1_: e.tensor_copy(out=a1_.rearrange("p a b -> p (a b)"), in_=ps[:, C1, :]),
                                   reads=PSU(C1, 0, 512), writes=[N_("NN", (l + 1) % 2)])
                    fns = []
                    for jj in range(4):
                        for e_ in range(2):
                            rows = slice(64 * e_, 64 * e_ + 64)
                            cs_ = slice(jj * 64, (jj + 1) * 64)
                            cs2 = slice(256 + jj * 64, 256 + (jj + 1) * 64)
                            fns.append(lambda e, rows=rows, cs_=cs_, cs2=cs2, a1_=a1_, l=l: e.matmul(
                                ps[rows, C0, cs2], a1_[rows, 1, cs_], PTm[l % 2][rows, cs_], start=True, stop=True))
                    yield S.multi('pe', fns, reads=[N_("NN", (l + 1) % 2), N_("PTm", l % 2)], writes=PSU(C0, 256, 512))
                    yield S.op('dve', lambda e, l=l: e.tensor_tensor(out=PTm[(l + 1) % 2], in0=ps[:, C0, HI], in1=PTm[l % 2], op=ALU.add),
                               reads=PSU(C0, 256, 512) + [N_("PTm", l % 2)], writes=[N_("PTm", (l + 1) % 2)])
                TT = PTm[1]
                hb = Hb[hbpar]
                hbn = Hb[1 - hbpar]
                fns = []
                for jj in range(4):
                    j = grp * 4 + jj
                    for e_ in range(2):
                        rows = slice(64 * e_, 64 * e_ + 64)
                        cs_ = slice(jj * 64, (jj + 1) * 64)
                        fns.append(lambda e, rows=rows, cs_=cs_, jj=jj, j=j: e.matmul(ps[rows, C1, cs_], AT[rows, jj, tokc], hb[rows, j, :], start=True, stop=False))
                        fns.append(lambda e, rows=rows, cs_=cs_, jj=jj: e.matmul(ps[rows, C1, cs_], G2[cu][rows, 0, cs_], VT[rows, c, jj, :], start=False, stop=True))
                yield S.multi('pe', fns, reads=allj("AT") + [("Hb", hbpar, grp), N_("G2", cu)] + allj("VT"), writes=PSU(C1, 0, 256))
                yield S.op('act', lambda e: e.activation(out=Xb, in_=ps[:, C1, LO], func=AF.Copy), reads=PSU(C1, 0, 256), writes=[N_("Xb")])
                fns = []
                for jj in range(4):
                    for e_ in range(2):
                        rows = slice(64 * e_, 64 * e_ + 64)
                        cs_ = slice(jj * 64, (jj + 1) * 64)
                        cs2 = slice(256 + jj * 64, 256 + (jj + 1) * 64)
                        fns.append(lambda e, rows=rows, cs_=cs_, cs2=cs2: e.matmul(ps[rows, C1, cs2], TT[rows, cs_], Xb[rows, cs_], start=True, stop=True))
                yield S.multi('pe', fns, reads=[N_("PTm", 1), N_("Xb")], writes=PSU(C1, 256, 512))
                yield S.op('act', lambda e: e.activation(out=Ub, in_=ps[:, C1, HI], func=AF.Copy), reads=PSU(C1, 256, 512), writes=[N_("Ub")])
                fns = []
                for jj in range(4):
                    j = grp * 4 + jj
                    for e_ in range(2):
                        rows = slice(64 * e_, 64 * e_ + 64)
                        cs_ = slice(jj * 64, (jj + 1) * 64)
                        cs2 = slice(256 + jj * 64, 256 + (jj + 1) * 64)
                        fns.append(lambda e, rows=rows, cs2=cs2, jj=jj, j=j: e.matmul(ps[rows, C0, cs2], hb[rows, j, :], RT[rows, jj, tokc], start=True, stop=False))
                        fns.append(lambda e, rows=rows, cs_=cs_, cs2=cs2: e.matmul(ps[rows, C0, cs2], Ub[rows, cs_], G2[cu][rows, 1, cs_], start=False, stop=False))
                        fns.append(lambda e, rows=rows, cs_=cs_, cs2=cs2, jj=jj: e.matmul(ps[rows, C0, cs2], VT[rows, c, jj, :], G3[cu][rows, cs_], start=False, stop=True))
                        fns.append(lambda e, rows=rows, cs_=cs_, jj=jj: e.matmul(ps[rows, C0, cs_], BCt[rows, c, jj, :], Ub[rows, cs_], start=True, stop=False))
                        fns.append(lambda e, rows=rows, cs_=cs_, jj=jj: e.matmul(ps[rows, C0, cs_], KCt[rows, c, jj, :], VT[rows, c, jj, :], start=False, stop=True))
                yield S.multi('pe', fns, reads=[("Hb", hbpar, grp), N_("Ub"), N_("G2", cu), N_("G3", cu)] + allj("RT") + allj("VT") + allj("BCt") + allj("KCt"),
                              writes=PSU(C0, 0, 512))
                H4 = Hst[:, grp * 4:(grp + 1) * 4, :]
                yield S.op('dve', lambda e: e.tensor_tensor(out=H4, in0=H4, in1=Wend[:, :, c:c + 1].broadcast_to([128, 4, 64]), op=ALU.mult),
                           reads=[N_("H")] + allj("Wend"), writes=[N_("H")])
                yield S.op('dve', lambda e: e.tensor_tensor(out=H4, in0=v4(ps[:, C0, LO]), in1=H4, op=ALU.add),
                           reads=PSU(C0, 0, 256) + [N_("H")], writes=[N_("H")])
                if POOL_COPY:
                    yield S.op('pool', lambda e: e.tensor_copy(out=hbn[:, grp * 4:(grp + 1) * 4, :], in_=H4),
                               reads=[N_("H")], writes=[("Hb", 1 - hbpar, grp)])
                else:
                    yield S.op('act', lambda e: e.activation(out=hbn[:, grp * 4:(grp + 1) * 4, :], in_=H4, func=AF.Copy),
                               reads=[N_("H")], writes=[("Hb", 1 - hbpar, grp)])
                yield S.op('act', lambda e: e.activation(out=Yw[:, :, tokc], in_=v4(ps[:, C0, HI]), func=AF.Copy),
                           reads=PSU(C0, 256, 512), writes=[N_("Yw")])
                hbpar = 1 - hbpar
            for jj in range(4):
                j = grp * 4 + jj
                col = lambda base, j=j: pv[:, base + j:base + j + 1]
                yc, sq_, yn = L.ptmp
                yield S.op('pe', lambda e: e.matmul(ps[:, C1, LO], bones64, Yw[:, jj, :], start=True, stop=True), reads=[N_("Yw"), "cf"], writes=PSU(C1, 0, 256))
                yield S.op('dve', lambda e: e.tensor_tensor(out=yc, in0=Yw[:, jj, :], in1=ps[:, C1, LO], op=ALU.subtract),
                           reads=[N_("Yw")] + PSU(C1, 0, 256), writes=[N_("yc")])
                yield S.op('pool', lambda e: e.tensor_tensor(out=sq_, in0=yc, in1=yc, op=ALU.mult), reads=[N_("yc")], writes=[N_("sq_")])
                yield S.op('pe', lambda e: e.matmul(ps[:, C1, HI], bones64, sq_, start=True, stop=True), reads=[N_("sq_"), "cf"], writes=PSU(C1, 256, 512))
                yield S.op('act', lambda e: e.activation(out=sq_, in_=ps[:, C1, HI], func=AF.Ln, bias=epsc[:, 1:2]),
                           reads=PSU(C1, 256, 512) + ["epsc"], writes=[N_("sq_")])
                yield S.op('act', lambda e: e.activation(out=sq_, in_=sq_, func=AF.Exp, scale=-0.5), reads=[N_("sq_")], writes=[N_("sq_")])
                yield S.op('pool', lambda e: e.tensor_tensor(out=yn, in0=yc, in1=sq_, op=ALU.mult), reads=[N_("yc"), N_("sq_")], writes=[N_("yn")])
                yield S.op('act', lambda e: e.activation(out=yn, in_=yn, func=AF.Identity, bias=col(PV_LNB), scale=col(PV_LNW)),
                           reads=[N_("yn"), "pv"], writes=[N_("yn")])
                yield S.op('pool', lambda e: e.tensor_tensor(out=yn, in0=yn, in1=BON[:, jj, :], op=ALU.add), reads=[N_("yn"), N_("BON", wp, jj)], writes=[N_("yn")])
                yield S.op('pool', lambda e: e.tensor_tensor(out=mixr[:, jj, :], in0=yn, in1=grs[:, jj, :], op=ALU.mult),
                           reads=[N_("yn"), N_("grs", wp, jj)], writes=[N_("mixr", jj)])
                yield S.dma('pool', scrM[8 + j][:, wtok], mixr[:, jj, :], reads=[N_("mixr", jj)], writes=[("scrM", 8 + j, wn)], key=N_("rm", jj))
            chain_done[grp] = wn + 1

    prep_done = [0, 0]
    chain_done = [0, 0]
    gens = [prep_stream(0), prep_stream(1), chain_stream(0), chain_stream(1)]
    while gens:
        for gen in list(gens):
            try:
                next(gen)
            except StopIteration:
                gens.remove(gen)
    S.barrier()

    o = 0
    wo, o = carve(o, [128, 16, 2048], BF16)
    mt, o = carve(o, [128, 2, 16, 128], BF16)
    xt2, o = carve(o, [128, 2, D], F32)
    ot, o = carve(o, [128, 2, D], F32)
    assert o <= ARENA_W, o
    for nb in range(4):
        S.dma('pool', wo[:, :, nb * 512:(nb + 1) * 512], wout_d[nb], writes=[("wo", nb)], key=("wo", nb))
    allM = []
    def o_load(tt):
        u = tt % 2
        tsl = slice(tt * 128, (tt + 1) * 128)
        S.dma('sp', mt[:, u], scrM[:, :, tsl].rearrange("c p t -> p c t"), reads=allM, writes=[("mt", u)], key=("om", u))
        S.dma('sp', xt2[:, u, :], x_d[tsl, :], writes=[("xt2", u)], key=("ox", u))

    o_load(0)
    for tt in range(NT):
        u = tt % 2
        tsl = slice(tt * 128, (tt + 1) * 128)
        if tt + 1 < NT:
            o_load(tt + 1)
        for nb in range(4):
            bank = 4 * u + nb
            fns = [lambda e, kc=kc, bank=bank, nb=nb: e.matmul(ps[:, bank, :], mt[:, u, kc, :], wo[:, kc, nb * 512:(nb + 1) * 512],
                                                               start=(kc == 0), stop=(kc == 15)) for kc in range(16)]
            S.multi('pe', fns, reads=[("mt", u), ("wo", nb)], writes=PSU(bank, 0, 512))
        pso = ps[:, 4 * u:4 * u + 4, :]
        allb = [r for b in range(4 * u, 4 * u + 4) for r in PSU(b, 0, 512)]
        S.op('act', lambda e: e.activation(out=ot[:, u, :].rearrange("p (a b) -> p a b", a=4), in_=pso, func=AF.Square,
                                           scale=float(D) ** -0.5, accum_out=small[:, 32 + tt:33 + tt]),
             reads=allb, writes=[("ot", u), ("ss2", tt)])
        S.op('act', lambda e: e.activation(out=small[:, 48 + tt:49 + tt], in_=small[:, 32 + tt:33 + tt], func=AF.Sqrt, bias=epsc[:, 0:1]),
             reads=[("ss2", tt), "epsc"], writes=[("rs2", tt)])
        S.op('dve', lambda e: e.reciprocal(out=small[:, 48 + tt:49 + tt], in_=small[:, 48 + tt:49 + tt]),
             reads=[("rs2", tt)], writes=[("rs2", tt)])
        S.op('dve', lambda e: e.scalar_tensor_tensor(out=ot[:, u, :].rearrange("p (a b) -> p a b", a=4), in0=pso,
                                                     scalar=small[:, 48 + tt:49 + tt], in1=gpb[:].rearrange("p (a b) -> p a b", a=4),
                                                     op0=ALU.mult, op1=ALU.mult),
             reads=allb + [("rs2", tt), "gpb"], writes=[("ot", u)])
        S.op('pool', lambda e: e.tensor_tensor(out=ot[:, u, :], in0=ot[:, u, :], in1=xt2[:, u, :], op=ALU.add),
             reads=[("ot", u), ("xt2", u)], writes=[("ot", u)])
        S.dma('pool', out_d[tsl, :], ot[:, u, :], reads=[("ot", u)], writes=[("out", tt)], key=("oo", u))
    S.barrier()
    es.close()
    return nc, S


_CACHE = {}


def _host_consts():
    cbm = np.zeros((128, CB_N), np.float32)
    cbm[:, CB_IDENT:CB_IDENT + 128] = np.eye(128, dtype=np.float32)
    s = np.arange(128)[:, None]
    q = np.arange(128)[None, :]
    prev = (s > q).astype(np.float32)
    cur = (s <= q).astype(np.float32)
    cbm[:, CB_MASKA:CB_MASKA + 128] = prev
    cbm[:, CB_MASKA + 128:CB_MASKA + 256] = cur
    cbm[:, CB_MASKA0 + 128:CB_MASKA0 + 256] = cur
    p = (np.arange(128) % 64)[:, None]
    f = np.arange(64)[None, :]
    cbm[:, CB_MST:CB_MST + 64] = (p < f)
    cbm[:, CB_MS:CB_MS + 64] = (f < p)
    cbm[:, CB_MIT:CB_MIT + 64] = (p <= f)
    cbm[:, CB_I64:CB_I64 + 64] = (p == f)
    cbm[:, CB_OL0:CB_OL0 + 64] = 1.0
    cbm[:, CB_OL1 + 64:CB_OL1 + 128] = 1.0
    cfm = np.zeros((128, CF_N), np.float32)
    blk = (np.arange(128)[:, None] // 64 == np.arange(128)[None, :] // 64).astype(np.float32)
    cfm[:, CF_BONES:CF_BONES + 128] = blk
    cfm[:, CF_BONES64:CF_BONES64 + 128] = blk / 64.0
    rm = np.ones(512, np.float32)
    rm[::64] = 0.0
    cfm[:, CF_RESET:CF_RESET + 512] = rm[None, :]
    cfm[:, CF_ONES:CF_ONES + 128] = 1.0
    return cbm, cfm


def _colidx():
    idx = []
    idx += [np.arange(1024, 1152), np.arange(1152, 1280), np.arange(1024, 1152), np.arange(1152, 1280)]
    idx.append(np.arange(1280, 1536))
    idx.append(np.arange(0, 1024))
    idx.append(np.arange(1536, 2560))
    R0 = 2560
    idx.append(np.arange(R0 + 3072, R0 + 3200))
    for j in range(8):
        for base in (0, 1024, 2048, 3200):
            idx.append(np.arange(R0 + base + j * 128, R0 + base + (j + 1) * 128))
    idx = np.concatenate(idx)
    assert idx.size == NBLK * 128
    return idx


def kernel(x, c, w_ada, b_ada, pre_norm_g, post_norm_g, w_in, w_out, attn_sinks,
           rwkv_mu, rwkv_w0, rwkv_w_up, rwkv_a0, rwkv_a_up, rwkv_k_k, rwkv_k_a,
           rwkv_r_k, rwkv_ln_w, rwkv_ln_b):
    f = lambda a: np.ascontiguousarray(np.asarray(a, dtype=np.float32))
    x = f(x); c = f(c)
    B = x.shape[0]
    if 'nc' not in _CACHE:
        _CACHE['nc'] = build_program()
    nc, S = _CACHE['nc']
    wada_h = f(np.asarray(w_ada[0]).reshape(16, 128, 12, 512).transpose(2, 1, 0, 3))
    win_h = f(np.asarray(w_in[0])[:, _colidx()].reshape(16, 128, NBLK, 128).transpose(2, 1, 0, 3))
    wout_h = f(np.asarray(w_out[0]).reshape(16, 128, 4, 512).transpose(2, 1, 0, 3))
    col = lambda v: np.asarray(v, np.float32).reshape(-1, 128).T
    pvec = np.zeros((128, PV_N), np.float32)
    pvec[:, PV_PREG:PV_PREG + 16] = col(pre_norm_g[0])
    mu = np.asarray(rwkv_mu[0], np.float32)
    pvec[:, PV_MU] = mu[3072:3200]
    for j in range(8):
        for r_ in range(3):
            pvec[:, PV_MU + 1 + 3 * j + r_] = mu[r_ * 1024 + j * 128:r_ * 1024 + (j + 1) * 128]
    pvec[:, PV_W0:PV_W0 + 8] = col(rwkv_w0[0])
    pvec[:, PV_A0:PV_A0 + 8] = col(rwkv_a0[0])
    pvec[:, PV_KK:PV_KK + 8] = col(rwkv_k_k[0])
    pvec[:, PV_KA:PV_KA + 8] = col(rwkv_k_a[0])
    pvec[:, PV_RK:PV_RK + 8] = col(np.asarray(rwkv_r_k[0]).reshape(-1))
    pvec[:, PV_LNW:PV_LNW + 8] = col(rwkv_ln_w[0])
    pvec[:, PV_LNB:PV_LNB + 8] = col(rwkv_ln_b[0])
    lora_h = f(np.concatenate([np.asarray(rwkv_w_up[0]), np.asarray(rwkv_a_up[0])], axis=0))
    sinks_h = f(np.repeat(np.asarray(attn_sinks[0], np.float32), 64)[None, :])
    cbm, cfm = _host_consts()
    bada_h = f(np.asarray(b_ada[0])[None, :])
    postg_h = f(np.asarray(post_norm_g[0])[None, :])
    in_maps = []
    for b in range(B):
        in_maps.append({
            "x": x[b], "cvec": f(c[b].reshape(16, 128).T), "wada": wada_h, "bada": bada_h, "win": win_h,
            "wout": wout_h, "pvec": pvec, "lora": lora_h, "postg": postg_h, "sinksrep": sinks_h,
            "constb": cbm, "constf": cfm,
        })
    res = run_bass_kernel_spmd(nc, in_maps, core_ids=list(range(B)))
    return np.stack([np.asarray(r["out"], dtype=np.float32) for r in res.results], axis=0)
```

```python
import numpy as np
import concourse.bass as bass
import concourse.mybir as mybir
from concourse.bass_utils import run_bass_kernel_spmd
from contextlib import ExitStack

F32 = mybir.dt.float32
BF16 = mybir.dt.bfloat16
AF = mybir.ActivationFunctionType
ALU = mybir.AluOpType

T = 2048
D = 2048
NT = 16
RMS_EPS = 1e-6
GN_EPS = 64e-5
C0 = float(np.exp(-0.5))
NBLK = 55
DEBUG = False
POOL_COPY = False
DVE_EVAC = True

PV_PREG = 0
PV_MU = 16
PV_W0 = 41
PV_A0 = 49
PV_KK = 57
PV_KA = 65
PV_RK = 73
PV_LNW = 81
PV_LNB = 89
PV_N = 97

CB_IDENT = 0
CB_MASKA = 128
CB_MASKA0 = 384
CB_MST = 640
CB_MS = 704
CB_MIT = 768
CB_I64 = 832
CB_OL0 = 896
CB_OL1 = 1024
CB_N = 1152
CF_BONES = 0
CF_BONES64 = 128
CF_RESET = 256
CF_ONES = 768
CF_N = 896


class Sched:
    EPOCH = 12000
    RECYCLE = False

    def __init__(self, nc, es):
        self.nc = nc
        self.es = es
        self.engs = {'pe': nc.tensor, 'act': nc.scalar, 'dve': nc.vector, 'pool': nc.gpsimd, 'sp': nc.sync}
        self.sems = {e: [] for e in self.engs}
        self.cnt = {e: 0 for e in self.engs}
        self.waited = {}
        self.lastw = {}
        self.readers = {}
        self.dsem = {}
        self.dpool = []
        self.nsem = 0
        self.ninstr = 0

    def _newsem(self, name):
        self.nsem += 1
        return self.es.enter_context(self.nc.semaphore(name))

    def _cursem(self, e):
        if not self.sems[e] or self.cnt[e] >= self.EPOCH:
            self.sems[e].append(self._newsem("s_%s_%d" % (e, len(self.sems[e]))))
            self.cnt[e] = 0
        return len(self.sems[e]) - 1

    def _wait(self, e, tok):
        if tok is None:
            return
        if tok[0] == 'e':
            _, te, ep, c = tok
            if te == e and e == 'pe':
                return
            key = (e, 'e', te, ep)
            if self.waited.get(key, 0) >= c:
                return
            self.waited[key] = c
            self.engs[e].wait_ge(self.sems[te][ep], c)
        else:
            _, k, c = tok
            key = (e, 'd', k)
            if self.waited.get(key, 0) >= c:
                return
            self.waited[key] = c
            self.engs[e].wait_ge(self.dsem[k][0], c)

    def _deps(self, e, reads, writes):
        for r in reads:
            self._wait(e, self.lastw.get(r))
        for w in writes:
            self._wait(e, self.lastw.get(w))
            for t in self.readers.get(w, {}).values():
                self._wait(e, t)

    def _commit(self, tok, reads, writes):
        for w in writes:
            self.lastw[w] = tok
            self.readers[w] = {}
        for r in reads:
            if r in writes:
                continue
            d = self.readers.setdefault(r, {})
            k = (tok[1],) if tok[0] == 'e' else ('d', tok[1])
            d[k] = tok

    @staticmethod
    def _pe_writes(e, writes):
        if e != 'pe':
            return writes
        out = []
        seen = set()
        for w in writes:
            if isinstance(w, tuple) and len(w) == 3 and w[0] == "ps":
                if w[1] not in seen:
                    seen.add(w[1])
                    out += [("ps", w[1], s_) for s_ in range(8)]
            else:
                out.append(w)
        return out

    def op(self, e, fn, reads=(), writes=()):
        writes = self._pe_writes(e, writes)
        self._deps(e, reads, writes)
        ep = self._cursem(e)
        ins = fn(self.engs[e])
        ins.then_inc(self.sems[e][ep], 1)
        self.cnt[e] += 1
        self.ninstr += 1
        tok = ('e', e, ep, self.cnt[e])
        self._commit(tok, reads, writes)
        return tok

    def multi(self, e, fns, reads=(), writes=()):
        writes = self._pe_writes(e, writes)
        self._deps(e, reads, writes)
        ep = self._cursem(e)
        ins = None
        for fn in fns:
            ins = fn(self.engs[e])
            self.ninstr += 1
        ins.then_inc(self.sems[e][ep], 1)
        self.cnt[e] += 1
        tok = ('e', e, ep, self.cnt[e])
        self._commit(tok, reads, writes)
        return tok

    def dma(self, q, out, in_, reads=(), writes=(), key=None, serialize=True):
        if key not in self.dsem:
            if self.dpool:
                self.dsem[key] = self.dpool.pop()
            else:
                self.dsem[key] = [self._newsem("d%d" % self.nsem), 0]
        if serialize and self.dsem[key][1] > 0:
            self._wait(q, ('d', key, self.dsem[key][1]))
        self._deps(q, reads, writes)
        ins = self.engs[q].dma_start(out=out, in_=in_)
        ins.then_inc(self.dsem[key][0], 16)
        self.dsem[key][1] += 16
        self.ninstr += 1
        tok = ('d', key, self.dsem[key][1])
        self._commit(tok, reads, writes)
        return tok

    def barrier(self):
        last = {}
        for e in self.engs:
            if self.sems[e]:
                last[e] = ('e', e, len(self.sems[e]) - 1, self.cnt[e])
        for e in self.engs:
            for te, tok in last.items():
                if te != e and tok[3] > 0:
                    self._wait(e, tok)
            for k, (s, c) in self.dsem.items():
                if c > 0:
                    self._wait(e, ('d', k, c))
        self.lastw = {}
        self.readers = {}
        if self.RECYCLE:
            for k in list(self.dsem.keys()):
                self.dpool.append(self.dsem.pop(k))
                self.waited = {w: v for w, v in self.waited.items() if not (w[1] == 'd' and w[2] == k)}


def build_program():
    nc = bass.Bass("TRN2", target_bir_lowering=False)
    dt = nc.dram_tensor
    x_d = dt("x", [T, D], F32, kind="ExternalInput").ap()
    cvec_d = dt("cvec", [128, 16], F32, kind="ExternalInput").ap()
    wada_d = dt("wada", [12, 128, 16, 512], F32, kind="ExternalInput").ap()
    bada_d = dt("bada", [1, 6144], F32, kind="ExternalInput").ap()
    win_d = dt("win", [NBLK, 128, 16, 128], F32, kind="ExternalInput").ap()
    wout_d = dt("wout", [4, 128, 16, 512], F32, kind="ExternalInput").ap()
    pvec_d = dt("pvec", [128, PV_N], F32, kind="ExternalInput").ap()
    lora_d = dt("lora", [128, 1024], F32, kind="ExternalInput").ap()
    postg_d = dt("postg", [1, 2048], F32, kind="ExternalInput").ap()
    sinks_d = dt("sinksrep", [1, 1024], F32, kind="ExternalInput").ap()
    sinkc_d = dt("sinkcol", [128, 8], F32, kind="ExternalInput").ap()
    cb_d = dt("constb", [128, CB_N], F32, kind="ExternalInput").ap()
    cf_d = dt("constf", [128, CF_N], F32, kind="ExternalInput").ap()
    out_d = dt("out", [T, D], F32, kind="ExternalOutput").ap()
    scrA = dt("scrA", [28, 128, T], BF16, kind="Internal").ap()
    scrR = dt("scrR", [25, 128, T], F32, kind="Internal").ap()
    scrM = dt("scrM", [16, 128, T], BF16, kind="Internal").ap()

    es = ExitStack()
    S = Sched(nc, es)
    sb = lambda name, shape, dtype: es.enter_context(nc.sbuf_tensor(name, shape, dtype))

    cb = sb("cb", [128, CB_N], BF16)
    cf = sb("cf", [128, CF_N], F32)
    pv = sb("pv", [128, PV_N], F32)
    omm = sb("omm", [128, 25], F32)
    omka = sb("omka", [128, 8], F32)
    epsc = sb("epsc", [128, 2], F32)
    lora = sb("lora_sb", [128, 1024], BF16)
    gpb = sb("gpb", [128, 2048], F32)
    gs = sb("gs", [128, 16], F32)
    shiftc = sb("shiftc", [128, 16], F32)
    sinkl = sb("sinkl", [1, 1024], F32)
    sinkc = sb("sinkc", [128, 8], F32)
    Hst = sb("Hst", [128, 8, 64], F32)
    small = sb("small", [128, 64], F32)
    one11 = cf[0:1, CF_ONES:CF_ONES + 1]
    ones_row = cf[0:1, CF_ONES:CF_ONES + 128]

    ARENA_W = 42400
    VP_OFF = 27700
    arena = sb("arena", [128, ARENA_W], F32)
    ps = es.enter_context(nc.psum_tensor("ps", [128, 8, 512], F32))

    def carve(off, shape, dtype):
        n = int(np.prod(shape[1:]))
        if dtype == BF16:
            assert n % 2 == 0
            a = arena[:, off:off + n // 2].bitcast(BF16)
            w = n // 2
        else:
            a = arena[:, off:off + n]
            w = n
        if len(shape) == 3:
            a = a.rearrange("p (a b) -> p a b", a=shape[1])
        elif len(shape) == 4:
            a = a.rearrange("p (a b c) -> p a b c", a=shape[1], b=shape[2])
        return a, off + w

    def PSU(bank, lo, hi):
        return [("ps", bank, s) for s in range(lo // 64, (hi - 1) // 64 + 1)]

    Vp, _vpend = carve(VP_OFF, [128, 17, 4, 192], BF16)
    assert _vpend <= ARENA_W

    S.dma('pool', cb[:], cb_d, writes=["cb"], key="c0")
    S.dma('sp', cf[:], cf_d, writes=["cf"], key="c1")
    S.dma('sp', pv[:], pvec_d, writes=["pv"], key="c2")
    S.dma('pool', lora[:], lora_d, writes=["lora"], key="c3")
    S.dma('sp', sinkl[:], sinks_d, writes=["sinkl"], key="c4")
    S.op('act', lambda e: e.activation(out=sinkl[:], in_=sinkl[:], func=AF.Exp), reads=["sinkl"], writes=["sinkl"])
    S.dma('sp', sinkc[:], sinkc_d, writes=["sinkc"], key="c6")
    S.op('act', lambda e: e.activation(out=sinkc[:], in_=sinkc[:], func=AF.Exp), reads=["sinkc"], writes=["sinkc"])
    S.op('dve', lambda e: e.tensor_scalar(out=omm[:], in0=pv[:, PV_MU:PV_MU + 25], scalar1=-1.0, scalar2=1.0,
                                          op0=ALU.mult, op1=ALU.add), reads=["pv"], writes=["omm"])
    S.op('dve', lambda e: e.tensor_scalar(out=omka[:], in0=pv[:, PV_KA:PV_KA + 8], scalar1=-1.0, scalar2=1.0,
                                          op0=ALU.mult, op1=ALU.add), reads=["pv"], writes=["omka"])
    S.op('pool', lambda e: e.memset(epsc[:, 0:1], RMS_EPS), writes=["epsc"])
    S.op('pool', lambda e: e.memset(epsc[:, 1:2], GN_EPS), writes=["epsc"])
    S.op('pool', lambda e: e.memset(Hst[:].rearrange("p a b -> p (a b)"), 0.0), writes=["H"])

    ident = cb[:, CB_IDENT:CB_IDENT + 128]

    o = 30000
    wa, o = carve(o, [128, 2, 16, 512], BF16)
    csb, o = carve(o, [128, 16], F32)
    scb, o = carve(o, [128, 16], BF16)
    brow, o = carve(o, [128, 2, 512], F32)
    mrow, o = carve(o, [128, 2, 512], F32)
    pgrow, o = carve(o, [128, 2, 512], F32)
    S.dma('sp', csb, cvec_d, writes=["csb"], key="c5")
    S.op('act', lambda e: e.activation(out=scb, in_=csb, func=AF.Silu), reads=["csb"], writes=["scb"])
    assert o <= ARENA_W, o

    def phase0():
      for nb in range(12):
          u = nb % 2
          S.dma('pool', wa[:, u], wada_d[nb], writes=[("wa", u)], key=("wa", u))
          S.dma('pool', brow[0:1, u, :], bada_d[0:1, nb * 512:(nb + 1) * 512], writes=[("brow", u)], key=("br", u))
          fns = []
          for kc in range(16):
              fns.append(lambda e, kc=kc: e.matmul(ps[0:1, u, :], scb[:, kc:kc + 1], wa[:, u, kc, :],
                                                   start=(kc == 0), stop=(kc == 15)))
          S.multi('pe', fns, reads=[("wa", u), "scb"], writes=PSU(u, 0, 512))
          S.op('dve', lambda e: e.tensor_tensor(out=mrow[0:1, u, :], in0=ps[0:1, u, :], in1=brow[0:1, u, :], op=ALU.add),
               reads=PSU(u, 0, 512) + [("brow", u)], writes=[("mrow", u)])
          if nb < 8:
              fns = []
              for i in range(4):
                  col = nb * 4 + i
                  fns.append(lambda e, i=i, col=col: e.matmul(ps[:, 2, col:col + 1], mrow[0:1, u, i * 128:(i + 1) * 128],
                                                              one11, start=True, stop=True))
              S.multi('pe', fns, reads=[("mrow", u), "cf"], writes=PSU(2, 0, 64))
          else:
              gb = nb - 8
              S.dma('pool', pgrow[0:1, u, :], postg_d[0:1, gb * 512:(gb + 1) * 512], writes=[("pgrow", u)], key=("pg", u))
              S.op('dve', lambda e: e.tensor_tensor(out=mrow[0:1, u, :], in0=mrow[0:1, u, :], in1=pgrow[0:1, u, :], op=ALU.mult),
                   reads=[("mrow", u), ("pgrow", u)], writes=[("mrow", u)])
              S.op('pe', lambda e: e.matmul(ps[:, 3, :], ones_row, mrow[0:1, u, :], start=True, stop=True),
                   reads=[("mrow", u), "cf"], writes=PSU(3, 0, 512))
              S.op('act', lambda e: e.activation(out=gpb[:, gb * 512:(gb + 1) * 512], in_=ps[:, 3, :], func=AF.Copy),
                   reads=PSU(3, 0, 512), writes=["gpb"])
          if nb == 7:
              S.op('dve', lambda e: e.tensor_copy(out=shiftc[:], in_=ps[:, 2, 0:16]), reads=PSU(2, 0, 64), writes=["shiftc"])
              S.op('dve', lambda e: e.scalar_tensor_tensor(out=gs[:], in0=ps[:, 2, 16:32], scalar=1.0, in1=pv[:, PV_PREG:PV_PREG + 16],
                                                           op0=ALU.add, op1=ALU.mult), reads=PSU(2, 0, 64) + ["pv"], writes=["gs"])
          yield

    o = 0
    hT, o = carve(o, [128, 16, T], BF16)
    o_after_hT = o
    xt, o = carve(o, [128, 2, D], F32)
    NG = 2
    xn, o = carve(o, [128, 4, NG, D], BF16)
    junk, o = carve(o, [128, D], BF16)
    assert o <= 30000, o

    def n_tile(tt):
        q, t4 = divmod(tt, NG)
        slot = q % 4
        u = tt % 2
        S.dma('sp', xt[:, u, :], x_d[tt * 128:(tt + 1) * 128, :], writes=[("xt", u)], key=("xt", u))
        S.op('act', lambda e: e.activation(out=junk, in_=xt[:, u, :], func=AF.Square, scale=float(D) ** -0.5,
                                           accum_out=small[:, tt:tt + 1]),
             reads=[("xt", u)], writes=["junk", ("ssq", tt)])
        S.op('act', lambda e: e.activation(out=small[:, 16 + tt:17 + tt], in_=small[:, tt:tt + 1], func=AF.Sqrt, bias=epsc[:, 0:1]),
             reads=[("ssq", tt), "epsc"], writes=[("rs", tt)])
        S.op('dve', lambda e: e.reciprocal(out=small[:, 16 + tt:17 + tt], in_=small[:, 16 + tt:17 + tt]),
             reads=[("rs", tt)], writes=[("rs", tt)])
        S.op('dve', lambda e: e.tensor_scalar(out=xn[:, slot, t4, :], in0=xt[:, u, :], scalar1=small[:, 16 + tt:17 + tt], scalar2=None,
                                              op0=ALU.mult), reads=[("xt", u), ("rs", tt)], writes=[("xn", slot, t4)])

    def n_stage1(q):
        for t4 in range(NG):
            n_tile(q * NG + t4)

    def n_stage2(q):
        slot = q % 4
        W_ = NG * 128
        for kc in range(16):
            bank = 4 + (kc // 2) % 4
            half = kc % 2
            pst = ps[:, bank, :].bitcast(BF16)[:, half * 512:half * 512 + W_]
            fns = []
            for t4 in range(NG):
                fns.append(lambda e, t4=t4, pst=pst: e.transpose(pst[:, t4 * 128:(t4 + 1) * 128],
                                                                 xn[:, slot, t4, kc * 128:(kc + 1) * 128], ident))
            S.multi('pe', fns, reads=[("xn", slot, t4) for t4 in range(NG)] + ["cb"], writes=PSU(bank, half * 256, half * 256 + 256))
            dst = hT[:, kc, q * W_:(q + 1) * W_]
            wr = [("hT", q * NG + t4) for t4 in range(NG)]
            if kc % 2 == 0:
                S.op('act', lambda e, pst=pst, dst=dst: e.activation(out=dst, in_=pst, func=AF.Identity,
                                                                    bias=shiftc[:, kc:kc + 1], scale=gs[:, kc:kc + 1]),
                     reads=PSU(bank, half * 256, half * 256 + 256) + ["gs", "shiftc"], writes=wr)
            else:
                S.op('dve', lambda e, pst=pst, dst=dst: e.tensor_scalar(out=dst, in0=pst, scalar1=gs[:, kc:kc + 1],
                                                                       scalar2=shiftc[:, kc:kc + 1], op0=ALU.mult, op1=ALU.add),
                     reads=PSU(bank, half * 256, half * 256 + 256) + ["gs", "shiftc"], writes=wr)

    p0 = phase0()

    def p0_step(n):
        for _ in range(n):
            try:
                next(p0)
            except StopIteration:
                return

    NQ = NT // NG
    for i in range(8):
        p0_step(1)
        if i < 3 * NG:
            n_tile(i)
    for q in range(NQ):
        n_stage2(q)
        if q % 2 == 1:
            p0_step(1)
        if q + 3 < NQ:
            n_stage1(q + 3)
    p0_step(12)
    S.barrier()

    o = o_after_hT
    wbuf, o = carve(o, [128, 3, 16, 128], BF16)
    stgb, o = carve(o, [128, 2, T], BF16)
    stgf, o = carve(o, [128, 2, T], F32)
    amu, o = carve(o, [128, T + 1], F32)
    S.op('pool', lambda e: e.memset(amu[:, 0:1], 0.0), writes=["amu0"])
    assert o <= VP_OFF, o
    S.op('pool', lambda e: e.memset(Vp.rearrange("p a b c -> p (a b c)"), 0.0), writes=[("Vp", i) for i in range(17)] + ["Vp"])
    hT_all = [("hT", tt) for tt in range(NT)]

    def blk_kind(blk):
        if blk < 2: return ('copy', blk, None)
        if blk < 6: return ('v', blk - 4, None)
        if blk < 14: return ('q', 4 + (blk - 6), None)
        if blk < 22: return ('silu', 12 + (blk - 14), None)
        if blk == 22: return ('lerp', 0, 0)
        j, r = divmod(blk - 23, 4)
        if r < 3: return ('lerp', 1 + 3 * j + r, 1 + 3 * j + r)
        return ('silu', 20 + j, None)

    SEQ = [0, 1] + list(range(4, NBLK))
    for p_ in range(2):
        S.dma('pool', wbuf[:, p_ % 3], win_d[SEQ[p_]], writes=[("wbuf", p_ % 3)], key=("wb", p_ % 3))
    def phaseI(blks):
      bankrot = 0
      nb_i = 0
      nf_i = 0
      for p_ in blks:
          blk = SEQ[p_]
          wu = p_ % 3
          if p_ + 2 < len(SEQ):
              yield S.dma('pool', wbuf[:, (p_ + 2) % 3], win_d[SEQ[p_ + 2]], writes=[("wbuf", (p_ + 2) % 3)], key=("wb", (p_ + 2) % 3))
          kind, sidx, mucol = blk_kind(blk)
          if kind == 'v':
              for tt in range(NT):
                  bank = bankrot % 4; bankrot += 1
                  fns = [lambda e, kc=kc, bank=bank, tt=tt: e.matmul(ps[:, bank, 0:128], hT[:, kc, tt * 128:(tt + 1) * 128],
                                                                     wbuf[:, wu, kc, :], start=(kc == 0), stop=(kc == 15))
                         for kc in range(16)]
                  yield S.multi('pe', fns, reads=[("hT", tt), ("wbuf", wu)], writes=PSU(bank, 0, 128))
                  dst = Vp[:, 1 + tt, 2 * sidx:2 * sidx + 2, 64:128]
                  src = ps[:, bank, 0:128].rearrange("p (a b) -> p a b", a=2)
                  eng = 'act' if tt % 2 == 0 else 'dve'
                  if eng == 'act':
                      yield S.op('act', lambda e, dst=dst, src=src: e.activation(out=dst, in_=src, func=AF.Copy),
                           reads=PSU(bank, 0, 128), writes=[("Vp", 1 + tt)])
                  else:
                      yield S.op('dve', lambda e, dst=dst, src=src: e.tensor_copy(out=dst, in_=src),
                           reads=PSU(bank, 0, 128), writes=[("Vp", 1 + tt)])
              continue
          if kind == 'lerp':
              su = nf_i % 2; nf_i += 1
              stg = stgf[:, su, :]
              stgname = ("stgf", su)
          else:
              su = nb_i % 2; nb_i += 1
              stg = stgb[:, su, :]
              stgname = ("stgb", su)
          for wn in range(4):
              bank = bankrot % 4; bankrot += 1
              tok = slice(wn * 512, (wn + 1) * 512)
              for pc4 in range(2):
                  fns = [lambda e, kc=kc, bank=bank, tok=tok: e.matmul(ps[:, bank, :], wbuf[:, wu, kc, :], hT[:, kc, tok],
                                                                       start=(kc == 0), stop=(kc == 15)) for kc in range(pc4 * 8, pc4 * 8 + 8)]
                  yield S.multi('pe', fns, reads=hT_all[wn * 4:(wn + 1) * 4] + [("wbuf", wu)], writes=PSU(bank, 0, 512))
              pin = ps[:, bank, :]
              if kind == 'copy':
                  yield S.op('act', lambda e, pin=pin, tok=tok: e.activation(out=stg[:, tok], in_=pin, func=AF.Copy),
                       reads=PSU(bank, 0, 512), writes=[stgname])
              elif kind == 'q':
                  yield S.op('dve', lambda e, pin=pin, tok=tok: e.tensor_scalar(out=stg[:, tok], in0=pin, scalar1=0.125, scalar2=None,
                                                                         op0=ALU.mult), reads=PSU(bank, 0, 512), writes=[stgname])
              elif kind == 'silu':
                  yield S.op('act', lambda e, pin=pin, tok=tok: e.activation(out=stg[:, tok], in_=pin, func=AF.Silu),
                       reads=PSU(bank, 0, 512), writes=[stgname])
              else:
                  mc = pv[:, PV_MU + mucol:PV_MU + mucol + 1]
                  oc = omm[:, mucol:mucol + 1]
                  yield S.op('act', lambda e, pin=pin, wn=wn, mc=mc: e.activation(out=amu[:, 1 + wn * 512:1 + (wn + 1) * 512], in_=pin,
                                                                            func=AF.Copy, scale=mc),
                       reads=PSU(bank, 0, 512) + ["pv"], writes=[("amu", wn)])
                  rd = [("amu", wn)] + ([("amu", wn - 1)] if wn > 0 else ["amu0"])
                  yield S.op('dve', lambda e, pin=pin, wn=wn, oc=oc, tok=tok: e.scalar_tensor_tensor(
                      out=stg[:, tok], in0=pin, scalar=oc, in1=amu[:, wn * 512:(wn + 1) * 512], op0=ALU.mult, op1=ALU.add),
                      reads=PSU(bank, 0, 512) + rd + ["omm"], writes=[stgname])
          if kind == 'lerp':
              yield S.dma('sp', scrR[sidx], stg, reads=[stgname], writes=[("scrR", sidx)], key=("spf", su))
          else:
              yield S.dma('sp', scrA[sidx], stg, reads=[stgname], writes=[("scrA", sidx)], key=("spb", su))

    o = _vpend
    qT, o = carve(o, [128, 1, 2, T], BF16)
    kTg, o = carve(o, [128, 1, T + 128], BF16)
    gaT, o = carve(o, [128, 1, 2, T], BF16)
    PT, o = carve(o, [128, 2, 2, 512], BF16)
    rden, o = carve(o, [128, 2, 256], F32)
    ynum, o = carve(o, [128, 2, 256], F32)
    mixst, o = carve(o, [128, 2, 2, 128], BF16)
    assert o <= ARENA_W, o
    S.op('pool', lambda e: e.memset(kTg[:, :, 0:128], 0.0), writes=[("kTg", 0)])
    maskA = cb[:, CB_MASKA:CB_MASKA + 256]
    maskA0 = cb[:, CB_MASKA0:CB_MASKA0 + 256]
    OL = [cb[:, CB_OL0:CB_OL0 + 128], cb[:, CB_OL1:CB_OL1 + 128]]
    def a_load(g):
        gu = 0
        for pi in range(2):
            yield S.dma('sp', qT[:, gu, pi, :], scrA[4 + 2 * g + pi], reads=[("scrA", 4 + 2 * g + pi)], writes=[("qT", gu)],
                  key=("aq", gu), serialize=(pi == 0))
        ksrc = scrA[g // 2][64 * (g % 2):64 * (g % 2) + 64, :]
        yield S.dma('sp', kTg[0:64, gu, 128:], ksrc, reads=[("scrA", g // 2)], writes=[("kTg", gu)], key=("ak", gu))
        yield S.dma('sp', kTg[64:128, gu, 128:], ksrc, reads=[("scrA", g // 2)], writes=[("kTg", gu)], key=("ak", gu), serialize=False)

    def a_stage1(unit):
        g, tt = divmod(unit, NT)
        gu = 0
        u = unit % 2
        if tt == 0:
            yield from a_load(g)
        for e_ in range(2):
            bank = 4 + e_
            rows = slice(64 * e_, 64 * e_ + 64)
            fns = []
            scv = ps[:, bank, :].rearrange("p (a b c) -> p a b c", a=2, b=2)
            for pc in range(2):
                fns.append(lambda e, rows=rows, pc=pc, scv=scv: e.matmul(
                    scv[:, :, pc, :], kTg[rows, gu, (tt + pc) * 128:(tt + pc + 1) * 128],
                    qT[rows, gu, :, tt * 128:(tt + 1) * 128], start=True, stop=True))
            yield S.multi('pe', fns, reads=[("kTg", gu), ("qT", gu)], writes=PSU(bank, 0, 512))
            yield S.op('act', lambda e, bank=bank, e_=e_: e.activation(out=PT[:, u, e_, :], in_=ps[:, bank, :], func=AF.Exp),
                 reads=PSU(bank, 0, 512), writes=[("PT", u, e_)])
        mk = maskA0 if tt == 0 else maskA
        ptv = PT[:, u].rearrange("p e (a b) -> p (e a) b", a=2)
        yield S.op('dve', lambda e, ptv=ptv, mk=mk: e.tensor_tensor(out=ptv, in0=ptv, in1=mk.unsqueeze(1).broadcast_to([128, 4, 256]),
                                                              op=ALU.mult),
             reads=[("PT", u, 0), ("PT", u, 1), "cb"], writes=[("PT", u, 0), ("PT", u, 1)])

    def a_stage2(unit):
        g, tt = divmod(unit, NT)
        gu = 0
        u = unit % 2
        nbank = 6 + u
        if tt == 0:
            for pi in range(2):
                yield S.dma('sp', gaT[:, gu, pi, :], scrA[12 + 2 * g + pi], reads=[("scrA", 12 + 2 * g + pi)], writes=[("gaT", gu)],
                            key=("ag", gu), serialize=(pi == 0))
        fns = []
        seq = [(e_, pc) for e_ in range(2) for pc in range(2)]
        numv = ps[:, nbank, 0:256].rearrange("p (a b) -> p a b", a=2)
        denv = ps[:, nbank, 256:512].rearrange("p (a b) -> p a b", a=2)
        ptq = lambda e_, pc: PT[:, u, e_, :].rearrange("p (a b c) -> p a b c", a=2, b=2)[:, :, pc, :]
        for i, (e_, pc) in enumerate(seq):
            vl = Vp[:, tt + pc, g, 64:192] if e_ == 0 else Vp[:, tt + pc, g, 0:128]
            fns.append(lambda e, e_=e_, pc=pc, vl=vl, i=i: e.matmul(numv, vl, ptq(e_, pc), start=(i == 0), stop=(i == 3)))
        for i, (e_, pc) in enumerate(seq):
            fns.append(lambda e, e_=e_, pc=pc, i=i: e.matmul(denv, OL[e_], ptq(e_, pc), start=(i == 0), stop=(i == 3)))
        yield S.multi('pe', fns, reads=[("PT", u, 0), ("PT", u, 1), ("Vp", tt), ("Vp", tt + 1), "Vp", "cb", "cf", "sinkl"],
                writes=PSU(nbank, 0, 512))
        for pi in range(2):
            pr = 2 * g + pi
            yield S.op('act', lambda e, pi=pi, pr=pr: e.activation(out=rden[:, u, pi * 128:(pi + 1) * 128],
                                                                 in_=ps[:, nbank, 256 + pi * 128:256 + (pi + 1) * 128],
                                                                 func=AF.Ln, bias=sinkc[:, pr:pr + 1]),
                       reads=PSU(nbank, 256, 512) + ["sinkc"], writes=[("rden", u)])
        yield S.op('act', lambda e: e.activation(out=rden[:, u, :], in_=rden[:, u, :], func=AF.Exp, scale=-1.0), reads=[("rden", u)],
             writes=[("rden", u)])
        yield S.op('dve', lambda e: e.tensor_tensor(out=ynum[:, u, :], in0=ps[:, nbank, 0:256], in1=rden[:, u, :], op=ALU.mult),
             reads=PSU(nbank, 0, 256) + [("rden", u)], writes=[("ynum", u)])
        yield S.op('pool', lambda e: e.tensor_tensor(out=mixst[:, u, :, :],
                                               in0=ynum[:, u, :].rearrange("p (a b) -> p a b", a=2),
                                               in1=gaT[:, gu, :, tt * 128:(tt + 1) * 128], op=ALU.mult),
             reads=[("ynum", u), ("gaT", gu)], writes=[("mixst", u)])
        yield S.dma('sp', scrM[2 * g:2 * g + 2, :, tt * 128:(tt + 1) * 128].rearrange("c p t -> p c t"), mixst[:, u, :, :],
              reads=[("mixst", u)], writes=[("scrM", g, tt)], key=("am", u))

    NU = 4 * NT

    def phaseA():
        yield from a_stage1(0)
        for unit in range(NU):
            if unit + 1 < NU:
                yield from a_stage1(unit + 1)
            yield from a_stage2(unit)

    for _ in phaseI(range(0, 20)):
        pass
    gens = [phaseI(range(20, len(SEQ))), phaseA()]
    while gens:
        for gen in list(gens):
            try:
                next(gen)
            except StopIteration:
                gens.remove(gen)
    S.barrier()

    WR = 256
    NWR = T // WR
    CW = WR // 64
    o = 0
    Hba, o = carve(o, [128, 8, 64], BF16)
    Hbb, o = carve(o, [128, 8, 64], BF16)
    Hb = [Hba, Hbb]
    S.op('pool', lambda e: e.memset(Hba.rearrange("p a b -> p (a b)"), 0.0), writes=[("Hb", 0, 0), ("Hb", 0, 1)])
    S.op('pool', lambda e: e.memset(Hbb.rearrange("p a b -> p (a b)"), 0.0), writes=[("Hb", 1, 0), ("Hb", 1, 1)])
    bones = cf[:, CF_BONES:CF_BONES + 128]
    bones64 = cf[:, CF_BONES64:CF_BONES64 + 128]
    resetm = cf[:, CF_RESET:CF_RESET + WR]
    mST = cb[:, CB_MST:CB_MST + 64]
    mS = cb[:, CB_MS:CB_MS + 64]
    mIT = cb[:, CB_MIT:CB_MIT + 64]
    I64 = cb[:, CB_I64:CB_I64 + 64]
    bc4 = lambda m: m.unsqueeze(1).broadcast_to([128, 4, 64])
    v4 = lambda a: a.rearrange("p (a b) -> p a b", a=4)

    class LB:
        pass
    lanes = []
    for g in range(2):
        L = LB()
        L.tw, o = carve(o, [128, WR], BF16)
        L.wdl, o = carve(o, [128, WR], F32)
        L.rkv, o = carve(o, [128, 2, 3, WR], F32)
        L.tmp = []
        for i in range(12):
            t_, o = carve(o, [128, WR], F32)
            L.tmp.append(t_)
        L.tb = []
        for i in range(3):
            t_, o = carve(o, [128, WR], BF16)
            L.tb.append(t_)
        L.grs, L.RT, L.AT, L.BT, L.KT, L.VT, L.BCt, L.KCt, L.BON, L.Wend = [], [], [], [], [], [], [], [], [], []
        for wp in range(2):
            for lst, shp, dt_ in ((L.grs, [128, 4, WR], BF16), (L.RT, [128, 4, WR], BF16), (L.AT, [128, 4, WR], BF16),
                                  (L.BT, [128, 4, WR], BF16), (L.KT, [128, 4, WR], BF16), (L.VT, [128, CW, 4, 64], BF16),
                                  (L.BCt, [128, CW, 4, 64], BF16), (L.KCt, [128, CW, 4, 64], BF16),
                                  (L.BON, [128, 4, WR], F32), (L.Wend, [128, 4, CW], F32)):
                t_, o = carve(o, shp, dt_)
                lst.append(t_)
        L.Yw, o = carve(o, [128, 4, WR], F32)
        L.mixr, o = carve(o, [128, 4, WR], BF16)
        L.ptmp = []
        for i in range(3):
            t_, o = carve(o, [128, WR], F32)
            L.ptmp.append(t_)
        L.NNs = []
        for i in range(2):
            t_, o = carve(o, [128, 2, 256], BF16)
            L.NNs.append(t_)
        L.PTm = []
        for i in range(2):
            t_, o = carve(o, [128, 256], BF16)
            L.PTm.append(t_)
        L.G2 = []
        for i in range(2):
            t_, o = carve(o, [128, 2, 256], BF16)
            L.G2.append(t_)
        L.G3 = []
        for i in range(2):
            t_, o = carve(o, [128, 256], BF16)
            L.G3.append(t_)
        L.Xb, o = carve(o, [128, 256], BF16)
        L.Ub, o = carve(o, [128, 256], BF16)
        lanes.append(L)
    assert o <= ARENA_W, o
    LO, HI = slice(0, 256), slice(256, 512)

    def prep_stream(grp):
        L = lanes[grp]
        N_ = lambda n, *a: (n, grp) + a
        P0, P1 = 4 * grp, 4 * grp + 1
        tw, wdl, rkv = L.tw, L.wdl, L.rkv
        sw, a_, cs, W_, Wi, Wp, WC, kkr, sq, kkn, kmod, ba = L.tmp
        vlb, bCb, kCb = L.tb
        for wn in range(NWR):
            wp = wn % 2
            while chain_done[grp] < wn - 1:
                yield None
            grs, RT, AT, BT, KT, VT, BCt, KCt, BON, Wend = (L.grs[wp], L.RT[wp], L.AT[wp], L.BT[wp], L.KT[wp], L.VT[wp],
                                                            L.BCt[wp], L.KCt[wp], L.BON[wp], L.Wend[wp])
            wtok = slice(wn * WR, (wn + 1) * WR)
            yield S.dma('sp', wdl, scrR[0][:, wtok], reads=[("scrR", 0)], writes=[N_("wdl")], key=N_("rw"))
            yield S.op('act', lambda e: e.activation(out=tw[0:64, :], in_=wdl[0:64, :], func=AF.Tanh), reads=[N_("wdl")], writes=[N_("tw")])
            yield S.op('dve', lambda e: e.tensor_copy(out=tw[64:128, :], in_=wdl[64:128, :]), reads=[N_("wdl")], writes=[N_("tw1")])
            for jj in range(4):
                j = grp * 4 + jj
                ru = jj % 2
                rl, kl, vl_ = rkv[:, ru, 0, :], rkv[:, ru, 1, :], rkv[:, ru, 2, :]
                RKV = N_("rkv", ru)
                for r_ in range(3):
                    yield S.dma('sp', rkv[:, ru, r_, :], scrR[1 + 3 * j + r_][:, wtok], reads=[("scrR", 1 + 3 * j + r_)],
                                writes=[RKV], key=N_("rr", ru), serialize=(r_ == 0))
                yield S.dma('sp', grs[:, jj, :], scrA[20 + j][:, wtok], reads=[("scrA", 20 + j)], writes=[N_("grs", wp, jj)], key=N_("rg", jj))
                col = lambda base: pv[:, base + j:base + j + 1]
                yield S.op('pe', lambda e: e.matmul(ps[:, P0, LO], lora[0:64, j * 128:(j + 1) * 128], tw[0:64, :], start=True, stop=True),
                           reads=["lora", N_("tw")], writes=PSU(P0, 0, 256))
                yield S.op('pe', lambda e: e.matmul(ps[:, P1, LO], lora[64:128, j * 128:(j + 1) * 128], tw[64:128, :], start=True, stop=True),
                           reads=["lora", N_("tw1")], writes=PSU(P1, 0, 256))
                yield S.op('act', lambda e: e.activation(out=sw, in_=ps[:, P0, LO], func=AF.Sigmoid, bias=col(PV_W0)),
                           reads=PSU(P0, 0, 256) + ["pv"], writes=[N_("sw")])
                yield S.op('act', lambda e: e.activation(out=a_, in_=ps[:, P1, LO], func=AF.Sigmoid, bias=col(PV_A0)),
                           reads=PSU(P1, 0, 256) + ["pv"], writes=[N_("a_")])
                yield S.op('dve', lambda e: e.tensor_tensor_scan(out=cs, data0=resetm, data1=sw, initial=0.0, op0=ALU.mult, op1=ALU.add),
                           reads=[N_("sw"), "cf"], writes=[N_("cs")])
                yield S.op('act', lambda e: e.activation(out=sq, in_=kl, func=AF.Square, scale=col(PV_KK)),
                           reads=[RKV, "pv"], writes=[N_("sq")])
                yield S.op('pe', lambda e: e.matmul(ps[:, P0, HI], bones, sq, start=True, stop=True), reads=[N_("sq"), "cf"], writes=PSU(P0, 256, 512))
                yield S.op('act', lambda e: e.activation(out=W_, in_=cs, func=AF.Exp, scale=-C0), reads=[N_("cs")], writes=[N_("W_")])
                yield S.op('act', lambda e: e.activation(out=Wi, in_=cs, func=AF.Exp, scale=C0), reads=[N_("cs")], writes=[N_("Wi")])
                yield S.op('dve', lambda e: e.tensor_tensor(out=Wp, in0=cs, in1=sw, op=ALU.subtract), reads=[N_("cs"), N_("sw")], writes=[N_("Wp")])
                yield S.op('act', lambda e: e.activation(out=Wp, in_=Wp, func=AF.Exp, scale=-C0), reads=[N_("Wp")], writes=[N_("Wp")])
                cs3 = cs.rearrange("p (a b) -> p a b", a=CW)
                yield S.op('dve', lambda e: e.tensor_tensor(out=WC.rearrange("p (a b) -> p a b", a=CW),
                                                            in0=cs3[:, :, 63:64].broadcast_to([128, CW, 64]), in1=cs3, op=ALU.subtract),
                           reads=[N_("cs")], writes=[N_("WC")])
                yield S.op('act', lambda e: e.activation(out=WC, in_=WC, func=AF.Exp, scale=-C0), reads=[N_("WC")], writes=[N_("WC")])
                yield S.op('act', lambda e: e.activation(out=Wend[:, jj, :], in_=cs3[:, :, 63], func=AF.Exp, scale=-C0),
                           reads=[N_("cs")], writes=[N_("Wend", wp, jj)])
                yield S.op('dve', lambda e: e.tensor_scalar(out=sq, in0=ps[:, P0, HI], scalar1=1e-24, scalar2=None, op0=ALU.max),
                           reads=PSU(P0, 256, 512), writes=[N_("sq")])
                yield S.op('act', lambda e: e.activation(out=sq, in_=sq, func=AF.Ln), reads=[N_("sq")], writes=[N_("sq")])
                yield S.op('act', lambda e: e.activation(out=sq, in_=sq, func=AF.Exp, scale=-0.5), reads=[N_("sq")], writes=[N_("sq")])
                yield S.op('dve', lambda e: e.scalar_tensor_tensor(out=kkn, in0=kl, scalar=col(PV_KK), in1=sq, op0=ALU.mult, op1=ALU.mult),
                           reads=[RKV, "pv", N_("sq")], writes=[N_("kkn")])
                yield S.op('act', lambda e: e.activation(out=kmod, in_=a_, func=AF.Identity, scale=col(PV_KA), bias=omka[:, j:j + 1]),
                           reads=[N_("a_"), "pv", "omka"], writes=[N_("kmod")])
                yield S.op('pool', lambda e: e.tensor_tensor(out=kmod, in0=kmod, in1=kl, op=ALU.mult),
                           reads=[N_("kmod"), RKV], writes=[N_("kmod")])
                yield S.op('pool', lambda e: e.tensor_tensor(out=RT[:, jj, :], in0=rl, in1=W_, op=ALU.mult),
                           reads=[RKV, N_("W_")], writes=[N_("RT", wp, jj)])
                yield S.op('dve', lambda e: e.scalar_tensor_tensor(out=AT[:, jj, :], in0=kkn, scalar=-1.0, in1=Wp, op0=ALU.mult, op1=ALU.mult),
                           reads=[N_("kkn"), N_("Wp")], writes=[N_("AT", wp, jj)])
                yield S.op('pool', lambda e: e.tensor_tensor(out=ba, in0=kkn, in1=a_, op=ALU.mult), reads=[N_("kkn"), N_("a_")], writes=[N_("ba")])
                yield S.op('pool', lambda e: e.tensor_tensor(out=BT[:, jj, :], in0=ba, in1=Wi, op=ALU.mult), reads=[N_("ba"), N_("Wi")], writes=[N_("BT", wp, jj)])
                yield S.op('pool', lambda e: e.tensor_tensor(out=bCb, in0=ba, in1=WC, op=ALU.mult), reads=[N_("ba"), N_("WC")], writes=[N_("bCb")])
                yield S.op('pool', lambda e: e.tensor_tensor(out=KT[:, jj, :], in0=kmod, in1=Wi, op=ALU.mult), reads=[N_("kmod"), N_("Wi")], writes=[N_("KT", wp, jj)])
                yield S.op('pool', lambda e: e.tensor_tensor(out=kCb, in0=kmod, in1=WC, op=ALU.mult), reads=[N_("kmod"), N_("WC")], writes=[N_("kCb")])
                yield S.op('dve', lambda e: e.scalar_tensor_tensor(out=kkr, in0=rl, scalar=col(PV_RK), in1=kmod, op0=ALU.mult, op1=ALU.mult),
                           reads=[RKV, "pv", N_("kmod")], writes=[N_("kkr")])
                yield S.op('pe', lambda e: e.matmul(ps[:, P1, HI], bones, kkr, start=True, stop=True), reads=[N_("kkr"), "cf"], writes=PSU(P1, 256, 512))
                yield S.op('dve', lambda e: e.tensor_tensor(out=BON[:, jj, :], in0=ps[:, P1, HI], in1=vl_, op=ALU.mult),
                           reads=PSU(P1, 256, 512) + [RKV], writes=[N_("BON", wp, jj)])
                if POOL_COPY:
                    yield S.op('pool', lambda e: e.tensor_copy(out=vlb, in_=vl_), reads=[RKV], writes=[N_("vlb")])
                else:
                    yield S.op('act', lambda e: e.activation(out=vlb, in_=vl_, func=AF.Copy), reads=[RKV], writes=[N_("vlb")])
                for si, (src, sname, dst, dname, bank, hh) in enumerate([(vlb, "vlb", VT, "VT", P0, 0), (bCb, "bCb", BCt, "BCt", P1, 0),
                                                                          (kCb, "kCb", KCt, "KCt", P0, 256)]):
                    fns = []
                    for c in range(CW):
                        for e_ in range(2):
                            rows = slice(64 * e_, 64 * e_ + 64)
                            fns.append(lambda e, src=src, bank=bank, rows=rows, c=c, hh=hh: e.matmul(
                                ps[rows, bank, hh + c * 64:hh + (c + 1) * 64], src[rows, c * 64:(c + 1) * 64], I64[rows, :], start=True, stop=True))
                    yield S.multi('pe', fns, reads=[N_(sname), "cb"], writes=PSU(bank, hh, hh + 256))
                    pv3 = ps[:, bank, hh:hh + 256].rearrange("p (a b) -> p a b", a=CW)
                    if si == 1:
                        yield S.op('dve', lambda e, dst=dst, pv3=pv3: e.tensor_copy(out=dst[:, :, jj, :], in_=pv3),
                                   reads=PSU(bank, hh, hh + 256), writes=[N_(dname, wp, jj)])
                    else:
                        yield S.op('act', lambda e, dst=dst, pv3=pv3: e.activation(out=dst[:, :, jj, :], in_=pv3, func=AF.Copy),
                                   reads=PSU(bank, hh, hh + 256), writes=[N_(dname, wp, jj)])
            prep_done[grp] = wn + 1

    def chain_stream(grp):
        L = lanes[grp]
        N_ = lambda n, *a: (n, grp) + a
        C0, C1 = 4 * grp + 2, 4 * grp + 3
        Yw, mixr = L.Yw, L.mixr
        NNs, PTm, G2, G3, Xb, Ub = L.NNs, L.PTm, L.G2, L.G3, L.Xb, L.Ub
        hbpar = 0
        for wn in range(NWR):
            wp = wn % 2
            grs, RT, AT, BT, KT, VT, BCt, KCt, BON, Wend = (L.grs[wp], L.RT[wp], L.AT[wp], L.BT[wp], L.KT[wp], L.VT[wp],
                                                            L.BCt[wp], L.KCt[wp], L.BON[wp], L.Wend[wp])
            allj = lambda n: [N_(n, wp, jj) for jj in range(4)]
            wtok = slice(wn * WR, (wn + 1) * WR)
            while prep_done[grp] < wn + 1:
                yield None
            for c in range(CW):
                cu = c % 2
                tokc = slice(c * 64, (c + 1) * 64)
                fa, fb, fc = [], [], []
                for jj in range(4):
                    for e_ in range(2):
                        rows = slice(64 * e_, 64 * e_ + 64)
                        cs_ = slice(jj * 64, (jj + 1) * 64)
                        cs2 = slice(256 + jj * 64, 256 + (jj + 1) * 64)
                        fa.append(lambda e, rows=rows, cs_=cs_, jj=jj: e.matmul(ps[rows, C0, cs_], BT[rows, jj, tokc], AT[rows, jj, tokc], start=True, stop=True))
                        fa.append(lambda e, rows=rows, cs2=cs2, jj=jj: e.matmul(ps[rows, C0, cs2], AT[rows, jj, tokc], BT[rows, jj, tokc], start=True, stop=True))
                        fb.append(lambda e, rows=rows, cs_=cs_, jj=jj: e.matmul(ps[rows, C1, cs_], KT[rows, jj, tokc], AT[rows, jj, tokc], start=True, stop=True))
                        fb.append(lambda e, rows=rows, cs2=cs2, jj=jj: e.matmul(ps[rows, C1, cs2], BT[rows, jj, tokc], RT[rows, jj, tokc], start=True, stop=True))
                        fc.append(lambda e, rows=rows, cs_=cs_, jj=jj: e.matmul(ps[rows, C0, cs_], KT[rows, jj, tokc], RT[rows, jj, tokc], start=True, stop=True))
                yield S.multi('pe', fa, reads=allj("BT") + allj("AT"), writes=PSU(C0, 0, 512))
                yield S.multi('pe', fb, reads=allj("BT") + allj("AT") + allj("KT") + allj("RT"), writes=PSU(C1, 0, 512))
                N0 = NNs[0]
                yield S.op('dve', lambda e: e.tensor_tensor(out=v4(N0[:, 0, :]), in0=v4(ps[:, C0, LO]), in1=bc4(mST), op=ALU.mult),
                           reads=PSU(C0, 0, 256) + ["cb"], writes=[N_("NN", 0)])
                yield S.op('dve', lambda e: e.tensor_tensor(out=v4(N0[:, 1, :]), in0=v4(ps[:, C0, HI]), in1=bc4(mS), op=ALU.mult),
                           reads=PSU(C0, 256, 512) + ["cb"], writes=[N_("NN", 0)])
                yield S.multi('pe', fc, reads=allj("KT") + allj("RT"), writes=PSU(C0, 0, 256))
                yield S.op('pool', lambda e: e.tensor_tensor(out=v4(PTm[0]), in0=v4(N0[:, 0, :]), in1=bc4(I64), op=ALU.add),
                           reads=[N_("NN", 0), "cb"], writes=[N_("PTm", 0)])
                yield S.op('dve', lambda e: e.tensor_tensor(out=v4(G2[cu][:, 0, :]), in0=v4(ps[:, C1, LO]), in1=bc4(mST), op=ALU.mult),
                           reads=PSU(C1, 0, 256) + ["cb"], writes=[N_("G2", cu)])
                yield S.op('dve', lambda e: e.tensor_tensor(out=v4(G2[cu][:, 1, :]), in0=v4(ps[:, C1, HI]), in1=bc4(mIT), op=ALU.mult),
                           reads=PSU(C1, 256, 512) + ["cb"], writes=[N_("G2", cu)])
                yield S.op('dve', lambda e: e.tensor_tensor(out=v4(G3[cu]), in0=v4(ps[:, C0, LO]), in1=bc4(mIT), op=ALU.mult),
                           reads=PSU(C0, 0, 256) + ["cb"], writes=[N_("G3", cu)])
                for l in range(5):
                    a0_, a1_ = NNs[l % 2], NNs[(l + 1) % 2]
                    fns = []
                    for jj in range(4):
                        for e_ in range(2):
                            rows = slice(64 * e_, 64 * e_ + 64)
                            cs_ = slice(jj * 64, (jj + 1) * 64)
                            cs2 = slice(256 + jj * 64, 256 + (jj + 1) * 64)
                            fns.append(lambda e, rows=rows, cs_=cs_, cs2=cs2, a0_=a0_: e.matmul(
                                ps[rows, C1, cs2], a0_[rows, 0, cs_], a0_[rows, 1, cs_], start=True, stop=True))
                            fns.append(lambda e, rows=rows, cs_=cs_, a0_=a0_: e.matmul(
                                ps[rows, C1, cs_], a0_[rows, 1, cs_], a0_[rows, 0, cs_], start=True, stop=True))
                    yield S.multi('pe', fns, reads=[N_("NN", l % 2)], writes=PSU(C1, 0, 512))
                    if l % 2 == 0 or not DVE_EVAC:
                        yield S.op('act', lambda e, a1_=a1_: e.activation(out=a1_.rearrange("p a b -> p (a b)"), in_=ps[:, C1, :], func=AF.Copy),
                                   reads=PSU(C1, 0, 512), writes=[N_("NN", (l + 1) % 2)])
                    else:
                        yield S.op('dve', lambda e, a1_=a1_: e.tensor_copy(out=a1_.rearrange("p a b -> p (a b)"), in_=ps[:, C1, :]),
                                   reads=PSU(C1, 0, 512), writes=[N_("NN", (l + 1) % 2)])
                    fns = []
                    for jj in range(4):
                        for e_ in range(2):
                            rows = slice(64 * e_, 64 * e_ + 64)
                            cs_ = slice(jj * 64, (jj + 1) * 64)
                            cs2 = slice(256 + jj * 64, 256 + (jj + 1) * 64)
                            fns.append(lambda e, rows=rows, cs_=cs_, cs2=cs2, a1_=a1_, l=l: e.matmul(
                                ps[rows, C0, cs2], a1_[rows, 1, cs_], PTm[l % 2][rows, cs_], start=True, stop=True))
                    yield S.multi('pe', fns, reads=[N_("NN", (l + 1) % 2), N_("PTm", l % 2)], writes=PSU(C0, 256, 512))
                    yield S.op('dve', lambda e, l=l: e.tensor_tensor(out=PTm[(l + 1) % 2], in0=ps[:, C0, HI], in1=PTm[l % 2], op=ALU.add),
                               reads=PSU(C0, 256, 512) + [N_("PTm", l % 2)], writes=[N_("PTm", (l + 1) % 2)])
                TT = PTm[1]
                hb = Hb[hbpar]
                hbn = Hb[1 - hbpar]
                fns = []
                for jj in range(4):
                    j = grp * 4 + jj
                    for e_ in range(2):
                        rows = slice(64 * e_, 64 * e_ + 64)
                        cs_ = slice(jj * 64, (jj + 1) * 64)
                        fns.append(lambda e, rows=rows, cs_=cs_, jj=jj, j=j: e.matmul(ps[rows, C1, cs_], AT[rows, jj, tokc], hb[rows, j, :], start=True, stop=False))
                        fns.append(lambda e, rows=rows, cs_=cs_, jj=jj: e.matmul(ps[rows, C1, cs_], G2[cu][rows, 0, cs_], VT[rows, c, jj, :], start=False, stop=True))
                yield S.multi('pe', fns, reads=allj("AT") + [("Hb", hbpar, grp), N_("G2", cu)] + allj("VT"), writes=PSU(C1, 0, 256))
                yield S.op('act', lambda e: e.activation(out=Xb, in_=ps[:, C1, LO], func=AF.Copy), reads=PSU(C1, 0, 256), writes=[N_("Xb")])
                fns = []
                for jj in range(4):
                    for e_ in range(2):
                        rows = slice(64 * e_, 64 * e_ + 64)
                        cs_ = slice(jj * 64, (jj + 1) * 64)
                        cs2 = slice(256 + jj * 64, 256 + (jj + 1) * 64)
                        fns.append(lambda e, rows=rows, cs_=cs_, cs2=cs2: e.matmul(ps[rows, C1, cs2], TT[rows, cs_], Xb[rows, cs_], start=True, stop=True))
                yield S.multi('pe', fns, reads=[N_("PTm", 1), N_("Xb")], writes=PSU(C1, 256, 512))
                yield S.op('act', lambda e: e.activation(out=Ub, in_=ps[:, C1, HI], func=AF.Copy), reads=PSU(C1, 256, 512), writes=[N_("Ub")])
                fns = []
                for jj in range(4):
                    j = grp * 4 + jj
                    for e_ in range(2):
                        rows = slice(64 * e_, 64 * e_ + 64)
                        cs_ = slice(jj * 64, (jj + 1) * 64)
                        cs2 = slice(256 + jj * 64, 256 + (jj + 1) * 64)
                        fns.append(lambda e, rows=rows, cs2=cs2, jj=jj, j=j: e.matmul(ps[rows, C0, cs2], hb[rows, j, :], RT[rows, jj, tokc], start=True, stop=False))
                        fns.append(lambda e, rows=rows, cs_=cs_, cs2=cs2: e.matmul(ps[rows, C0, cs2], Ub[rows, cs_], G2[cu][rows, 1, cs_], start=False, stop=False))
                        fns.append(lambda e, rows=rows, cs_=cs_, cs2=cs2, jj=jj: e.matmul(ps[rows, C0, cs2], VT[rows, c, jj, :], G3[cu][rows, cs_], start=False, stop=True))
                        fns.append(lambda e, rows=rows, cs_=cs_, jj=jj: e.matmul(ps[rows, C0, cs_], BCt[rows, c, jj, :], Ub[rows, cs_], start=True, stop=False))
                        fns.append(lambda e, rows=rows, cs_=cs_, jj=jj: e.matmul(ps[rows, C0, cs_], KCt[rows, c, jj, :], VT[rows, c, jj, :], start=False, stop=True))
                yield S.multi('pe', fns, reads=[("Hb", hbpar, grp), N_("Ub"), N_("G2", cu), N_("G3", cu)] + allj("RT") + allj("VT") + allj("BCt") + allj("KCt"),
                              writes=PSU(C0, 0, 512))
                H4 = Hst[:, grp * 4:(grp + 1) * 4, :]
                yield S.op('dve', lambda e: e.tensor_tensor(out=H4, in0=H4, in1=Wend[:, :, c:c + 1].broadcast_to([128, 4, 64]), op=ALU.mult),
                           reads=[N_("H")] + allj("Wend"), writes=[N_("H")])
                yield S.op('dve', lambda e: e.tensor_tensor(out=H4, in0=v4(ps[:, C0, LO]), in1=H4, op=ALU.add),
                           reads=PSU(C0, 0, 256) + [N_("H")], writes=[N_("H")])
                if POOL_COPY:
                    yield S.op('pool', lambda e: e.tensor_copy(out=hbn[:, grp * 4:(grp + 1) * 4, :], in_=H4),
                               reads=[N_("H")], writes=[("Hb", 1 - hbpar, grp)])
                else:
                    yield S.op('act', lambda e: e.activation(out=hbn[:, grp * 4:(grp + 1) * 4, :], in_=H4, func=AF.Copy),
                               reads=[N_("H")], writes=[("Hb", 1 - hbpar, grp)])
                yield S.op('act', lambda e: e.activation(out=Yw[:, :, tokc], in_=v4(ps[:, C0, HI]), func=AF.Copy),
                           reads=PSU(C0, 256, 512), writes=[N_("Yw")])
                hbpar = 1 - hbpar
            for jj in range(4):
                j = grp * 4 + jj
                col = lambda base, j=j: pv[:, base + j:base + j + 1]
                yc, sq_, yn = L.ptmp
                yield S.op('pe', lambda e: e.matmul(ps[:, C1, LO], bones64, Yw[:, jj, :], start=True, stop=True), reads=[N_("Yw"), "cf"], writes=PSU(C1, 0, 256))
                yield S.op('dve', lambda e: e.tensor_tensor(out=yc, in0=Yw[:, jj, :], in1=ps[:, C1, LO], op=ALU.subtract),
                           reads=[N_("Yw")] + PSU(C1, 0, 256), writes=[N_("yc")])
                yield S.op('pool', lambda e: e.tensor_tensor(out=sq_, in0=yc, in1=yc, op=ALU.mult), reads=[N_("yc")], writes=[N_("sq_")])
                yield S.op('pe', lambda e: e.matmul(ps[:, C1, HI], bones64, sq_, start=True, stop=True), reads=[N_("sq_"), "cf"], writes=PSU(C1, 256, 512))
                yield S.op('act', lambda e: e.activation(out=sq_, in_=ps[:, C1, HI], func=AF.Ln, bias=epsc[:, 1:2]),
                           reads=PSU(C1, 256, 512) + ["epsc"], writes=[N_("sq_")])
                yield S.op('act', lambda e: e.activation(out=sq_, in_=sq_, func=AF.Exp, scale=-0.5), reads=[N_("sq_")], writes=[N_("sq_")])
                yield S.op('pool', lambda e: e.tensor_tensor(out=yn, in0=yc, in1=sq_, op=ALU.mult), reads=[N_("yc"), N_("sq_")], writes=[N_("yn")])
                yield S.op('act', lambda e: e.activation(out=yn, in_=yn, func=AF.Identity, bias=col(PV_LNB), scale=col(PV_LNW)),
                           reads=[N_("yn"), "pv"], writes=[N_("yn")])
                yield S.op('pool', lambda e: e.tensor_tensor(out=yn, in0=yn, in1=BON[:, jj, :], op=ALU.add), reads=[N_("yn"), N_("BON", wp, jj)], writes=[N_("yn")])
                yield S.op('pool', lambda e: e.tensor_tensor(out=mixr[:, jj, :], in0=yn, in1=grs[:, jj, :], op=ALU.mult),
                           reads=[N_("yn"), N_("grs", wp, jj)], writes=[N_("mixr", jj)])
                yield S.dma('pool', scrM[8 + j][:, wtok], mixr[:, jj, :], reads=[N_("mixr", jj)], writes=[("scrM", 8 + j, wn)], key=N_("rm", jj))
            chain_done[grp] = wn + 1

    prep_done = [0, 0]
    chain_done = [0, 0]
    gens = [prep_stream(0), prep_stream(1), chain_stream(0), chain_stream(1)]
    while gens:
        for gen in list(gens):
            try:
                next(gen)
            except StopIteration:
                gens.remove(gen)
    S.barrier()

    o = 0
    wo, o = carve(o, [128, 16, 2048], BF16)
    mt, o = carve(o, [128, 2, 16, 128], BF16)
    xt2, o = carve(o, [128, 2, D], F32)
    ot, o = carve(o, [128, 2, D], F32)
    assert o <= ARENA_W, o
    for nb in range(4):
        S.dma('pool', wo[:, :, nb * 512:(nb + 1) * 512], wout_d[nb], writes=[("wo", nb)], key=("wo", nb))
    allM = []
    def o_load(tt):
        u = tt % 2
        tsl = slice(tt * 128, (tt + 1) * 128)
        S.dma('sp', mt[:, u], scrM[:, :, tsl].rearrange("c p t -> p c t"), reads=allM, writes=[("mt", u)], key=("om", u))
        S.dma('sp', xt2[:, u, :], x_d[tsl, :], writes=[("xt2", u)], key=("ox", u))

    o_load(0)
    for tt in range(NT):
        u = tt % 2
        tsl = slice(tt * 128, (tt + 1) * 128)
        if tt + 1 < NT:
            o_load(tt + 1)
        for nb in range(4):
            bank = 4 * u + nb
            fns = [lambda e, kc=kc, bank=bank, nb=nb: e.matmul(ps[:, bank, :], mt[:, u, kc, :], wo[:, kc, nb * 512:(nb + 1) * 512],
                                                               start=(kc == 0), stop=(kc == 15)) for kc in range(16)]
            S.multi('pe', fns, reads=[("mt", u), ("wo", nb)], writes=PSU(bank, 0, 512))
        pso = ps[:, 4 * u:4 * u + 4, :]
        allb = [r for b in range(4 * u, 4 * u + 4) for r in PSU(b, 0, 512)]
        S.op('act', lambda e: e.activation(out=ot[:, u, :].rearrange("p (a b) -> p a b", a=4), in_=pso, func=AF.Square,
                                           scale=float(D) ** -0.5, accum_out=small[:, 32 + tt:33 + tt]),
             reads=allb, writes=[("ot", u), ("ss2", tt)])
        S.op('act', lambda e: e.activation(out=small[:, 48 + tt:49 + tt], in_=small[:, 32 + tt:33 + tt], func=AF.Sqrt, bias=epsc[:, 0:1]),
             reads=[("ss2", tt), "epsc"], writes=[("rs2", tt)])
        S.op('dve', lambda e: e.reciprocal(out=small[:, 48 + tt:49 + tt], in_=small[:, 48 + tt:49 + tt]),
             reads=[("rs2", tt)], writes=[("rs2", tt)])
        S.op('dve', lambda e: e.scalar_tensor_tensor(out=ot[:, u, :].rearrange("p (a b) -> p a b", a=4), in0=pso,
                                                     scalar=small[:, 48 + tt:49 + tt], in1=gpb[:].rearrange("p (a b) -> p a b", a=4),
                                                     op0=ALU.mult, op1=ALU.mult),
             reads=allb + [("rs2", tt), "gpb"], writes=[("ot", u)])
        S.op('pool', lambda e: e.tensor_tensor(out=ot[:, u, :], in0=ot[:, u, :], in1=xt2[:, u, :], op=ALU.add),
             reads=[("ot", u), ("xt2", u)], writes=[("ot", u)])
        S.dma('pool', out_d[tsl, :], ot[:, u, :], reads=[("ot", u)], writes=[("out", tt)], key=("oo", u))
    S.barrier()
    es.close()
    return nc, S


_CACHE = {}


def _host_consts():
    cbm = np.zeros((128, CB_N), np.float32)
    cbm[:, CB_IDENT:CB_IDENT + 128] = np.eye(128, dtype=np.float32)
    s = np.arange(128)[:, None]
    q = np.arange(128)[None, :]
    prev = (s > q).astype(np.float32)
    cur = (s <= q).astype(np.float32)
    cbm[:, CB_MASKA:CB_MASKA + 128] = prev
    cbm[:, CB_MASKA + 128:CB_MASKA + 256] = cur
    cbm[:, CB_MASKA0 + 128:CB_MASKA0 + 256] = cur
    p = (np.arange(128) % 64)[:, None]
    f = np.arange(64)[None, :]
    cbm[:, CB_MST:CB_MST + 64] = (p < f)
    cbm[:, CB_MS:CB_MS + 64] = (f < p)
    cbm[:, CB_MIT:CB_MIT + 64] = (p <= f)
    cbm[:, CB_I64:CB_I64 + 64] = (p == f)
    cbm[:, CB_OL0:CB_OL0 + 64] = 1.0
    cbm[:, CB_OL1 + 64:CB_OL1 + 128] = 1.0
    cfm = np.zeros((128, CF_N), np.float32)
    blk = (np.arange(128)[:, None] // 64 == np.arange(128)[None, :] // 64).astype(np.float32)
    cfm[:, CF_BONES:CF_BONES + 128] = blk
    cfm[:, CF_BONES64:CF_BONES64 + 128] = blk / 64.0
    rm = np.ones(512, np.float32)
    rm[::64] = 0.0
    cfm[:, CF_RESET:CF_RESET + 512] = rm[None, :]
    cfm[:, CF_ONES:CF_ONES + 128] = 1.0
    return cbm, cfm


def _colidx():
    idx = []
    idx += [np.arange(1024, 1152), np.arange(1152, 1280), np.arange(1024, 1152), np.arange(1152, 1280)]
    idx.append(np.arange(1280, 1536))
    idx.append(np.arange(0, 1024))
    idx.append(np.arange(1536, 2560))
    R0 = 2560
    idx.append(np.arange(R0 + 3072, R0 + 3200))
    for j in range(8):
        for base in (0, 1024, 2048, 3200):
            idx.append(np.arange(R0 + base + j * 128, R0 + base + (j + 1) * 128))
    idx = np.concatenate(idx)
    assert idx.size == NBLK * 128
    return idx


def kernel(x, c, w_ada, b_ada, pre_norm_g, post_norm_g, w_in, w_out, attn_sinks,
           rwkv_mu, rwkv_w0, rwkv_w_up, rwkv_a0, rwkv_a_up, rwkv_k_k, rwkv_k_a,
           rwkv_r_k, rwkv_ln_w, rwkv_ln_b):
    f = lambda a: np.ascontiguousarray(np.asarray(a, dtype=np.float32))
    x = f(x); c = f(c)
    B = x.shape[0]
    if 'nc' not in _CACHE:
        _CACHE['nc'] = build_program()
    nc, S = _CACHE['nc']
    wada_h = f(np.asarray(w_ada[0]).reshape(16, 128, 12, 512).transpose(2, 1, 0, 3))
    win_h = f(np.asarray(w_in[0])[:, _colidx()].reshape(16, 128, NBLK, 128).transpose(2, 1, 0, 3))
    wout_h = f(np.asarray(w_out[0]).reshape(16, 128, 4, 512).transpose(2, 1, 0, 3))
    col = lambda v: np.asarray(v, np.float32).reshape(-1, 128).T
    pvec = np.zeros((128, PV_N), np.float32)
    pvec[:, PV_PREG:PV_PREG + 16] = col(pre_norm_g[0])
    mu = np.asarray(rwkv_mu[0], np.float32)
    pvec[:, PV_MU] = mu[3072:3200]
    for j in range(8):
        for r_ in range(3):
            pvec[:, PV_MU + 1 + 3 * j + r_] = mu[r_ * 1024 + j * 128:r_ * 1024 + (j + 1) * 128]
    pvec[:, PV_W0:PV_W0 + 8] = col(rwkv_w0[0])
    pvec[:, PV_A0:PV_A0 + 8] = col(rwkv_a0[0])
    pvec[:, PV_KK:PV_KK + 8] = col(rwkv_k_k[0])
    pvec[:, PV_KA:PV_KA + 8] = col(rwkv_k_a[0])
    pvec[:, PV_RK:PV_RK + 8] = col(np.asarray(rwkv_r_k[0]).reshape(-1))
    pvec[:, PV_LNW:PV_LNW + 8] = col(rwkv_ln_w[0])
    pvec[:, PV_LNB:PV_LNB + 8] = col(rwkv_ln_b[0])
    lora_h = f(np.concatenate([np.asarray(rwkv_w_up[0]), np.asarray(rwkv_a_up[0])], axis=0))
    sinks_h = f(np.repeat(np.asarray(attn_sinks[0], np.float32), 64)[None, :])
    sinkc_h = f(np.asarray(attn_sinks[0], np.float32).reshape(8, 2)[:, np.arange(128) // 64].T)
    cbm, cfm = _host_consts()
    bada_h = f(np.asarray(b_ada[0])[None, :])
    postg_h = f(np.asarray(post_norm_g[0])[None, :])
    in_maps = []
    for b in range(B):
        in_maps.append({
            "x": x[b], "cvec": f(c[b].reshape(16, 128).T), "wada": wada_h, "bada": bada_h, "win": win_h,
            "wout": wout_h, "pvec": pvec, "lora": lora_h, "postg": postg_h, "sinksrep": sinks_h, "sinkcol": sinkc_h,
            "constb": cbm, "constf": cfm,
        })
    res = run_bass_kernel_spmd(nc, in_maps, core_ids=list(range(B)))
    return np.stack([np.asarray(r["out"], dtype=np.float32) for r in res.results], axis=0)
```

```python
import numpy as np
import concourse.bass as bass
import concourse.mybir as mybir
from concourse.bass_utils import run_bass_kernel_spmd
from contextlib import ExitStack

F32 = mybir.dt.float32
BF16 = mybir.dt.bfloat16
AF = mybir.ActivationFunctionType
ALU = mybir.AluOpType

T = 2048
D = 2048
NT = 16
RMS_EPS = 1e-6
GN_EPS = 64e-5
C0 = float(np.exp(-0.5))
NBLK = 55
DEBUG = False
POOL_COPY = False
DVE_EVAC = True

PV_PREG = 0
PV_MU = 16
PV_W0 = 41
PV_A0 = 49
PV_KK = 57
PV_KA = 65
PV_RK = 73
PV_LNW = 81
PV_LNB = 89
PV_N = 97

CB_IDENT = 0
CB_MASKA = 128
CB_MASKA0 = 384
CB_MST = 640
CB_MS = 704
CB_MIT = 768
CB_I64 = 832
CB_OL0 = 896
CB_OL1 = 1024
CB_N = 1152
CF_BONES = 0
CF_BONES64 = 128
CF_RESET = 256
CF_ONES = 768
CF_N = 896


class Sched:
    EPOCH = 12000
    RECYCLE = False

    def __init__(self, nc, es):
        self.nc = nc
        self.es = es
        self.engs = {'pe': nc.tensor, 'act': nc.scalar, 'dve': nc.vector, 'pool': nc.gpsimd, 'sp': nc.sync}
        self.sems = {e: [] for e in self.engs}
        self.cnt = {e: 0 for e in self.engs}
        self.waited = {}
        self.lastw = {}
        self.readers = {}
        self.dsem = {}
        self.dpool = []
        self.nsem = 0
        self.ninstr = 0

    def _newsem(self, name):
        self.nsem += 1
        return self.es.enter_context(self.nc.semaphore(name))

    def _cursem(self, e):
        if not self.sems[e] or self.cnt[e] >= self.EPOCH:
            self.sems[e].append(self._newsem("s_%s_%d" % (e, len(self.sems[e]))))
            self.cnt[e] = 0
        return len(self.sems[e]) - 1

    def _wait(self, e, tok):
        if tok is None:
            return
        if tok[0] == 'e':
            _, te, ep, c = tok
            if te == e and e == 'pe':
                return
            key = (e, 'e', te, ep)
            if self.waited.get(key, 0) >= c:
                return
            self.waited[key] = c
            self.engs[e].wait_ge(self.sems[te][ep], c)
        else:
            _, k, c = tok
            key = (e, 'd', k)
            if self.waited.get(key, 0) >= c:
                return
            self.waited[key] = c
            self.engs[e].wait_ge(self.dsem[k][0], c)

    def _deps(self, e, reads, writes):
        for r in reads:
            self._wait(e, self.lastw.get(r))
        for w in writes:
            self._wait(e, self.lastw.get(w))
            for t in self.readers.get(w, {}).values():
                self._wait(e, t)

    def _commit(self, tok, reads, writes):
        for w in writes:
            self.lastw[w] = tok
            self.readers[w] = {}
        for r in reads:
            if r in writes:
                continue
            d = self.readers.setdefault(r, {})
            k = (tok[1],) if tok[0] == 'e' else ('d', tok[1])
            d[k] = tok

    @staticmethod
    def _pe_writes(e, writes):
        if e != 'pe':
            return writes
        out = []
        seen = set()
        for w in writes:
            if isinstance(w, tuple) and len(w) == 3 and w[0] == "ps":
                if w[1] not in seen:
                    seen.add(w[1])
                    out += [("ps", w[1], s_) for s_ in range(8)]
            else:
                out.append(w)
        return out

    def op(self, e, fn, reads=(), writes=()):
        writes = self._pe_writes(e, writes)
        self._deps(e, reads, writes)
        ep = self._cursem(e)
        ins = fn(self.engs[e])
        ins.then_inc(self.sems[e][ep], 1)
        self.cnt[e] += 1
        self.ninstr += 1
        tok = ('e', e, ep, self.cnt[e])
        self._commit(tok, reads, writes)
        return tok

    def multi(self, e, fns, reads=(), writes=()):
        writes = self._pe_writes(e, writes)
        self._deps(e, reads, writes)
        ep = self._cursem(e)
        ins = None
        for fn in fns:
            ins = fn(self.engs[e])
            self.ninstr += 1
        ins.then_inc(self.sems[e][ep], 1)
        self.cnt[e] += 1
        tok = ('e', e, ep, self.cnt[e])
        self._commit(tok, reads, writes)
        return tok

    def dma(self, q, out, in_, reads=(), writes=(), key=None, serialize=True):
        if key not in self.dsem:
            if self.dpool:
                self.dsem[key] = self.dpool.pop()
            else:
                self.dsem[key] = [self._newsem("d%d" % self.nsem), 0]
        if serialize and self.dsem[key][1] > 0:
            self._wait(q, ('d', key, self.dsem[key][1]))
        self._deps(q, reads, writes)
        ins = self.engs[q].dma_start(out=out, in_=in_)
        ins.then_inc(self.dsem[key][0], 16)
        self.dsem[key][1] += 16
        self.ninstr += 1
        tok = ('d', key, self.dsem[key][1])
        self._commit(tok, reads, writes)
        return tok

    def barrier(self):
        last = {}
        for e in self.engs:
            if self.sems[e]:
                last[e] = ('e', e, len(self.sems[e]) - 1, self.cnt[e])
        for e in self.engs:
            for te, tok in last.items():
                if te != e and tok[3] > 0:
                    self._wait(e, tok)
            for k, (s, c) in self.dsem.items():
                if c > 0:
                    self._wait(e, ('d', k, c))
        self.lastw = {}
        self.readers = {}
        if self.RECYCLE:
            for k in list(self.dsem.keys()):
                self.dpool.append(self.dsem.pop(k))
                self.waited = {w: v for w, v in self.waited.items() if not (w[1] == 'd' and w[2] == k)}


def build_program():
    nc = bass.Bass("TRN2", target_bir_lowering=False)
    dt = nc.dram_tensor
    x_d = dt("x", [T, D], F32, kind="ExternalInput").ap()
    cvec_d = dt("cvec", [128, 16], F32, kind="ExternalInput").ap()
    wada_d = dt("wada", [12, 128, 16, 512], F32, kind="ExternalInput").ap()
    bada_d = dt("bada", [1, 6144], F32, kind="ExternalInput").ap()
    win_d = dt("win", [NBLK, 128, 16, 128], F32, kind="ExternalInput").ap()
    wout_d = dt("wout", [4, 128, 16, 512], F32, kind="ExternalInput").ap()
    pvec_d = dt("pvec", [128, PV_N], F32, kind="ExternalInput").ap()
    lora_d = dt("lora", [128, 1024], F32, kind="ExternalInput").ap()
    postg_d = dt("postg", [1, 2048], F32, kind="ExternalInput").ap()
    sinks_d = dt("sinksrep", [1, 1024], F32, kind="ExternalInput").ap()
    sinkc_d = dt("sinkcol", [128, 8], F32, kind="ExternalInput").ap()
    cb_d = dt("constb", [128, CB_N], F32, kind="ExternalInput").ap()
    cf_d = dt("constf", [128, CF_N], F32, kind="ExternalInput").ap()
    out_d = dt("out", [T, D], F32, kind="ExternalOutput").ap()
    scrA = dt("scrA", [28, 128, T], BF16, kind="Internal").ap()
    scrR = dt("scrR", [25, 128, T], F32, kind="Internal").ap()
    scrM = dt("scrM", [16, 128, T], BF16, kind="Internal").ap()

    es = ExitStack()
    S = Sched(nc, es)
    sb = lambda name, shape, dtype: es.enter_context(nc.sbuf_tensor(name, shape, dtype))

    cb = sb("cb", [128, CB_N], BF16)
    cf = sb("cf", [128, CF_N], F32)
    pv = sb("pv", [128, PV_N], F32)
    omm = sb("omm", [128, 25], F32)
    omka = sb("omka", [128, 8], F32)
    epsc = sb("epsc", [128, 2], F32)
    lora = sb("lora_sb", [128, 1024], BF16)
    gpb = sb("gpb", [128, 2048], F32)
    gs = sb("gs", [128, 16], F32)
    shiftc = sb("shiftc", [128, 16], F32)
    sinkl = sb("sinkl", [1, 1024], F32)
    sinkc = sb("sinkc", [128, 8], F32)
    Hst = sb("Hst", [128, 8, 64], F32)
    small = sb("small", [128, 64], F32)
    one11 = cf[0:1, CF_ONES:CF_ONES + 1]
    ones_row = cf[0:1, CF_ONES:CF_ONES + 128]

    ARENA_W = 42400
    VP_OFF = 27700
    arena = sb("arena", [128, ARENA_W], F32)
    ps = es.enter_context(nc.psum_tensor("ps", [128, 8, 512], F32))

    def carve(off, shape, dtype):
        n = int(np.prod(shape[1:]))
        if dtype == BF16:
            assert n % 2 == 0
            a = arena[:, off:off + n // 2].bitcast(BF16)
            w = n // 2
        else:
            a = arena[:, off:off + n]
            w = n
        if len(shape) == 3:
            a = a.rearrange("p (a b) -> p a b", a=shape[1])
        elif len(shape) == 4:
            a = a.rearrange("p (a b c) -> p a b c", a=shape[1], b=shape[2])
        return a, off + w

    def PSU(bank, lo, hi):
        return [("ps", bank, s) for s in range(lo // 64, (hi - 1) // 64 + 1)]

    Vp, _vpend = carve(VP_OFF, [128, 17, 4, 192], BF16)
    assert _vpend <= ARENA_W

    S.dma('pool', cb[:], cb_d, writes=["cb"], key="c0")
    S.dma('sp', cf[:], cf_d, writes=["cf"], key="c1")
    S.dma('sp', pv[:], pvec_d, writes=["pv"], key="c2")
    S.dma('pool', lora[:], lora_d, writes=["lora"], key="c3")
    S.dma('sp', sinkl[:], sinks_d, writes=["sinkl"], key="c4")
    S.op('act', lambda e: e.activation(out=sinkl[:], in_=sinkl[:], func=AF.Exp), reads=["sinkl"], writes=["sinkl"])
    S.dma('sp', sinkc[:], sinkc_d, writes=["sinkc"], key="c6")
    S.op('act', lambda e: e.activation(out=sinkc[:], in_=sinkc[:], func=AF.Exp), reads=["sinkc"], writes=["sinkc"])
    S.op('dve', lambda e: e.tensor_scalar(out=omm[:], in0=pv[:, PV_MU:PV_MU + 25], scalar1=-1.0, scalar2=1.0,
                                          op0=ALU.mult, op1=ALU.add), reads=["pv"], writes=["omm"])
    S.op('dve', lambda e: e.tensor_scalar(out=omka[:], in0=pv[:, PV_KA:PV_KA + 8], scalar1=-1.0, scalar2=1.0,
                                          op0=ALU.mult, op1=ALU.add), reads=["pv"], writes=["omka"])
    S.op('pool', lambda e: e.memset(epsc[:, 0:1], RMS_EPS), writes=["epsc"])
    S.op('pool', lambda e: e.memset(epsc[:, 1:2], GN_EPS), writes=["epsc"])
    S.op('pool', lambda e: e.memset(Hst[:].rearrange("p a b -> p (a b)"), 0.0), writes=["H"])

    ident = cb[:, CB_IDENT:CB_IDENT + 128]

    o = 30000
    wa, o = carve(o, [128, 2, 16, 512], BF16)
    csb, o = carve(o, [128, 16], F32)
    scb, o = carve(o, [128, 16], BF16)
    brow, o = carve(o, [128, 2, 512], F32)
    mrow, o = carve(o, [128, 2, 512], F32)
    pgrow, o = carve(o, [128, 2, 512], F32)
    S.dma('sp', csb, cvec_d, writes=["csb"], key="c5")
    S.op('act', lambda e: e.activation(out=scb, in_=csb, func=AF.Silu), reads=["csb"], writes=["scb"])
    assert o <= ARENA_W, o

    def phase0():
      for nb in range(12):
          u = nb % 2
          S.dma('pool', wa[:, u], wada_d[nb], writes=[("wa", u)], key=("wa", u))
          S.dma('pool', brow[0:1, u, :], bada_d[0:1, nb * 512:(nb + 1) * 512], writes=[("brow", u)], key=("br", u))
          fns = []
          for kc in range(16):
              fns.append(lambda e, kc=kc: e.matmul(ps[0:1, u, :], scb[:, kc:kc + 1], wa[:, u, kc, :],
                                                   start=(kc == 0), stop=(kc == 15)))
          S.multi('pe', fns, reads=[("wa", u), "scb"], writes=PSU(u, 0, 512))
          S.op('dve', lambda e: e.tensor_tensor(out=mrow[0:1, u, :], in0=ps[0:1, u, :], in1=brow[0:1, u, :], op=ALU.add),
               reads=PSU(u, 0, 512) + [("brow", u)], writes=[("mrow", u)])
          if nb < 8:
              fns = []
              for i in range(4):
                  col = nb * 4 + i
                  fns.append(lambda e, i=i, col=col: e.matmul(ps[:, 2, col:col + 1], mrow[0:1, u, i * 128:(i + 1) * 128],
                                                              one11, start=True, stop=True))
              S.multi('pe', fns, reads=[("mrow", u), "cf"], writes=PSU(2, 0, 64))
          else:
              gb = nb - 8
              S.dma('pool', pgrow[0:1, u, :], postg_d[0:1, gb * 512:(gb + 1) * 512], writes=[("pgrow", u)], key=("pg", u))
              S.op('dve', lambda e: e.tensor_tensor(out=mrow[0:1, u, :], in0=mrow[0:1, u, :], in1=pgrow[0:1, u, :], op=ALU.mult),
                   reads=[("mrow", u), ("pgrow", u)], writes=[("mrow", u)])
              S.op('pe', lambda e: e.matmul(ps[:, 3, :], ones_row, mrow[0:1, u, :], start=True, stop=True),
                   reads=[("mrow", u), "cf"], writes=PSU(3, 0, 512))
              S.op('act', lambda e: e.activation(out=gpb[:, gb * 512:(gb + 1) * 512], in_=ps[:, 3, :], func=AF.Copy),
                   reads=PSU(3, 0, 512), writes=["gpb"])
          if nb == 7:
              S.op('dve', lambda e: e.tensor_copy(out=shiftc[:], in_=ps[:, 2, 0:16]), reads=PSU(2, 0, 64), writes=["shiftc"])
              S.op('dve', lambda e: e.scalar_tensor_tensor(out=gs[:], in0=ps[:, 2, 16:32], scalar=1.0, in1=pv[:, PV_PREG:PV_PREG + 16],
                                                           op0=ALU.add, op1=ALU.mult), reads=PSU(2, 0, 64) + ["pv"], writes=["gs"])
          yield

    o = 0
    hT, o = carve(o, [128, 16, T], BF16)
    o_after_hT = o
    xt, o = carve(o, [128, 2, D], F32)
    NG = 2
    xn, o = carve(o, [128, 4, NG, D], BF16)
    junk, o = carve(o, [128, D], BF16)
    assert o <= 30000, o

    def n_tile(tt):
        q, t4 = divmod(tt, NG)
        slot = q % 4
        u = tt % 2
        S.dma('sp', xt[:, u, :], x_d[tt * 128:(tt + 1) * 128, :], writes=[("xt", u)], key=("xt", u))
        S.op('act', lambda e: e.activation(out=junk, in_=xt[:, u, :], func=AF.Square, scale=float(D) ** -0.5,
                                           accum_out=small[:, tt:tt + 1]),
             reads=[("xt", u)], writes=["junk", ("ssq", tt)])
        S.op('act', lambda e: e.activation(out=small[:, 16 + tt:17 + tt], in_=small[:, tt:tt + 1], func=AF.Sqrt, bias=epsc[:, 0:1]),
             reads=[("ssq", tt), "epsc"], writes=[("rs", tt)])
        S.op('dve', lambda e: e.reciprocal(out=small[:, 16 + tt:17 + tt], in_=small[:, 16 + tt:17 + tt]),
             reads=[("rs", tt)], writes=[("rs", tt)])
        S.op('dve', lambda e: e.tensor_scalar(out=xn[:, slot, t4, :], in0=xt[:, u, :], scalar1=small[:, 16 + tt:17 + tt], scalar2=None,
                                              op0=ALU.mult), reads=[("xt", u), ("rs", tt)], writes=[("xn", slot, t4)])

    def n_stage1(q):
        for t4 in range(NG):
            n_tile(q * NG + t4)

    def n_stage2(q):
        slot = q % 4
        W_ = NG * 128
        for kc in range(16):
            bank = 4 + (kc // 2) % 4
            half = kc % 2
            pst = ps[:, bank, :].bitcast(BF16)[:, half * 512:half * 512 + W_]
            fns = []
            for t4 in range(NG):
                fns.append(lambda e, t4=t4, pst=pst: e.transpose(pst[:, t4 * 128:(t4 + 1) * 128],
                                                                 xn[:, slot, t4, kc * 128:(kc + 1) * 128], ident))
            S.multi('pe', fns, reads=[("xn", slot, t4) for t4 in range(NG)] + ["cb"], writes=PSU(bank, half * 256, half * 256 + 256))
            dst = hT[:, kc, q * W_:(q + 1) * W_]
            wr = [("hT", q * NG + t4) for t4 in range(NG)]
            if kc % 2 == 0:
                S.op('act', lambda e, pst=pst, dst=dst: e.activation(out=dst, in_=pst, func=AF.Identity,
                                                                    bias=shiftc[:, kc:kc + 1], scale=gs[:, kc:kc + 1]),
                     reads=PSU(bank, half * 256, half * 256 + 256) + ["gs", "shiftc"], writes=wr)
            else:
                S.op('dve', lambda e, pst=pst, dst=dst: e.tensor_scalar(out=dst, in0=pst, scalar1=gs[:, kc:kc + 1],
                                                                       scalar2=shiftc[:, kc:kc + 1], op0=ALU.mult, op1=ALU.add),
                     reads=PSU(bank, half * 256, half * 256 + 256) + ["gs", "shiftc"], writes=wr)

    p0 = phase0()

    def p0_step(n):
        for _ in range(n):
            try:
                next(p0)
            except StopIteration:
                return

    NQ = NT // NG
    for i in range(8):
        p0_step(1)
        if i < 3 * NG:
            n_tile(i)
    for q in range(NQ):
        n_stage2(q)
        if q % 2 == 1:
            p0_step(1)
        if q + 3 < NQ:
            n_stage1(q + 3)
    p0_step(12)
    S.barrier()

    o = o_after_hT
    wbuf, o = carve(o, [128, 3, 16, 128], BF16)
    stgb, o = carve(o, [128, 2, T], BF16)
    stgf, o = carve(o, [128, 2, T], F32)
    amu, o = carve(o, [128, T + 1], F32)
    S.op('pool', lambda e: e.memset(amu[:, 0:1], 0.0), writes=["amu0"])
    assert o <= VP_OFF, o
    S.op('pool', lambda e: e.memset(Vp.rearrange("p a b c -> p (a b c)"), 0.0), writes=[("Vp", i) for i in range(17)] + ["Vp"])
    hT_all = [("hT", tt) for tt in range(NT)]

    def blk_kind(blk):
        if blk < 2: return ('copy', blk, None)
        if blk < 6: return ('v', blk - 4, None)
        if blk < 14: return ('q', 4 + (blk - 6), None)
        if blk < 22: return ('silu', 12 + (blk - 14), None)
        if blk == 22: return ('lerp', 0, 0)
        j, r = divmod(blk - 23, 4)
        if r < 3: return ('lerp', 1 + 3 * j + r, 1 + 3 * j + r)
        return ('silu', 20 + j, None)

    SEQ = [0, 1] + list(range(4, NBLK))
    for p_ in range(2):
        S.dma('pool', wbuf[:, p_ % 3], win_d[SEQ[p_]], writes=[("wbuf", p_ % 3)], key=("wb", p_ % 3))
    def phaseI(blks):
      bankrot = 0
      nb_i = 0
      nf_i = 0
      for p_ in blks:
          blk = SEQ[p_]
          wu = p_ % 3
          if p_ + 2 < len(SEQ):
              yield S.dma('pool', wbuf[:, (p_ + 2) % 3], win_d[SEQ[p_ + 2]], writes=[("wbuf", (p_ + 2) % 3)], key=("wb", (p_ + 2) % 3))
          kind, sidx, mucol = blk_kind(blk)
          if kind == 'v':
              for tt in range(NT):
                  bank = bankrot % 4; bankrot += 1
                  fns = [lambda e, kc=kc, bank=bank, tt=tt: e.matmul(ps[:, bank, 0:128], hT[:, kc, tt * 128:(tt + 1) * 128],
                                                                     wbuf[:, wu, kc, :], start=(kc == 0), stop=(kc == 15))
                         for kc in range(16)]
                  yield S.multi('pe', fns, reads=[("hT", tt), ("wbuf", wu)], writes=PSU(bank, 0, 128))
                  dst = Vp[:, 1 + tt, 2 * sidx:2 * sidx + 2, 64:128]
                  src = ps[:, bank, 0:128].rearrange("p (a b) -> p a b", a=2)
                  eng = 'act' if tt % 2 == 0 else 'dve'
                  if eng == 'act':
                      yield S.op('act', lambda e, dst=dst, src=src: e.activation(out=dst, in_=src, func=AF.Copy),
                           reads=PSU(bank, 0, 128), writes=[("Vp", 1 + tt)])
                  else:
                      yield S.op('dve', lambda e, dst=dst, src=src: e.tensor_copy(out=dst, in_=src),
                           reads=PSU(bank, 0, 128), writes=[("Vp", 1 + tt)])
              continue
          if kind == 'lerp':
              su = nf_i % 2; nf_i += 1
              stg = stgf[:, su, :]
              stgname = ("stgf", su)
          else:
              su = nb_i % 2; nb_i += 1
              stg = stgb[:, su, :]
              stgname = ("stgb", su)
          for wn in range(4):
              bank = bankrot % 4; bankrot += 1
              tok = slice(wn * 512, (wn + 1) * 512)
              for pc4 in range(2):
                  fns = [lambda e, kc=kc, bank=bank, tok=tok: e.matmul(ps[:, bank, :], wbuf[:, wu, kc, :], hT[:, kc, tok],
                                                                       start=(kc == 0), stop=(kc == 15)) for kc in range(pc4 * 8, pc4 * 8 + 8)]
                  yield S.multi('pe', fns, reads=hT_all[wn * 4:(wn + 1) * 4] + [("wbuf", wu)], writes=PSU(bank, 0, 512))
              pin = ps[:, bank, :]
              if kind == 'copy':
                  yield S.op('act', lambda e, pin=pin, tok=tok: e.activation(out=stg[:, tok], in_=pin, func=AF.Copy),
                       reads=PSU(bank, 0, 512), writes=[stgname])
              elif kind == 'q':
                  yield S.op('dve', lambda e, pin=pin, tok=tok: e.tensor_scalar(out=stg[:, tok], in0=pin, scalar1=0.125, scalar2=None,
                                                                         op0=ALU.mult), reads=PSU(bank, 0, 512), writes=[stgname])
              elif kind == 'silu':
                  yield S.op('act', lambda e, pin=pin, tok=tok: e.activation(out=stg[:, tok], in_=pin, func=AF.Silu),
                       reads=PSU(bank, 0, 512), writes=[stgname])
              else:
                  mc = pv[:, PV_MU + mucol:PV_MU + mucol + 1]
                  oc = omm[:, mucol:mucol + 1]
                  yield S.op('act', lambda e, pin=pin, wn=wn, mc=mc: e.activation(out=amu[:, 1 + wn * 512:1 + (wn + 1) * 512], in_=pin,
                                                                            func=AF.Copy, scale=mc),
                       reads=PSU(bank, 0, 512) + ["pv"], writes=[("amu", wn)])
                  rd = [("amu", wn)] + ([("amu", wn - 1)] if wn > 0 else ["amu0"])
                  yield S.op('dve', lambda e, pin=pin, wn=wn, oc=oc, tok=tok: e.scalar_tensor_tensor(
                      out=stg[:, tok], in0=pin, scalar=oc, in1=amu[:, wn * 512:(wn + 1) * 512], op0=ALU.mult, op1=ALU.add),
                      reads=PSU(bank, 0, 512) + rd + ["omm"], writes=[stgname])
          if kind == 'lerp':
              yield S.dma('sp', scrR[sidx], stg, reads=[stgname], writes=[("scrR", sidx)], key=("spf", su))
          else:
              yield S.dma('sp', scrA[sidx], stg, reads=[stgname], writes=[("scrA", sidx)], key=("spb", su))

    o = _vpend
    qT, o = carve(o, [128, 1, 2, T], BF16)
    kTg, o = carve(o, [128, 1, T + 128], BF16)
    gaT, o = carve(o, [128, 1, 2, T], BF16)
    PT, o = carve(o, [128, 2, 2, 512], BF16)
    rden, o = carve(o, [128, 2, 256], F32)
    ynum, o = carve(o, [128, 2, 256], F32)
    mixst, o = carve(o, [128, 2, 2, 128], BF16)
    assert o <= ARENA_W, o
    S.op('pool', lambda e: e.memset(kTg[:, :, 0:128], 0.0), writes=[("kTg", 0)])
    maskA = cb[:, CB_MASKA:CB_MASKA + 256]
    maskA0 = cb[:, CB_MASKA0:CB_MASKA0 + 256]
    OL = [cb[:, CB_OL0:CB_OL0 + 128], cb[:, CB_OL1:CB_OL1 + 128]]
    def a_load(g):
        gu = 0
        for pi in range(2):
            yield S.dma('sp', qT[:, gu, pi, :], scrA[4 + 2 * g + pi], reads=[("scrA", 4 + 2 * g + pi)], writes=[("qT", gu)],
                  key=("aq", gu), serialize=(pi == 0))
        ksrc = scrA[g // 2][64 * (g % 2):64 * (g % 2) + 64, :]
        yield S.dma('sp', kTg[0:64, gu, 128:], ksrc, reads=[("scrA", g // 2)], writes=[("kTg", gu)], key=("ak", gu))
        yield S.dma('sp', kTg[64:128, gu, 128:], ksrc, reads=[("scrA", g // 2)], writes=[("kTg", gu)], key=("ak", gu), serialize=False)

    def a_stage1(unit):
        g, tt = divmod(unit, NT)
        gu = 0
        u = unit % 2
        if tt == 0:
            yield from a_load(g)
        for e_ in range(2):
            bank = 4 + e_
            rows = slice(64 * e_, 64 * e_ + 64)
            fns = []
            scv = ps[:, bank, :].rearrange("p (a b c) -> p a b c", a=2, b=2)
            for pc in range(2):
                fns.append(lambda e, rows=rows, pc=pc, scv=scv: e.matmul(
                    scv[:, :, pc, :], kTg[rows, gu, (tt + pc) * 128:(tt + pc + 1) * 128],
                    qT[rows, gu, :, tt * 128:(tt + 1) * 128], start=True, stop=True))
            yield S.multi('pe', fns, reads=[("kTg", gu), ("qT", gu)], writes=PSU(bank, 0, 512))
            yield S.op('act', lambda e, bank=bank, e_=e_: e.activation(out=PT[:, u, e_, :], in_=ps[:, bank, :], func=AF.Exp),
                 reads=PSU(bank, 0, 512), writes=[("PT", u, e_)])
        mk = maskA0 if tt == 0 else maskA
        ptv = PT[:, u].rearrange("p e (a b) -> p (e a) b", a=2)
        yield S.op('dve', lambda e, ptv=ptv, mk=mk: e.tensor_tensor(out=ptv, in0=ptv, in1=mk.unsqueeze(1).broadcast_to([128, 4, 256]),
                                                              op=ALU.mult),
             reads=[("PT", u, 0), ("PT", u, 1), "cb"], writes=[("PT", u, 0), ("PT", u, 1)])

    def a_stage2(unit):
        g, tt = divmod(unit, NT)
        gu = 0
        u = unit % 2
        nbank = 6 + u
        if tt == 0:
            for pi in range(2):
                yield S.dma('sp', gaT[:, gu, pi, :], scrA[12 + 2 * g + pi], reads=[("scrA", 12 + 2 * g + pi)], writes=[("gaT", gu)],
                            key=("ag", gu), serialize=(pi == 0))
        fns = []
        seq = [(e_, pc) for e_ in range(2) for pc in range(2)]
        numv = ps[:, nbank, 0:256].rearrange("p (a b) -> p a b", a=2)
        denv = ps[:, nbank, 256:512].rearrange("p (a b) -> p a b", a=2)
        ptq = lambda e_, pc: PT[:, u, e_, :].rearrange("p (a b c) -> p a b c", a=2, b=2)[:, :, pc, :]
        for i, (e_, pc) in enumerate(seq):
            vl = Vp[:, tt + pc, g, 64:192] if e_ == 0 else Vp[:, tt + pc, g, 0:128]
            fns.append(lambda e, e_=e_, pc=pc, vl=vl, i=i: e.matmul(numv, vl, ptq(e_, pc), start=(i == 0), stop=(i == 3)))
        for i, (e_, pc) in enumerate(seq):
            fns.append(lambda e, e_=e_, pc=pc, i=i: e.matmul(denv, OL[e_], ptq(e_, pc), start=(i == 0), stop=(i == 3)))
        yield S.multi('pe', fns, reads=[("PT", u, 0), ("PT", u, 1), ("Vp", tt), ("Vp", tt + 1), "Vp", "cb", "cf", "sinkl"],
                writes=PSU(nbank, 0, 512))
        for pi in range(2):
            pr = 2 * g + pi
            yield S.op('act', lambda e, pi=pi, pr=pr: e.activation(out=rden[:, u, pi * 128:(pi + 1) * 128],
                                                                 in_=ps[:, nbank, 256 + pi * 128:256 + (pi + 1) * 128],
                                                                 func=AF.Ln, bias=sinkc[:, pr:pr + 1]),
                       reads=PSU(nbank, 256, 512) + ["sinkc"], writes=[("rden", u)])
        yield S.op('act', lambda e: e.activation(out=rden[:, u, :], in_=rden[:, u, :], func=AF.Exp, scale=-1.0), reads=[("rden", u)],
             writes=[("rden", u)])
        yield S.op('dve', lambda e: e.tensor_tensor(out=ynum[:, u, :], in0=ps[:, nbank, 0:256], in1=rden[:, u, :], op=ALU.mult),
             reads=PSU(nbank, 0, 256) + [("rden", u)], writes=[("ynum", u)])
        yield S.op('pool', lambda e: e.tensor_tensor(out=mixst[:, u, :, :],
                                               in0=ynum[:, u, :].rearrange("p (a b) -> p a b", a=2),
                                               in1=gaT[:, gu, :, tt * 128:(tt + 1) * 128], op=ALU.mult),
             reads=[("ynum", u), ("gaT", gu)], writes=[("mixst", u)])
        yield S.dma('sp', scrM[2 * g:2 * g + 2, :, tt * 128:(tt + 1) * 128].rearrange("c p t -> p c t"), mixst[:, u, :, :],
              reads=[("mixst", u)], writes=[("scrM", g, tt)], key=("am", u))

    NU = 4 * NT

    def phaseA():
        yield from a_stage1(0)
        for unit in range(NU):
            if unit + 1 < NU:
                yield from a_stage1(unit + 1)
            yield from a_stage2(unit)

    for _ in phaseI(range(0, 20)):
        pass
    gens = [phaseI(range(20, len(SEQ))), phaseA()]
    while gens:
        for gen in list(gens):
            try:
                next(gen)
            except StopIteration:
                gens.remove(gen)
    S.barrier()

    WR = 256
    NWR = T // WR
    CW = WR // 64
    o = 0
    Hba, o = carve(o, [128, 8, 64], BF16)
    Hbb, o = carve(o, [128, 8, 64], BF16)
    Hb = [Hba, Hbb]
    S.op('pool', lambda e: e.memset(Hba.rearrange("p a b -> p (a b)"), 0.0), writes=[("Hb", 0, 0), ("Hb", 0, 1)])
    S.op('pool', lambda e: e.memset(Hbb.rearrange("p a b -> p (a b)"), 0.0), writes=[("Hb", 1, 0), ("Hb", 1, 1)])
    bones = cf[:, CF_BONES:CF_BONES + 128]
    bones64 = cf[:, CF_BONES64:CF_BONES64 + 128]
    resetm = cf[:, CF_RESET:CF_RESET + WR]
    mST = cb[:, CB_MST:CB_MST + 64]
    mS = cb[:, CB_MS:CB_MS + 64]
    mIT = cb[:, CB_MIT:CB_MIT + 64]
    I64 = cb[:, CB_I64:CB_I64 + 64]
    bc4 = lambda m: m.unsqueeze(1).broadcast_to([128, 4, 64])
    v4 = lambda a: a.rearrange("p (a b) -> p a b", a=4)

    class LB:
        pass
    lanes = []
    for g in range(2):
        L = LB()
        L.tw, o = carve(o, [128, WR], BF16)
        L.wdl, o = carve(o, [128, WR], F32)
        L.rkv, o = carve(o, [128, 2, 3, WR], F32)
        L.tmp = []
        for i in range(12):
            t_, o = carve(o, [128, WR], F32)
            L.tmp.append(t_)
        L.tb = []
        for i in range(3):
            t_, o = carve(o, [128, WR], BF16)
            L.tb.append(t_)
        L.grs, L.RT, L.AT, L.BT, L.KT, L.VT, L.BCt, L.KCt, L.BON, L.Wend = [], [], [], [], [], [], [], [], [], []
        for wp in range(2):
            for lst, shp, dt_ in ((L.grs, [128, 4, WR], BF16), (L.RT, [128, 4, WR], BF16), (L.AT, [128, 4, WR], BF16),
                                  (L.BT, [128, 4, WR], BF16), (L.KT, [128, 4, WR], BF16), (L.VT, [128, CW, 4, 64], BF16),
                                  (L.BCt, [128, CW, 4, 64], BF16), (L.KCt, [128, CW, 4, 64], BF16),
                                  (L.BON, [128, 4, WR], F32), (L.Wend, [128, 4, CW], F32)):
                t_, o = carve(o, shp, dt_)
                lst.append(t_)
        L.Yw, o = carve(o, [128, 4, WR], F32)
        L.mixr, o = carve(o, [128, 4, WR], BF16)
        L.ptmp = []
        for i in range(3):
            t_, o = carve(o, [128, WR], F32)
            L.ptmp.append(t_)
        L.NNs = []
        for i in range(2):
            t_, o = carve(o, [128, 2, 256], BF16)
            L.NNs.append(t_)
        L.PTm = []
        for i in range(2):
            t_, o = carve(o, [128, 256], BF16)
            L.PTm.append(t_)
        L.G2 = []
        for i in range(2):
            t_, o = carve(o, [128, 2, 256], BF16)
            L.G2.append(t_)
        L.G3 = []
        for i in range(2):
            t_, o = carve(o, [128, 256], BF16)
            L.G3.append(t_)
        L.Xb, o = carve(o, [128, 256], BF16)
        L.Ub, o = carve(o, [128, 256], BF16)
        lanes.append(L)
    assert o <= ARENA_W, o
    LO, HI = slice(0, 256), slice(256, 512)

    def prep_stream(grp):
        L = lanes[grp]
        N_ = lambda n, *a: (n, grp) + a
        P0, P1 = 4 * grp, 4 * grp + 1
        tw, wdl, rkv = L.tw, L.wdl, L.rkv
        sw, a_, cs, W_, Wi, Wp, WC, kkr, sq, kkn, kmod, ba = L.tmp
        vlb, bCb, kCb = L.tb
        for wn in range(NWR):
            wp = wn % 2
            while chain_done[grp] < wn - 1:
                yield None
            grs, RT, AT, BT, KT, VT, BCt, KCt, BON, Wend = (L.grs[wp], L.RT[wp], L.AT[wp], L.BT[wp], L.KT[wp], L.VT[wp],
                                                            L.BCt[wp], L.KCt[wp], L.BON[wp], L.Wend[wp])
            wtok = slice(wn * WR, (wn + 1) * WR)
            yield S.dma('sp', wdl, scrR[0][:, wtok], reads=[("scrR", 0)], writes=[N_("wdl")], key=N_("rw"))
            yield S.op('act', lambda e: e.activation(out=tw[0:64, :], in_=wdl[0:64, :], func=AF.Tanh), reads=[N_("wdl")], writes=[N_("tw")])
            yield S.op('dve', lambda e: e.tensor_copy(out=tw[64:128, :], in_=wdl[64:128, :]), reads=[N_("wdl")], writes=[N_("tw1")])
            for jj in range(4):
                j = grp * 4 + jj
                ru = jj % 2
                rl, kl, vl_ = rkv[:, ru, 0, :], rkv[:, ru, 1, :], rkv[:, ru, 2, :]
                RKV = N_("rkv", ru)
                for r_ in range(3):
                    yield S.dma('sp', rkv[:, ru, r_, :], scrR[1 + 3 * j + r_][:, wtok], reads=[("scrR", 1 + 3 * j + r_)],
                                writes=[RKV], key=N_("rr", ru), serialize=(r_ == 0))
                yield S.dma('sp', grs[:, jj, :], scrA[20 + j][:, wtok], reads=[("scrA", 20 + j)], writes=[N_("grs", wp, jj)], key=N_("rg", jj))
                col = lambda base: pv[:, base + j:base + j + 1]
                yield S.op('pe', lambda e: e.matmul(ps[:, P0, LO], lora[0:64, j * 128:(j + 1) * 128], tw[0:64, :], start=True, stop=True),
                           reads=["lora", N_("tw")], writes=PSU(P0, 0, 256))
                yield S.op('pe', lambda e: e.matmul(ps[:, P1, LO], lora[64:128, j * 128:(j + 1) * 128], tw[64:128, :], start=True, stop=True),
                           reads=["lora", N_("tw1")], writes=PSU(P1, 0, 256))
                yield S.op('act', lambda e: e.activation(out=sw, in_=ps[:, P0, LO], func=AF.Sigmoid, bias=col(PV_W0)),
                           reads=PSU(P0, 0, 256) + ["pv"], writes=[N_("sw")])
                yield S.op('act', lambda e: e.activation(out=a_, in_=ps[:, P1, LO], func=AF.Sigmoid, bias=col(PV_A0)),
                           reads=PSU(P1, 0, 256) + ["pv"], writes=[N_("a_")])
                yield S.op('dve', lambda e: e.tensor_tensor_scan(out=cs, data0=resetm, data1=sw, initial=0.0, op0=ALU.mult, op1=ALU.add),
                           reads=[N_("sw"), "cf"], writes=[N_("cs")])
                yield S.op('act', lambda e: e.activation(out=sq, in_=kl, func=AF.Square, scale=col(PV_KK)),
                           reads=[RKV, "pv"], writes=[N_("sq")])
                yield S.op('pe', lambda e: e.matmul(ps[:, P0, HI], bones, sq, start=True, stop=True), reads=[N_("sq"), "cf"], writes=PSU(P0, 256, 512))
                yield S.op('act', lambda e: e.activation(out=W_, in_=cs, func=AF.Exp, scale=-C0), reads=[N_("cs")], writes=[N_("W_")])
                yield S.op('act', lambda e: e.activation(out=Wi, in_=cs, func=AF.Exp, scale=C0), reads=[N_("cs")], writes=[N_("Wi")])
                yield S.op('dve', lambda e: e.tensor_tensor(out=Wp, in0=cs, in1=sw, op=ALU.subtract), reads=[N_("cs"), N_("sw")], writes=[N_("Wp")])
                yield S.op('act', lambda e: e.activation(out=Wp, in_=Wp, func=AF.Exp, scale=-C0), reads=[N_("Wp")], writes=[N_("Wp")])
                cs3 = cs.rearrange("p (a b) -> p a b", a=CW)
                yield S.op('act', lambda e: e.activation(out=Wend[:, jj, :], in_=cs3[:, :, 63], func=AF.Exp, scale=-C0),
                           reads=[N_("cs")], writes=[N_("Wend", wp, jj)])
                yield S.op('dve', lambda e: e.tensor_scalar(out=sq, in0=ps[:, P0, HI], scalar1=1e-24, scalar2=None, op0=ALU.max),
                           reads=PSU(P0, 256, 512), writes=[N_("sq")])
                yield S.op('act', lambda e: e.activation(out=sq, in_=sq, func=AF.Ln), reads=[N_("sq")], writes=[N_("sq")])
                yield S.op('act', lambda e: e.activation(out=sq, in_=sq, func=AF.Exp, scale=-0.5), reads=[N_("sq")], writes=[N_("sq")])
                yield S.op('dve', lambda e: e.scalar_tensor_tensor(out=kkn, in0=kl, scalar=col(PV_KK), in1=sq, op0=ALU.mult, op1=ALU.mult),
                           reads=[RKV, "pv", N_("sq")], writes=[N_("kkn")])
                yield S.op('act', lambda e: e.activation(out=kmod, in_=a_, func=AF.Identity, scale=col(PV_KA), bias=omka[:, j:j + 1]),
                           reads=[N_("a_"), "pv", "omka"], writes=[N_("kmod")])
                yield S.op('pool', lambda e: e.tensor_tensor(out=kmod, in0=kmod, in1=kl, op=ALU.mult),
                           reads=[N_("kmod"), RKV], writes=[N_("kmod")])
                yield S.op('pool', lambda e: e.tensor_tensor(out=RT[:, jj, :], in0=rl, in1=W_, op=ALU.mult),
                           reads=[RKV, N_("W_")], writes=[N_("RT", wp, jj)])
                yield S.op('dve', lambda e: e.scalar_tensor_tensor(out=AT[:, jj, :], in0=kkn, scalar=-1.0, in1=Wp, op0=ALU.mult, op1=ALU.mult),
                           reads=[N_("kkn"), N_("Wp")], writes=[N_("AT", wp, jj)])
                yield S.op('pool', lambda e: e.tensor_tensor(out=ba, in0=kkn, in1=a_, op=ALU.mult), reads=[N_("kkn"), N_("a_")], writes=[N_("ba")])
                yield S.op('pool', lambda e: e.tensor_tensor(out=BT[:, jj, :], in0=ba, in1=Wi, op=ALU.mult), reads=[N_("ba"), N_("Wi")], writes=[N_("BT", wp, jj)])
                wend_bc = Wend[:, jj, :].unsqueeze(2).broadcast_to([128, CW, 64])
                r3 = lambda a: a.rearrange("p (a b) -> p a b", a=CW)
                yield S.op('pool', lambda e: e.tensor_tensor(out=r3(bCb), in0=r3(BT[:, jj, :]), in1=wend_bc, op=ALU.mult),
                           reads=[N_("BT", wp, jj), N_("Wend", wp, jj)], writes=[N_("bCb")])
                yield S.op('pool', lambda e: e.tensor_tensor(out=KT[:, jj, :], in0=kmod, in1=Wi, op=ALU.mult), reads=[N_("kmod"), N_("Wi")], writes=[N_("KT", wp, jj)])
                yield S.op('pool', lambda e: e.tensor_tensor(out=r3(kCb), in0=r3(KT[:, jj, :]), in1=wend_bc, op=ALU.mult),
                           reads=[N_("KT", wp, jj), N_("Wend", wp, jj)], writes=[N_("kCb")])
                yield S.op('dve', lambda e: e.scalar_tensor_tensor(out=kkr, in0=rl, scalar=col(PV_RK), in1=kmod, op0=ALU.mult, op1=ALU.mult),
                           reads=[RKV, "pv", N_("kmod")], writes=[N_("kkr")])
                yield S.op('pe', lambda e: e.matmul(ps[:, P1, HI], bones, kkr, start=True, stop=True), reads=[N_("kkr"), "cf"], writes=PSU(P1, 256, 512))
                yield S.op('dve', lambda e: e.tensor_tensor(out=BON[:, jj, :], in0=ps[:, P1, HI], in1=vl_, op=ALU.mult),
                           reads=PSU(P1, 256, 512) + [RKV], writes=[N_("BON", wp, jj)])
                if POOL_COPY:
                    yield S.op('pool', lambda e: e.tensor_copy(out=vlb, in_=vl_), reads=[RKV], writes=[N_("vlb")])
                else:
                    yield S.op('act', lambda e: e.activation(out=vlb, in_=vl_, func=AF.Copy), reads=[RKV], writes=[N_("vlb")])
                for si, (src, sname, dst, dname, bank, hh) in enumerate([(vlb, "vlb", VT, "VT", P0, 0), (bCb, "bCb", BCt, "BCt", P1, 0),
                                                                          (kCb, "kCb", KCt, "KCt", P0, 256)]):
                    fns = []
                    for c in range(CW):
                        for e_ in range(2):
                            rows = slice(64 * e_, 64 * e_ + 64)
                            fns.append(lambda e, src=src, bank=bank, rows=rows, c=c, hh=hh: e.matmul(
                                ps[rows, bank, hh + c * 64:hh + (c + 1) * 64], src[rows, c * 64:(c + 1) * 64], I64[rows, :], start=True, stop=True))
                    yield S.multi('pe', fns, reads=[N_(sname), "cb"], writes=PSU(bank, hh, hh + 256))
                    pv3 = ps[:, bank, hh:hh + 256].rearrange("p (a b) -> p a b", a=CW)
                    if si == 1:
                        yield S.op('dve', lambda e, dst=dst, pv3=pv3: e.tensor_copy(out=dst[:, :, jj, :], in_=pv3),
                                   reads=PSU(bank, hh, hh + 256), writes=[N_(dname, wp, jj)])
                    else:
                        yield S.op('act', lambda e, dst=dst, pv3=pv3: e.activation(out=dst[:, :, jj, :], in_=pv3, func=AF.Copy),
                                   reads=PSU(bank, hh, hh + 256), writes=[N_(dname, wp, jj)])
            prep_done[grp] = wn + 1

    def chain_stream(grp):
        L = lanes[grp]
        N_ = lambda n, *a: (n, grp) + a
        C0, C1 = 4 * grp + 2, 4 * grp + 3
        Yw, mixr = L.Yw, L.mixr
        NNs, PTm, G2, G3, Xb, Ub = L.NNs, L.PTm, L.G2, L.G3, L.Xb, L.Ub
        hbpar = 0
        for wn in range(NWR):
            wp = wn % 2
            grs, RT, AT, BT, KT, VT, BCt, KCt, BON, Wend = (L.grs[wp], L.RT[wp], L.AT[wp], L.BT[wp], L.KT[wp], L.VT[wp],
                                                            L.BCt[wp], L.KCt[wp], L.BON[wp], L.Wend[wp])
            allj = lambda n: [N_(n, wp, jj) for jj in range(4)]
            wtok = slice(wn * WR, (wn + 1) * WR)
            while prep_done[grp] < wn + 1:
                yield None
            for c in range(CW):
                cu = c % 2
                tokc = slice(c * 64, (c + 1) * 64)
                fa, fb, fc = [], [], []
                for jj in range(4):
                    for e_ in range(2):
                        rows = slice(64 * e_, 64 * e_ + 64)
                        cs_ = slice(jj * 64, (jj + 1) * 64)
                        cs2 = slice(256 + jj * 64, 256 + (jj + 1) * 64)
                        fa.append(lambda e, rows=rows, cs_=cs_, jj=jj: e.matmul(ps[rows, C0, cs_], BT[rows, jj, tokc], AT[rows, jj, tokc], start=True, stop=True))
                        fa.append(lambda e, rows=rows, cs2=cs2, jj=jj: e.matmul(ps[rows, C0, cs2], AT[rows, jj, tokc], BT[rows, jj, tokc], start=True, stop=True))
                        fb.append(lambda e, rows=rows, cs_=cs_, jj=jj: e.matmul(ps[rows, C1, cs_], KT[rows, jj, tokc], AT[rows, jj, tokc], start=True, stop=True))
                        fb.append(lambda e, rows=rows, cs2=cs2, jj=jj: e.matmul(ps[rows, C1, cs2], BT[rows, jj, tokc], RT[rows, jj, tokc], start=True, stop=True))
                        fc.append(lambda e, rows=rows, cs_=cs_, jj=jj: e.matmul(ps[rows, C0, cs_], KT[rows, jj, tokc], RT[rows, jj, tokc], start=True, stop=True))
                yield S.multi('pe', fa, reads=allj("BT") + allj("AT"), writes=PSU(C0, 0, 512))
                yield S.multi('pe', fb, reads=allj("BT") + allj("AT") + allj("KT") + allj("RT"), writes=PSU(C1, 0, 512))
                N0 = NNs[0]
                yield S.op('dve', lambda e: e.tensor_tensor(out=v4(N0[:, 0, :]), in0=v4(ps[:, C0, LO]), in1=bc4(mST), op=ALU.mult),
                           reads=PSU(C0, 0, 256) + ["cb"], writes=[N_("NN", 0)])
                yield S.op('dve', lambda e: e.tensor_tensor(out=v4(N0[:, 1, :]), in0=v4(ps[:, C0, HI]), in1=bc4(mS), op=ALU.mult),
                           reads=PSU(C0, 256, 512) + ["cb"], writes=[N_("NN", 0)])
                yield S.multi('pe', fc, reads=allj("KT") + allj("RT"), writes=PSU(C0, 0, 256))
                yield S.op('pool', lambda e: e.tensor_tensor(out=v4(PTm[0]), in0=v4(N0[:, 0, :]), in1=bc4(I64), op=ALU.add),
                           reads=[N_("NN", 0), "cb"], writes=[N_("PTm", 0)])
                yield S.op('dve', lambda e: e.tensor_tensor(out=v4(G2[cu][:, 0, :]), in0=v4(ps[:, C1, LO]), in1=bc4(mST), op=ALU.mult),
                           reads=PSU(C1, 0, 256) + ["cb"], writes=[N_("G2", cu)])
                yield S.op('dve', lambda e: e.tensor_tensor(out=v4(G2[cu][:, 1, :]), in0=v4(ps[:, C1, HI]), in1=bc4(mIT), op=ALU.mult),
                           reads=PSU(C1, 256, 512) + ["cb"], writes=[N_("G2", cu)])
                yield S.op('dve', lambda e: e.tensor_tensor(out=v4(G3[cu]), in0=v4(ps[:, C0, LO]), in1=bc4(mIT), op=ALU.mult),
                           reads=PSU(C0, 0, 256) + ["cb"], writes=[N_("G3", cu)])
                for l in range(5):
                    a0_, a1_ = NNs[l % 2], NNs[(l + 1) % 2]
                    fns = []
                    for jj in range(4):
                        for e_ in range(2):
                            rows = slice(64 * e_, 64 * e_ + 64)
                            cs_ = slice(jj * 64, (jj + 1) * 64)
                            cs2 = slice(256 + jj * 64, 256 + (jj + 1) * 64)
                            fns.append(lambda e, rows=rows, cs_=cs_, cs2=cs2, a0_=a0_: e.matmul(
                                ps[rows, C1, cs2], a0_[rows, 0, cs_], a0_[rows, 1, cs_], start=True, stop=True))
                            fns.append(lambda e, rows=rows, cs_=cs_, a0_=a0_: e.matmul(
                                ps[rows, C1, cs_], a0_[rows, 1, cs_], a0_[rows, 0, cs_], start=True, stop=True))
                    yield S.multi('pe', fns, reads=[N_("NN", l % 2)], writes=PSU(C1, 0, 512))
                    if l % 2 == 0 or not DVE_EVAC:
                        yield S.op('act', lambda e, a1_=a1_: e.activation(out=a1_.rearrange("p a b -> p (a b)"), in_=ps[:, C1, :], func=AF.Copy),
                                   reads=PSU(C1, 0, 512), writes=[N_("NN", (l + 1) % 2)])
                    else:
                        yield S.op('dve', lambda e, a1_=a1_: e.tensor_copy(out=a1_.rearrange("p a b -> p (a b)"), in_=ps[:, C1, :]),
                                   reads=PSU(C1, 0, 512), writes=[N_("NN", (l + 1) % 2)])
                    fns = []
                    for jj in range(4):
                        for e_ in range(2):
                            rows = slice(64 * e_, 64 * e_ + 64)
                            cs_ = slice(jj * 64, (jj + 1) * 64)
                            cs2 = slice(256 + jj * 64, 256 + (jj + 1) * 64)
                            fns.append(lambda e, rows=rows, cs_=cs_, cs2=cs2, a1_=a1_, l=l: e.matmul(
                                ps[rows, C0, cs2], a1_[rows, 1, cs_], PTm[l % 2][rows, cs_], start=True, stop=True))
                    yield S.multi('pe', fns, reads=[N_("NN", (l + 1) % 2), N_("PTm", l % 2)], writes=PSU(C0, 256, 512))
                    yield S.op('dve', lambda e, l=l: e.tensor_tensor(out=PTm[(l + 1) % 2], in0=ps[:, C0, HI], in1=PTm[l % 2], op=ALU.add),
                               reads=PSU(C0, 256, 512) + [N_("PTm", l % 2)], writes=[N_("PTm", (l + 1) % 2)])
                TT = PTm[1]
                hb = Hb[hbpar]
                hbn = Hb[1 - hbpar]
                fns = []
                for jj in range(4):
                    j = grp * 4 + jj
                    for e_ in range(2):
                        rows = slice(64 * e_, 64 * e_ + 64)
                        cs_ = slice(jj * 64, (jj + 1) * 64)
                        fns.append(lambda e, rows=rows, cs_=cs_, jj=jj, j=j: e.matmul(ps[rows, C1, cs_], AT[rows, jj, tokc], hb[rows, j, :], start=True, stop=False))
                        fns.append(lambda e, rows=rows, cs_=cs_, jj=jj: e.matmul(ps[rows, C1, cs_], G2[cu][rows, 0, cs_], VT[rows, c, jj, :], start=False, stop=True))
                yield S.multi('pe', fns, reads=allj("AT") + [("Hb", hbpar, grp), N_("G2", cu)] + allj("VT"), writes=PSU(C1, 0, 256))
                yield S.op('act', lambda e: e.activation(out=Xb, in_=ps[:, C1, LO], func=AF.Copy), reads=PSU(C1, 0, 256), writes=[N_("Xb")])
                fns = []
                for jj in range(4):
                    for e_ in range(2):
                        rows = slice(64 * e_, 64 * e_ + 64)
                        cs_ = slice(jj * 64, (jj + 1) * 64)
                        cs2 = slice(256 + jj * 64, 256 + (jj + 1) * 64)
                        fns.append(lambda e, rows=rows, cs_=cs_, cs2=cs2: e.matmul(ps[rows, C1, cs2], TT[rows, cs_], Xb[rows, cs_], start=True, stop=True))
                yield S.multi('pe', fns, reads=[N_("PTm", 1), N_("Xb")], writes=PSU(C1, 256, 512))
                yield S.op('act', lambda e: e.activation(out=Ub, in_=ps[:, C1, HI], func=AF.Copy), reads=PSU(C1, 256, 512), writes=[N_("Ub")])
                fns = []
                for jj in range(4):
                    j = grp * 4 + jj
                    for e_ in range(2):
                        rows = slice(64 * e_, 64 * e_ + 64)
                        cs_ = slice(jj * 64, (jj + 1) * 64)
                        cs2 = slice(256 + jj * 64, 256 + (jj + 1) * 64)
                        fns.append(lambda e, rows=rows, cs2=cs2, jj=jj, j=j: e.matmul(ps[rows, C0, cs2], hb[rows, j, :], RT[rows, jj, tokc], start=True, stop=False))
                        fns.append(lambda e, rows=rows, cs_=cs_, cs2=cs2: e.matmul(ps[rows, C0, cs2], Ub[rows, cs_], G2[cu][rows, 1, cs_], start=False, stop=False))
                        fns.append(lambda e, rows=rows, cs_=cs_, cs2=cs2, jj=jj: e.matmul(ps[rows, C0, cs2], VT[rows, c, jj, :], G3[cu][rows, cs_], start=False, stop=True))
                        fns.append(lambda e, rows=rows, cs_=cs_, jj=jj: e.matmul(ps[rows, C0, cs_], BCt[rows, c, jj, :], Ub[rows, cs_], start=True, stop=False))
                        fns.append(lambda e, rows=rows, cs_=cs_, jj=jj: e.matmul(ps[rows, C0, cs_], KCt[rows, c, jj, :], VT[rows, c, jj, :], start=False, stop=True))
                yield S.multi('pe', fns, reads=[("Hb", hbpar, grp), N_("Ub"), N_("G2", cu), N_("G3", cu)] + allj("RT") + allj("VT") + allj("BCt") + allj("KCt"),
                              writes=PSU(C0, 0, 512))
                H4 = Hst[:, grp * 4:(grp + 1) * 4, :]
                yield S.op('dve', lambda e: e.tensor_tensor(out=H4, in0=H4, in1=Wend[:, :, c:c + 1].broadcast_to([128, 4, 64]), op=ALU.mult),
                           reads=[N_("H")] + allj("Wend"), writes=[N_("H")])
                yield S.op('dve', lambda e: e.tensor_tensor(out=H4, in0=v4(ps[:, C0, LO]), in1=H4, op=ALU.add),
                           reads=PSU(C0, 0, 256) + [N_("H")], writes=[N_("H")])
                if POOL_COPY:
                    yield S.op('pool', lambda e: e.tensor_copy(out=hbn[:, grp * 4:(grp + 1) * 4, :], in_=H4),
                               reads=[N_("H")], writes=[("Hb", 1 - hbpar, grp)])
                else:
                    yield S.op('act', lambda e: e.activation(out=hbn[:, grp * 4:(grp + 1) * 4, :], in_=H4, func=AF.Copy),
                               reads=[N_("H")], writes=[("Hb", 1 - hbpar, grp)])
                yield S.op('act', lambda e: e.activation(out=Yw[:, :, tokc], in_=v4(ps[:, C0, HI]), func=AF.Copy),
                           reads=PSU(C0, 256, 512), writes=[N_("Yw")])
                hbpar = 1 - hbpar
            for jj in range(4):
                j = grp * 4 + jj
                col = lambda base, j=j: pv[:, base + j:base + j + 1]
                yc, sq_, yn = L.ptmp
                yield S.op('pe', lambda e: e.matmul(ps[:, C1, LO], bones64, Yw[:, jj, :], start=True, stop=True), reads=[N_("Yw"), "cf"], writes=PSU(C1, 0, 256))
                yield S.op('dve', lambda e: e.tensor_tensor(out=yc, in0=Yw[:, jj, :], in1=ps[:, C1, LO], op=ALU.subtract),
                           reads=[N_("Yw")] + PSU(C1, 0, 256), writes=[N_("yc")])
                yield S.op('pool', lambda e: e.tensor_tensor(out=sq_, in0=yc, in1=yc, op=ALU.mult), reads=[N_("yc")], writes=[N_("sq_")])
                yield S.op('pe', lambda e: e.matmul(ps[:, C1, HI], bones64, sq_, start=True, stop=True), reads=[N_("sq_"), "cf"], writes=PSU(C1, 256, 512))
                yield S.op('act', lambda e: e.activation(out=sq_, in_=ps[:, C1, HI], func=AF.Ln, bias=epsc[:, 1:2]),
                           reads=PSU(C1, 256, 512) + ["epsc"], writes=[N_("sq_")])
                yield S.op('act', lambda e: e.activation(out=sq_, in_=sq_, func=AF.Exp, scale=-0.5), reads=[N_("sq_")], writes=[N_("sq_")])
                yield S.op('pool', lambda e: e.tensor_tensor(out=yn, in0=yc, in1=sq_, op=ALU.mult), reads=[N_("yc"), N_("sq_")], writes=[N_("yn")])
                yield S.op('act', lambda e: e.activation(out=yn, in_=yn, func=AF.Identity, bias=col(PV_LNB), scale=col(PV_LNW)),
                           reads=[N_("yn"), "pv"], writes=[N_("yn")])
                yield S.op('pool', lambda e: e.tensor_tensor(out=yn, in0=yn, in1=BON[:, jj, :], op=ALU.add), reads=[N_("yn"), N_("BON", wp, jj)], writes=[N_("yn")])
                yield S.op('pool', lambda e: e.tensor_tensor(out=mixr[:, jj, :], in0=yn, in1=grs[:, jj, :], op=ALU.mult),
                           reads=[N_("yn"), N_("grs", wp, jj)], writes=[N_("mixr", jj)])
                yield S.dma('pool', scrM[8 + j][:, wtok], mixr[:, jj, :], reads=[N_("mixr", jj)], writes=[("scrM", 8 + j, wn)], key=N_("rm", jj))
            chain_done[grp] = wn + 1

    prep_done = [0, 0]
    chain_done = [0, 0]
    gens = [prep_stream(0), prep_stream(1), chain_stream(0), chain_stream(1)]
    while gens:
        for gen in list(gens):
            try:
                next(gen)
            except StopIteration:
                gens.remove(gen)
    S.barrier()

    o = 0
    wo, o = carve(o, [128, 16, 2048], BF16)
    mt, o = carve(o, [128, 2, 16, 128], BF16)
    xt2, o = carve(o, [128, 2, D], F32)
    ot, o = carve(o, [128, 2, D], F32)
    assert o <= ARENA_W, o
    for nb in range(4):
        S.dma('pool', wo[:, :, nb * 512:(nb + 1) * 512], wout_d[nb], writes=[("wo", nb)], key=("wo", nb))
    allM = []
    def o_load(tt):
        u = tt % 2
        tsl = slice(tt * 128, (tt + 1) * 128)
        S.dma('sp', mt[:, u], scrM[:, :, tsl].rearrange("c p t -> p c t"), reads=allM, writes=[("mt", u)], key=("om", u))
        S.dma('sp', xt2[:, u, :], x_d[tsl, :], writes=[("xt2", u)], key=("ox", u))

    o_load(0)
    for tt in range(NT):
        u = tt % 2
        tsl = slice(tt * 128, (tt + 1) * 128)
        if tt + 1 < NT:
            o_load(tt + 1)
        for nb in range(4):
            bank = 4 * u + nb
            fns = [lambda e, kc=kc, bank=bank, nb=nb: e.matmul(ps[:, bank, :], mt[:, u, kc, :], wo[:, kc, nb * 512:(nb + 1) * 512],
                                                               start=(kc == 0), stop=(kc == 15)) for kc in range(16)]
            S.multi('pe', fns, reads=[("mt", u), ("wo", nb)], writes=PSU(bank, 0, 512))
        pso = ps[:, 4 * u:4 * u + 4, :]
        allb = [r for b in range(4 * u, 4 * u + 4) for r in PSU(b, 0, 512)]
        S.op('act', lambda e: e.activation(out=ot[:, u, :].rearrange("p (a b) -> p a b", a=4), in_=pso, func=AF.Square,
                                           scale=float(D) ** -0.5, accum_out=small[:, 32 + tt:33 + tt]),
             reads=allb, writes=[("ot", u), ("ss2", tt)])
        S.op('act', lambda e: e.activation(out=small[:, 48 + tt:49 + tt], in_=small[:, 32 + tt:33 + tt], func=AF.Sqrt, bias=epsc[:, 0:1]),
             reads=[("ss2", tt), "epsc"], writes=[("rs2", tt)])
        S.op('dve', lambda e: e.reciprocal(out=small[:, 48 + tt:49 + tt], in_=small[:, 48 + tt:49 + tt]),
             reads=[("rs2", tt)], writes=[("rs2", tt)])
        S.op('dve', lambda e: e.scalar_tensor_tensor(out=ot[:, u, :].rearrange("p (a b) -> p a b", a=4), in0=pso,
                                                     scalar=small[:, 48 + tt:49 + tt], in1=gpb[:].rearrange("p (a b) -> p a b", a=4),
                                                     op0=ALU.mult, op1=ALU.mult),
             reads=allb + [("rs2", tt), "gpb"], writes=[("ot", u)])
        S.op('pool', lambda e: e.tensor_tensor(out=ot[:, u, :], in0=ot[:, u, :], in1=xt2[:, u, :], op=ALU.add),
             reads=[("ot", u), ("xt2", u)], writes=[("ot", u)])
        S.dma('pool', out_d[tsl, :], ot[:, u, :], reads=[("ot", u)], writes=[("out", tt)], key=("oo", u))
    S.barrier()
    es.close()
    return nc, S


_CACHE = {}


def _host_consts():
    cbm = np.zeros((128, CB_N), np.float32)
    cbm[:, CB_IDENT:CB_IDENT + 128] = np.eye(128, dtype=np.float32)
    s = np.arange(128)[:, None]
    q = np.arange(128)[None, :]
    prev = (s > q).astype(np.float32)
    cur = (s <= q).astype(np.float32)
    cbm[:, CB_MASKA:CB_MASKA + 128] = prev
    cbm[:, CB_MASKA + 128:CB_MASKA + 256] = cur
    cbm[:, CB_MASKA0 + 128:CB_MASKA0 + 256] = cur
    p = (np.arange(128) % 64)[:, None]
    f = np.arange(64)[None, :]
    cbm[:, CB_MST:CB_MST + 64] = (p < f)
    cbm[:, CB_MS:CB_MS + 64] = (f < p)
    cbm[:, CB_MIT:CB_MIT + 64] = (p <= f)
    cbm[:, CB_I64:CB_I64 + 64] = (p == f)
    cbm[:, CB_OL0:CB_OL0 + 64] = 1.0
    cbm[:, CB_OL1 + 64:CB_OL1 + 128] = 1.0
    cfm = np.zeros((128, CF_N), np.float32)
    blk = (np.arange(128)[:, None] // 64 == np.arange(128)[None, :] // 64).astype(np.float32)
    cfm[:, CF_BONES:CF_BONES + 128] = blk
    cfm[:, CF_BONES64:CF_BONES64 + 128] = blk / 64.0
    rm = np.ones(512, np.float32)
    rm[::64] = 0.0
    cfm[:, CF_RESET:CF_RESET + 512] = rm[None, :]
    cfm[:, CF_ONES:CF_ONES + 128] = 1.0
    return cbm, cfm


def _colidx():
    idx = []
    idx += [np.arange(1024, 1152), np.arange(1152, 1280), np.arange(1024, 1152), np.arange(1152, 1280)]
    idx.append(np.arange(1280, 1536))
    idx.append(np.arange(0, 1024))
    idx.append(np.arange(1536, 2560))
    R0 = 2560
    idx.append(np.arange(R0 + 3072, R0 + 3200))
    for j in range(8):
        for base in (0, 1024, 2048, 3200):
            idx.append(np.arange(R0 + base + j * 128, R0 + base + (j + 1) * 128))
    idx = np.concatenate(idx)
    assert idx.size == NBLK * 128
    return idx


def kernel(x, c, w_ada, b_ada, pre_norm_g, post_norm_g, w_in, w_out, attn_sinks,
           rwkv_mu, rwkv_w0, rwkv_w_up, rwkv_a0, rwkv_a_up, rwkv_k_k, rwkv_k_a,
           rwkv_r_k, rwkv_ln_w, rwkv_ln_b):
    f = lambda a: np.ascontiguousarray(np.asarray(a, dtype=np.float32))
    x = f(x); c = f(c)
    B = x.shape[0]
    if 'nc' not in _CACHE:
        _CACHE['nc'] = build_program()
    nc, S = _CACHE['nc']
    wada_h = f(np.asarray(w_ada[0]).reshape(16, 128, 12, 512).transpose(2, 1, 0, 3))
    win_h = f(np.asarray(w_in[0])[:, _colidx()].reshape(16, 128, NBLK, 128).transpose(2, 1, 0, 3))
    wout_h = f(np.asarray(w_out[0]).reshape(16, 128, 4, 512).transpose(2, 1, 0, 3))
    col = lambda v: np.asarray(v, np.float32).reshape(-1, 128).T
    pvec = np.zeros((128, PV_N), np.float32)
    pvec[:, PV_PREG:PV_PREG + 16] = col(pre_norm_g[0])
    mu = np.asarray(rwkv_mu[0], np.float32)
    pvec[:, PV_MU] = mu[3072:3200]
    for j in range(8):
        for r_ in range(3):
            pvec[:, PV_MU + 1 + 3 * j + r_] = mu[r_ * 1024 + j * 128:r_ * 1024 + (j + 1) * 128]
    pvec[:, PV_W0:PV_W0 + 8] = col(rwkv_w0[0])
    pvec[:, PV_A0:PV_A0 + 8] = col(rwkv_a0[0])
    pvec[:, PV_KK:PV_KK + 8] = col(rwkv_k_k[0])
    pvec[:, PV_KA:PV_KA + 8] = col(rwkv_k_a[0])
    pvec[:, PV_RK:PV_RK + 8] = col(np.asarray(rwkv_r_k[0]).reshape(-1))
    pvec[:, PV_LNW:PV_LNW + 8] = col(rwkv_ln_w[0])
    pvec[:, PV_LNB:PV_LNB + 8] = col(rwkv_ln_b[0])
    lora_h = f(np.concatenate([np.asarray(rwkv_w_up[0]), np.asarray(rwkv_a_up[0])], axis=0))
    sinks_h = f(np.repeat(np.asarray(attn_sinks[0], np.float32), 64)[None, :])
    sinkc_h = f(np.asarray(attn_sinks[0], np.float32).reshape(8, 2)[:, np.arange(128) // 64].T)
    cbm, cfm = _host_consts()
    bada_h = f(np.asarray(b_ada[0])[None, :])
    postg_h = f(np.asarray(post_norm_g[0])[None, :])
    in_maps = []
    for b in range(B):
        in_maps.append({
            "x": x[b], "cvec": f(c[b].reshape(16, 128).T), "wada": wada_h, "bada": bada_h, "win": win_h,
            "wout": wout_h, "pvec": pvec, "lora": lora_h, "postg": postg_h, "sinksrep": sinks_h, "sinkcol": sinkc_h,
            "constb": cbm, "constf": cfm,
        })
    res = run_bass_kernel_spmd(nc, in_maps, core_ids=list(range(B)))
    return np.stack([np.asarray(r["out"], dtype=np.float32) for r in res.results], axis=0)
```

```python
import numpy as np
import concourse.bass as bass
import concourse.mybir as mybir
from concourse.bass_utils import run_bass_kernel_spmd
from contextlib import ExitStack

F32 = mybir.dt.float32
BF16 = mybir.dt.bfloat16
AF = mybir.ActivationFunctionType
ALU = mybir.AluOpType

T = 2048
D = 2048
NT = 16
RMS_EPS = 1e-6
GN_EPS = 64e-5
C0 = float(np.exp(-0.5))
NBLK = 55
DEBUG = False
POOL_COPY = False
DVE_EVAC = True

PV_PREG = 0
PV_MU = 16
PV_W0 = 41
PV_A0 = 49
PV_KK = 57
PV_KA = 65
PV_RK = 73
PV_LNW = 81
PV_LNB = 89
PV_N = 97

CB_IDENT = 0
CB_MASKA = 128
CB_MASKA0 = 384
CB_MST = 640
CB_MS = 704
CB_MIT = 768
CB_I64 = 832
CB_OL0 = 896
CB_OL1 = 1024
CB_N = 1152
CF_BONES = 0
CF_BONES64 = 128
CF_RESET = 256
CF_ONES = 768
CF_N = 896


class Sched:
    EPOCH = 12000
    RECYCLE = False

    def __init__(self, nc, es):
        self.nc = nc
        self.es = es
        self.engs = {'pe': nc.tensor, 'act': nc.scalar, 'dve': nc.vector, 'pool': nc.gpsimd, 'sp': nc.sync}
        self.sems = {e: [] for e in self.engs}
        self.cnt = {e: 0 for e in self.engs}
        self.waited = {}
        self.lastw = {}
        self.readers = {}
        self.dsem = {}
        self.dpool = []
        self.nsem = 0
        self.ninstr = 0

    def _newsem(self, name):
        self.nsem += 1
        return self.es.enter_context(self.nc.semaphore(name))

    def _cursem(self, e):
        if not self.sems[e] or self.cnt[e] >= self.EPOCH:
            self.sems[e].append(self._newsem("s_%s_%d" % (e, len(self.sems[e]))))
            self.cnt[e] = 0
        return len(self.sems[e]) - 1

    def _wait(self, e, tok):
        if tok is None:
            return
        if tok[0] == 'e':
            _, te, ep, c = tok
            if te == e and e == 'pe':
                return
            key = (e, 'e', te, ep)
            if self.waited.get(key, 0) >= c:
                return
            self.waited[key] = c
            self.engs[e].wait_ge(self.sems[te][ep], c)
        else:
            _, k, c = tok
            key = (e, 'd', k)
            if self.waited.get(key, 0) >= c:
                return
            self.waited[key] = c
            self.engs[e].wait_ge(self.dsem[k][0], c)

    def _deps(self, e, reads, writes):
        for r in reads:
            self._wait(e, self.lastw.get(r))
        for w in writes:
            self._wait(e, self.lastw.get(w))
            for t in self.readers.get(w, {}).values():
                self._wait(e, t)

    def _commit(self, tok, reads, writes):
        for w in writes:
            self.lastw[w] = tok
            self.readers[w] = {}
        for r in reads:
            if r in writes:
                continue
            d = self.readers.setdefault(r, {})
            k = (tok[1],) if tok[0] == 'e' else ('d', tok[1])
            d[k] = tok

    @staticmethod
    def _pe_writes(e, writes):
        if e != 'pe':
            return writes
        out = []
        seen = set()
        for w in writes:
            if isinstance(w, tuple) and len(w) == 3 and w[0] == "ps":
                if w[1] not in seen:
                    seen.add(w[1])
                    out += [("ps", w[1], s_) for s_ in range(8)]
            else:
                out.append(w)
        return out

    def op(self, e, fn, reads=(), writes=()):
        writes = self._pe_writes(e, writes)
        self._deps(e, reads, writes)
        ep = self._cursem(e)
        ins = fn(self.engs[e])
        ins.then_inc(self.sems[e][ep], 1)
        self.cnt[e] += 1
        self.ninstr += 1
        tok = ('e', e, ep, self.cnt[e])
        self._commit(tok, reads, writes)
        return tok

    def multi(self, e, fns, reads=(), writes=()):
        writes = self._pe_writes(e, writes)
        self._deps(e, reads, writes)
        ep = self._cursem(e)
        ins = None
        for fn in fns:
            ins = fn(self.engs[e])
            self.ninstr += 1
        ins.then_inc(self.sems[e][ep], 1)
        self.cnt[e] += 1
        tok = ('e', e, ep, self.cnt[e])
        self._commit(tok, reads, writes)
        return tok

    def dma(self, q, out, in_, reads=(), writes=(), key=None, serialize=True):
        if key not in self.dsem:
            if self.dpool:
                self.dsem[key] = self.dpool.pop()
            else:
                self.dsem[key] = [self._newsem("d%d" % self.nsem), 0]
        if serialize and self.dsem[key][1] > 0:
            self._wait(q, ('d', key, self.dsem[key][1]))
        self._deps(q, reads, writes)
        ins = self.engs[q].dma_start(out=out, in_=in_)
        ins.then_inc(self.dsem[key][0], 16)
        self.dsem[key][1] += 16
        self.ninstr += 1
        tok = ('d', key, self.dsem[key][1])
        self._commit(tok, reads, writes)
        return tok

    def barrier(self):
        last = {}
        for e in self.engs:
            if self.sems[e]:
                last[e] = ('e', e, len(self.sems[e]) - 1, self.cnt[e])
        for e in self.engs:
            for te, tok in last.items():
                if te != e and tok[3] > 0:
                    self._wait(e, tok)
            for k, (s, c) in self.dsem.items():
                if c > 0:
                    self._wait(e, ('d', k, c))
        self.lastw = {}
        self.readers = {}
        if self.RECYCLE:
            for k in list(self.dsem.keys()):
                self.dpool.append(self.dsem.pop(k))
                self.waited = {w: v for w, v in self.waited.items() if not (w[1] == 'd' and w[2] == k)}


def build_program():
    nc = bass.Bass("TRN2", target_bir_lowering=False)
    dt = nc.dram_tensor
    x_d = dt("x", [T, D], F32, kind="ExternalInput").ap()
    cvec_d = dt("cvec", [128, 16], F32, kind="ExternalInput").ap()
    wada_d = dt("wada", [12, 128, 16, 512], F32, kind="ExternalInput").ap()
    bada_d = dt("bada", [1, 6144], F32, kind="ExternalInput").ap()
    win_d = dt("win", [NBLK, 128, 16, 128], F32, kind="ExternalInput").ap()
    wout_d = dt("wout", [4, 128, 16, 512], F32, kind="ExternalInput").ap()
    pvec_d = dt("pvec", [128, PV_N], F32, kind="ExternalInput").ap()
    lora_d = dt("lora", [128, 1024], F32, kind="ExternalInput").ap()
    postg_d = dt("postg", [1, 2048], F32, kind="ExternalInput").ap()
    sinks_d = dt("sinksrep", [1, 1024], F32, kind="ExternalInput").ap()
    sinkc_d = dt("sinkcol", [128, 8], F32, kind="ExternalInput").ap()
    cb_d = dt("constb", [128, CB_N], F32, kind="ExternalInput").ap()
    cf_d = dt("constf", [128, CF_N], F32, kind="ExternalInput").ap()
    out_d = dt("out", [T, D], F32, kind="ExternalOutput").ap()
    scrA = dt("scrA", [28, 128, T], BF16, kind="Internal").ap()
    scrR = dt("scrR", [25, 128, T], F32, kind="Internal").ap()
    scrM = dt("scrM", [16, 128, T], BF16, kind="Internal").ap()

    es = ExitStack()
    S = Sched(nc, es)
    sb = lambda name, shape, dtype: es.enter_context(nc.sbuf_tensor(name, shape, dtype))

    cb = sb("cb", [128, CB_N], BF16)
    cf = sb("cf", [128, CF_N], F32)
    pv = sb("pv", [128, PV_N], F32)
    omm = sb("omm", [128, 25], F32)
    omka = sb("omka", [128, 8], F32)
    epsc = sb("epsc", [128, 2], F32)
    lora = sb("lora_sb", [128, 1024], BF16)
    gpb = sb("gpb", [128, 2048], F32)
    gs = sb("gs", [128, 16], F32)
    shiftc = sb("shiftc", [128, 16], F32)
    sinkl = sb("sinkl", [1, 1024], F32)
    sinkc = sb("sinkc", [128, 8], F32)
    Hst = sb("Hst", [128, 8, 64], F32)
    small = sb("small", [128, 64], F32)
    one11 = cf[0:1, CF_ONES:CF_ONES + 1]
    ones_row = cf[0:1, CF_ONES:CF_ONES + 128]

    ARENA_W = 42400
    VP_OFF = 27700
    arena = sb("arena", [128, ARENA_W], F32)
    ps = es.enter_context(nc.psum_tensor("ps", [128, 8, 512], F32))

    def carve(off, shape, dtype):
        n = int(np.prod(shape[1:]))
        if dtype == BF16:
            assert n % 2 == 0
            a = arena[:, off:off + n // 2].bitcast(BF16)
            w = n // 2
        else:
            a = arena[:, off:off + n]
            w = n
        if len(shape) == 3:
            a = a.rearrange("p (a b) -> p a b", a=shape[1])
        elif len(shape) == 4:
            a = a.rearrange("p (a b c) -> p a b c", a=shape[1], b=shape[2])
        return a, off + w

    def PSU(bank, lo, hi):
        return [("ps", bank, s) for s in range(lo // 64, (hi - 1) // 64 + 1)]

    Vp, _vpend = carve(VP_OFF, [128, 17, 4, 192], BF16)
    assert _vpend <= ARENA_W

    S.dma('pool', cb[:], cb_d, writes=["cb"], key="c0")
    S.dma('sp', cf[:], cf_d, writes=["cf"], key="c1")
    S.dma('sp', pv[:], pvec_d, writes=["pv"], key="c2")
    S.dma('pool', lora[:], lora_d, writes=["lora"], key="c3")
    S.dma('sp', sinkl[:], sinks_d, writes=["sinkl"], key="c4")
    S.op('act', lambda e: e.activation(out=sinkl[:], in_=sinkl[:], func=AF.Exp), reads=["sinkl"], writes=["sinkl"])
    S.dma('sp', sinkc[:], sinkc_d, writes=["sinkc"], key="c6")
    S.op('act', lambda e: e.activation(out=sinkc[:], in_=sinkc[:], func=AF.Exp), reads=["sinkc"], writes=["sinkc"])
    S.op('dve', lambda e: e.tensor_scalar(out=omm[:], in0=pv[:, PV_MU:PV_MU + 25], scalar1=-1.0, scalar2=1.0,
                                          op0=ALU.mult, op1=ALU.add), reads=["pv"], writes=["omm"])
    S.op('dve', lambda e: e.tensor_scalar(out=omka[:], in0=pv[:, PV_KA:PV_KA + 8], scalar1=-1.0, scalar2=1.0,
                                          op0=ALU.mult, op1=ALU.add), reads=["pv"], writes=["omka"])
    S.op('pool', lambda e: e.memset(epsc[:, 0:1], RMS_EPS), writes=["epsc"])
    S.op('pool', lambda e: e.memset(epsc[:, 1:2], GN_EPS), writes=["epsc"])
    S.op('pool', lambda e: e.memset(Hst[:].rearrange("p a b -> p (a b)"), 0.0), writes=["H"])

    ident = cb[:, CB_IDENT:CB_IDENT + 128]

    o = 30000
    wa, o = carve(o, [128, 2, 16, 512], BF16)
    csb, o = carve(o, [128, 16], F32)
    scb, o = carve(o, [128, 16], BF16)
    brow, o = carve(o, [128, 2, 512], F32)
    mrow, o = carve(o, [128, 2, 512], F32)
    pgrow, o = carve(o, [128, 2, 512], F32)
    S.dma('sp', csb, cvec_d, writes=["csb"], key="c5")
    S.op('act', lambda e: e.activation(out=scb, in_=csb, func=AF.Silu), reads=["csb"], writes=["scb"])
    assert o <= ARENA_W, o

    def phase0():
      for nb in range(12):
          u = nb % 2
          S.dma('pool', wa[:, u], wada_d[nb], writes=[("wa", u)], key=("wa", u))
          S.dma('pool', brow[0:1, u, :], bada_d[0:1, nb * 512:(nb + 1) * 512], writes=[("brow", u)], key=("br", u))
          fns = []
          for kc in range(16):
              fns.append(lambda e, kc=kc: e.matmul(ps[0:1, u, :], scb[:, kc:kc + 1], wa[:, u, kc, :],
                                                   start=(kc == 0), stop=(kc == 15)))
          S.multi('pe', fns, reads=[("wa", u), "scb"], writes=PSU(u, 0, 512))
          S.op('dve', lambda e: e.tensor_tensor(out=mrow[0:1, u, :], in0=ps[0:1, u, :], in1=brow[0:1, u, :], op=ALU.add),
               reads=PSU(u, 0, 512) + [("brow", u)], writes=[("mrow", u)])
          if nb < 8:
              fns = []
              for i in range(4):
                  col = nb * 4 + i
                  fns.append(lambda e, i=i, col=col: e.matmul(ps[:, 2, col:col + 1], mrow[0:1, u, i * 128:(i + 1) * 128],
                                                              one11, start=True, stop=True))
              S.multi('pe', fns, reads=[("mrow", u), "cf"], writes=PSU(2, 0, 64))
          else:
              gb = nb - 8
              S.dma('pool', pgrow[0:1, u, :], postg_d[0:1, gb * 512:(gb + 1) * 512], writes=[("pgrow", u)], key=("pg", u))
              S.op('dve', lambda e: e.tensor_tensor(out=mrow[0:1, u, :], in0=mrow[0:1, u, :], in1=pgrow[0:1, u, :], op=ALU.mult),
                   reads=[("mrow", u), ("pgrow", u)], writes=[("mrow", u)])
              S.op('pe', lambda e: e.matmul(ps[:, 3, :], ones_row, mrow[0:1, u, :], start=True, stop=True),
                   reads=[("mrow", u), "cf"], writes=PSU(3, 0, 512))
              S.op('act', lambda e: e.activation(out=gpb[:, gb * 512:(gb + 1) * 512], in_=ps[:, 3, :], func=AF.Copy),
                   reads=PSU(3, 0, 512), writes=["gpb"])
          if nb == 7:
              S.op('dve', lambda e: e.tensor_copy(out=shiftc[:], in_=ps[:, 2, 0:16]), reads=PSU(2, 0, 64), writes=["shiftc"])
              S.op('dve', lambda e: e.scalar_tensor_tensor(out=gs[:], in0=ps[:, 2, 16:32], scalar=1.0, in1=pv[:, PV_PREG:PV_PREG + 16],
                                                           op0=ALU.add, op1=ALU.mult), reads=PSU(2, 0, 64) + ["pv"], writes=["gs"])
          yield

    o = 0
    hT, o = carve(o, [128, 16, T], BF16)
    o_after_hT = o
    xt, o = carve(o, [128, 2, D], F32)
    NG = 2
    xn, o = carve(o, [128, 4, NG, D], BF16)
    junk, o = carve(o, [128, D], BF16)
    assert o <= 30000, o

    def n_tile(tt):
        q, t4 = divmod(tt, NG)
        slot = q % 4
        u = tt % 2
        S.dma('sp', xt[:, u, :], x_d[tt * 128:(tt + 1) * 128, :], writes=[("xt", u)], key=("xt", u))
        S.op('act', lambda e: e.activation(out=junk, in_=xt[:, u, :], func=AF.Square, scale=float(D) ** -0.5,
                                           accum_out=small[:, tt:tt + 1]),
             reads=[("xt", u)], writes=["junk", ("ssq", tt)])
        S.op('act', lambda e: e.activation(out=small[:, 16 + tt:17 + tt], in_=small[:, tt:tt + 1], func=AF.Sqrt, bias=epsc[:, 0:1]),
             reads=[("ssq", tt), "epsc"], writes=[("rs", tt)])
        S.op('dve', lambda e: e.reciprocal(out=small[:, 16 + tt:17 + tt], in_=small[:, 16 + tt:17 + tt]),
             reads=[("rs", tt)], writes=[("rs", tt)])
        S.op('dve', lambda e: e.tensor_scalar(out=xn[:, slot, t4, :], in0=xt[:, u, :], scalar1=small[:, 16 + tt:17 + tt], scalar2=None,
                                              op0=ALU.mult), reads=[("xt", u), ("rs", tt)], writes=[("xn", slot, t4)])

    def n_stage1(q):
        for t4 in range(NG):
            n_tile(q * NG + t4)

    def n_stage2(q):
        slot = q % 4
        W_ = NG * 128
        for kc in range(16):
            bank = 4 + (kc // 2) % 4
            half = kc % 2
            pst = ps[:, bank, :].bitcast(BF16)[:, half * 512:half * 512 + W_]
            fns = []
            for t4 in range(NG):
                fns.append(lambda e, t4=t4, pst=pst: e.transpose(pst[:, t4 * 128:(t4 + 1) * 128],
                                                                 xn[:, slot, t4, kc * 128:(kc + 1) * 128], ident))
            S.multi('pe', fns, reads=[("xn", slot, t4) for t4 in range(NG)] + ["cb"], writes=PSU(bank, half * 256, half * 256 + 256))
            dst = hT[:, kc, q * W_:(q + 1) * W_]
            wr = [("hT", q * NG + t4) for t4 in range(NG)]
            if kc % 2 == 0:
                S.op('act', lambda e, pst=pst, dst=dst: e.activation(out=dst, in_=pst, func=AF.Identity,
                                                                    bias=shiftc[:, kc:kc + 1], scale=gs[:, kc:kc + 1]),
                     reads=PSU(bank, half * 256, half * 256 + 256) + ["gs", "shiftc"], writes=wr)
            else:
                S.op('dve', lambda e, pst=pst, dst=dst: e.tensor_scalar(out=dst, in0=pst, scalar1=gs[:, kc:kc + 1],
                                                                       scalar2=shiftc[:, kc:kc + 1], op0=ALU.mult, op1=ALU.add),
                     reads=PSU(bank, half * 256, half * 256 + 256) + ["gs", "shiftc"], writes=wr)

    p0 = phase0()

    def p0_step(n):
        for _ in range(n):
            try:
                next(p0)
            except StopIteration:
                return

    NQ = NT // NG
    for i in range(8):
        p0_step(1)
        if i < 3 * NG:
            n_tile(i)
    for q in range(NQ):
        n_stage2(q)
        if q % 2 == 1:
            p0_step(1)
        if q + 3 < NQ:
            n_stage1(q + 3)
    p0_step(12)
    S.barrier()

    o = o_after_hT
    wbuf, o = carve(o, [128, 3, 16, 128], BF16)
    stgb, o = carve(o, [128, 2, T], BF16)
    stgf, o = carve(o, [128, 2, T], F32)
    amu, o = carve(o, [128, T + 1], F32)
    S.op('pool', lambda e: e.memset(amu[:, 0:1], 0.0), writes=["amu0"])
    assert o <= VP_OFF, o
    S.op('pool', lambda e: e.memset(Vp.rearrange("p a b c -> p (a b c)"), 0.0), writes=[("Vp", i) for i in range(17)] + ["Vp"])
    hT_all = [("hT", tt) for tt in range(NT)]

    def blk_kind(blk):
        if blk < 2: return ('copy', blk, None)
        if blk < 6: return ('v', blk - 4, None)
        if blk < 14: return ('q', 4 + (blk - 6), None)
        if blk < 22: return ('silu', 12 + (blk - 14), None)
        if blk == 22: return ('lerp', 0, 0)
        j, r = divmod(blk - 23, 4)
        if r < 3: return ('lerp', 1 + 3 * j + r, 1 + 3 * j + r)
        return ('silu', 20 + j, None)

    SEQ = [0, 1] + list(range(4, NBLK))
    for p_ in range(2):
        S.dma('pool', wbuf[:, p_ % 3], win_d[SEQ[p_]], writes=[("wbuf", p_ % 3)], key=("wb", p_ % 3))
    def phaseI(blks):
      bankrot = 0
      nb_i = 0
      nf_i = 0
      for p_ in blks:
          blk = SEQ[p_]
          wu = p_ % 3
          if p_ + 2 < len(SEQ):
              yield S.dma('pool', wbuf[:, (p_ + 2) % 3], win_d[SEQ[p_ + 2]], writes=[("wbuf", (p_ + 2) % 3)], key=("wb", (p_ + 2) % 3))
          kind, sidx, mucol = blk_kind(blk)
          if kind == 'v':
              for tt in range(NT):
                  bank = bankrot % 4; bankrot += 1
                  fns = [lambda e, kc=kc, bank=bank, tt=tt: e.matmul(ps[:, bank, 0:128], hT[:, kc, tt * 128:(tt + 1) * 128],
                                                                     wbuf[:, wu, kc, :], start=(kc == 0), stop=(kc == 15))
                         for kc in range(16)]
                  yield S.multi('pe', fns, reads=[("hT", tt), ("wbuf", wu)], writes=PSU(bank, 0, 128))
                  dst = Vp[:, 1 + tt, 2 * sidx:2 * sidx + 2, 64:128]
                  src = ps[:, bank, 0:128].rearrange("p (a b) -> p a b", a=2)
                  eng = 'act' if tt % 2 == 0 else 'dve'
                  if eng == 'act':
                      yield S.op('act', lambda e, dst=dst, src=src: e.activation(out=dst, in_=src, func=AF.Copy),
                           reads=PSU(bank, 0, 128), writes=[("Vp", 1 + tt)])
                  else:
                      yield S.op('dve', lambda e, dst=dst, src=src: e.tensor_copy(out=dst, in_=src),
                           reads=PSU(bank, 0, 128), writes=[("Vp", 1 + tt)])
              continue
          if kind == 'lerp':
              su = nf_i % 2; nf_i += 1
              stg = stgf[:, su, :]
              stgname = ("stgf", su)
          else:
              su = nb_i % 2; nb_i += 1
              stg = stgb[:, su, :]
              stgname = ("stgb", su)
          for wn in range(4):
              bank = bankrot % 4; bankrot += 1
              tok = slice(wn * 512, (wn + 1) * 512)
              for pc4 in range(2):
                  fns = [lambda e, kc=kc, bank=bank, tok=tok: e.matmul(ps[:, bank, :], wbuf[:, wu, kc, :], hT[:, kc, tok],
                                                                       start=(kc == 0), stop=(kc == 15)) for kc in range(pc4 * 8, pc4 * 8 + 8)]
                  yield S.multi('pe', fns, reads=hT_all[wn * 4:(wn + 1) * 4] + [("wbuf", wu)], writes=PSU(bank, 0, 512))
              pin = ps[:, bank, :]
              if kind == 'copy':
                  yield S.op('act', lambda e, pin=pin, tok=tok: e.activation(out=stg[:, tok], in_=pin, func=AF.Copy),
                       reads=PSU(bank, 0, 512), writes=[stgname])
              elif kind == 'q':
                  yield S.op('dve', lambda e, pin=pin, tok=tok: e.tensor_scalar(out=stg[:, tok], in0=pin, scalar1=0.125, scalar2=None,
                                                                         op0=ALU.mult), reads=PSU(bank, 0, 512), writes=[stgname])
              elif kind == 'silu':
                  yield S.op('act', lambda e, pin=pin, tok=tok: e.activation(out=stg[:, tok], in_=pin, func=AF.Silu),
                       reads=PSU(bank, 0, 512), writes=[stgname])
              else:
                  mc = pv[:, PV_MU + mucol:PV_MU + mucol + 1]
                  oc = omm[:, mucol:mucol + 1]
                  yield S.op('act', lambda e, pin=pin, wn=wn, mc=mc: e.activation(out=amu[:, 1 + wn * 512:1 + (wn + 1) * 512], in_=pin,
                                                                            func=AF.Copy, scale=mc),
                       reads=PSU(bank, 0, 512) + ["pv"], writes=[("amu", wn)])
                  rd = [("amu", wn)] + ([("amu", wn - 1)] if wn > 0 else ["amu0"])
                  yield S.op('dve', lambda e, pin=pin, wn=wn, oc=oc, tok=tok: e.scalar_tensor_tensor(
                      out=stg[:, tok], in0=pin, scalar=oc, in1=amu[:, wn * 512:(wn + 1) * 512], op0=ALU.mult, op1=ALU.add),
                      reads=PSU(bank, 0, 512) + rd + ["omm"], writes=[stgname])
          if kind == 'lerp':
              yield S.dma('sp', scrR[sidx], stg, reads=[stgname], writes=[("scrR", sidx)], key=("spf", su))
          else:
              yield S.dma('sp', scrA[sidx], stg, reads=[stgname], writes=[("scrA", sidx)], key=("spb", su))

    o = _vpend
    qT, o = carve(o, [128, 1, 2, T], BF16)
    kTg, o = carve(o, [128, 1, T + 128], BF16)
    gaT, o = carve(o, [128, 1, 2, T], BF16)
    PT, o = carve(o, [128, 2, 2, 512], BF16)
    rden, o = carve(o, [128, 2, 256], F32)
    ynum, o = carve(o, [128, 2, 256], F32)
    mixst, o = carve(o, [128, 2, 2, 128], BF16)
    assert o <= ARENA_W, o
    S.op('pool', lambda e: e.memset(kTg[:, :, 0:128], 0.0), writes=[("kTg", 0)])
    maskA = cb[:, CB_MASKA:CB_MASKA + 256]
    maskA0 = cb[:, CB_MASKA0:CB_MASKA0 + 256]
    OL = [cb[:, CB_OL0:CB_OL0 + 128], cb[:, CB_OL1:CB_OL1 + 128]]
    def a_load(g):
        gu = 0
        for pi in range(2):
            yield S.dma('sp', qT[:, gu, pi, :], scrA[4 + 2 * g + pi], reads=[("scrA", 4 + 2 * g + pi)], writes=[("qT", gu)],
                  key=("aq", gu), serialize=(pi == 0))
        ksrc = scrA[g // 2][64 * (g % 2):64 * (g % 2) + 64, :]
        yield S.dma('sp', kTg[0:64, gu, 128:], ksrc, reads=[("scrA", g // 2)], writes=[("kTg", gu)], key=("ak", gu))
        yield S.dma('sp', kTg[64:128, gu, 128:], ksrc, reads=[("scrA", g // 2)], writes=[("kTg", gu)], key=("ak", gu), serialize=False)

    def a_stage1(unit):
        g, tt = divmod(unit, NT)
        gu = 0
        u = unit % 2
        if tt == 0:
            yield from a_load(g)
        for e_ in range(2):
            bank = 4 + e_
            rows = slice(64 * e_, 64 * e_ + 64)
            fns = []
            for pc in range(2):
                fns.append(lambda e, rows=rows, pc=pc, bank=bank: e.matmul(
                    ps[:, bank, pc * 256:(pc + 1) * 256], kTg[rows, gu, (tt + pc) * 128:(tt + pc + 1) * 128],
                    qT[rows, gu, :, tt * 128:(tt + 1) * 128], start=True, stop=True))
            yield S.multi('pe', fns, reads=[("kTg", gu), ("qT", gu)], writes=PSU(bank, 0, 512))
            yield S.op('act', lambda e, bank=bank, e_=e_: e.activation(out=PT[:, u, e_, :], in_=ps[:, bank, :], func=AF.Exp),
                 reads=PSU(bank, 0, 512), writes=[("PT", u, e_)])
        mk = maskA0 if tt == 0 else maskA
        for e_ in range(2):
            for pc in range(2):
                ptv = PT[:, u, e_, pc * 256:(pc + 1) * 256].rearrange("p (a b) -> p a b", a=2)
                mkp = mk[:, pc * 128:(pc + 1) * 128].unsqueeze(1).broadcast_to([128, 2, 128])
                yield S.op('dve', lambda e, ptv=ptv, mkp=mkp: e.tensor_tensor(out=ptv, in0=ptv, in1=mkp, op=ALU.mult),
                           reads=[("PT", u, e_), "cb"], writes=[("PT", u, e_)])

    def a_stage2(unit):
        g, tt = divmod(unit, NT)
        gu = 0
        u = unit % 2
        nbank = 6 + u
        if tt == 0:
            for pi in range(2):
                yield S.dma('sp', gaT[:, gu, pi, :], scrA[12 + 2 * g + pi], reads=[("scrA", 12 + 2 * g + pi)], writes=[("gaT", gu)],
                            key=("ag", gu), serialize=(pi == 0))
        fns = []
        seq = [(e_, pc) for e_ in range(2) for pc in range(2)]
        numv = ps[:, nbank, 0:256]
        denv = ps[:, nbank, 256:512]
        ptq = lambda e_, pc: PT[:, u, e_, pc * 256:(pc + 1) * 256]
        for i, (e_, pc) in enumerate(seq):
            vl = Vp[:, tt + pc, g, 64:192] if e_ == 0 else Vp[:, tt + pc, g, 0:128]
            fns.append(lambda e, e_=e_, pc=pc, vl=vl, i=i: e.matmul(numv, vl, ptq(e_, pc), start=(i == 0), stop=(i == 3)))
        for i, (e_, pc) in enumerate(seq):
            fns.append(lambda e, e_=e_, pc=pc, i=i: e.matmul(denv, OL[e_], ptq(e_, pc), start=(i == 0), stop=(i == 3)))
        yield S.multi('pe', fns, reads=[("PT", u, 0), ("PT", u, 1), ("Vp", tt), ("Vp", tt + 1), "Vp", "cb", "cf", "sinkl"],
                writes=PSU(nbank, 0, 512))
        for pi in range(2):
            pr = 2 * g + pi
            yield S.op('act', lambda e, pi=pi, pr=pr: e.activation(out=rden[:, u, pi * 128:(pi + 1) * 128],
                                                                 in_=ps[:, nbank, 256 + pi * 128:256 + (pi + 1) * 128],
                                                                 func=AF.Ln, bias=sinkc[:, pr:pr + 1]),
                       reads=PSU(nbank, 256, 512) + ["sinkc"], writes=[("rden", u)])
        yield S.op('act', lambda e: e.activation(out=rden[:, u, :], in_=rden[:, u, :], func=AF.Exp, scale=-1.0), reads=[("rden", u)],
             writes=[("rden", u)])
        yield S.op('dve', lambda e: e.tensor_tensor(out=ynum[:, u, :], in0=ps[:, nbank, 0:256], in1=rden[:, u, :], op=ALU.mult),
             reads=PSU(nbank, 0, 256) + [("rden", u)], writes=[("ynum", u)])
        yield S.op('pool', lambda e: e.tensor_tensor(out=mixst[:, u, :, :],
                                               in0=ynum[:, u, :].rearrange("p (a b) -> p a b", a=2),
                                               in1=gaT[:, gu, :, tt * 128:(tt + 1) * 128], op=ALU.mult),
             reads=[("ynum", u), ("gaT", gu)], writes=[("mixst", u)])
        yield S.dma('sp', scrM[2 * g:2 * g + 2, :, tt * 128:(tt + 1) * 128].rearrange("c p t -> p c t"), mixst[:, u, :, :],
              reads=[("mixst", u)], writes=[("scrM", g, tt)], key=("am", u))

    NU = 4 * NT

    def phaseA():
        yield from a_stage1(0)
        for unit in range(NU):
            if unit + 1 < NU:
                yield from a_stage1(unit + 1)
            yield from a_stage2(unit)

    for _ in phaseI(range(0, 20)):
        pass
    gens = [phaseI(range(20, len(SEQ))), phaseA()]
    while gens:
        for gen in list(gens):
            try:
                next(gen)
            except StopIteration:
                gens.remove(gen)
    S.barrier()

    WR = 256
    NWR = T // WR
    CW = WR // 64
    o = 0
    Hba, o = carve(o, [128, 8, 64], BF16)
    Hbb, o = carve(o, [128, 8, 64], BF16)
    Hb = [Hba, Hbb]
    S.op('pool', lambda e: e.memset(Hba.rearrange("p a b -> p (a b)"), 0.0), writes=[("Hb", 0, 0), ("Hb", 0, 1)])
    S.op('pool', lambda e: e.memset(Hbb.rearrange("p a b -> p (a b)"), 0.0), writes=[("Hb", 1, 0), ("Hb", 1, 1)])
    bones = cf[:, CF_BONES:CF_BONES + 128]
    bones64 = cf[:, CF_BONES64:CF_BONES64 + 128]
    resetm = cf[:, CF_RESET:CF_RESET + WR]
    mST = cb[:, CB_MST:CB_MST + 64]
    mS = cb[:, CB_MS:CB_MS + 64]
    mIT = cb[:, CB_MIT:CB_MIT + 64]
    I64 = cb[:, CB_I64:CB_I64 + 64]
    bc4 = lambda m: m.unsqueeze(1).broadcast_to([128, 4, 64])
    v4 = lambda a: a.rearrange("p (a b) -> p a b", a=4)

    class LB:
        pass
    lanes = []
    for g in range(2):
        L = LB()
        L.tw, o = carve(o, [128, WR], BF16)
        L.wdl, o = carve(o, [128, WR], F32)
        L.rkv, o = carve(o, [128, 2, 3, WR], F32)
        L.tmp = []
        for i in range(12):
            t_, o = carve(o, [128, WR], F32)
            L.tmp.append(t_)
        L.tb = []
        for i in range(3):
            t_, o = carve(o, [128, WR], BF16)
            L.tb.append(t_)
        L.grs, L.RT, L.AT, L.BT, L.KT, L.VT, L.BCt, L.KCt, L.BON, L.Wend = [], [], [], [], [], [], [], [], [], []
        for wp in range(2):
            for lst, shp, dt_ in ((L.grs, [128, 4, WR], BF16), (L.RT, [128, 4, WR], BF16), (L.AT, [128, 4, WR], BF16),
                                  (L.BT, [128, 4, WR], BF16), (L.KT, [128, 4, WR], BF16), (L.VT, [128, CW, 4, 64], BF16),
                                  (L.BCt, [128, CW, 4, 64], BF16), (L.KCt, [128, CW, 4, 64], BF16),
                                  (L.BON, [128, 4, WR], F32), (L.Wend, [128, 4, CW], F32)):
                t_, o = carve(o, shp, dt_)
                lst.append(t_)
        L.Yw, o = carve(o, [128, 4, WR], F32)
        L.mixr, o = carve(o, [128, 4, WR], BF16)
        L.ptmp = []
        for i in range(3):
            t_, o = carve(o, [128, WR], F32)
            L.ptmp.append(t_)
        L.NNs = []
        for i in range(2):
            t_, o = carve(o, [128, 2, 256], BF16)
            L.NNs.append(t_)
        L.PTm = []
        for i in range(2):
            t_, o = carve(o, [128, 256], BF16)
            L.PTm.append(t_)
        L.G2 = []
        for i in range(2):
            t_, o = carve(o, [128, 2, 256], BF16)
            L.G2.append(t_)
        L.G3 = []
        for i in range(2):
            t_, o = carve(o, [128, 256], BF16)
            L.G3.append(t_)
        L.Xb, o = carve(o, [128, 256], BF16)
        L.Ub, o = carve(o, [128, 256], BF16)
        lanes.append(L)
    assert o <= ARENA_W, o
    LO, HI = slice(0, 256), slice(256, 512)

    def prep_stream(grp):
        L = lanes[grp]
        N_ = lambda n, *a: (n, grp) + a
        P0, P1 = 4 * grp, 4 * grp + 1
        tw, wdl, rkv = L.tw, L.wdl, L.rkv
        sw, a_, cs, W_, Wi, Wp, WC, kkr, sq, kkn, kmod, ba = L.tmp
        vlb, bCb, kCb = L.tb
        for wn in range(NWR):
            wp = wn % 2
            while chain_done[grp] < wn - 1:
                yield None
            grs, RT, AT, BT, KT, VT, BCt, KCt, BON, Wend = (L.grs[wp], L.RT[wp], L.AT[wp], L.BT[wp], L.KT[wp], L.VT[wp],
                                                            L.BCt[wp], L.KCt[wp], L.BON[wp], L.Wend[wp])
            wtok = slice(wn * WR, (wn + 1) * WR)
            yield S.dma('sp', wdl, scrR[0][:, wtok], reads=[("scrR", 0)], writes=[N_("wdl")], key=N_("rw"))
            yield S.op('act', lambda e: e.activation(out=tw[0:64, :], in_=wdl[0:64, :], func=AF.Tanh), reads=[N_("wdl")], writes=[N_("tw")])
            yield S.op('dve', lambda e: e.tensor_copy(out=tw[64:128, :], in_=wdl[64:128, :]), reads=[N_("wdl")], writes=[N_("tw1")])
            for jj in range(4):
                j = grp * 4 + jj
                ru = jj % 2
                rl, kl, vl_ = rkv[:, ru, 0, :], rkv[:, ru, 1, :], rkv[:, ru, 2, :]
                RKV = N_("rkv", ru)
                for r_ in range(3):
                    yield S.dma('sp', rkv[:, ru, r_, :], scrR[1 + 3 * j + r_][:, wtok], reads=[("scrR", 1 + 3 * j + r_)],
                                writes=[RKV], key=N_("rr", ru), serialize=(r_ == 0))
                yield S.dma('sp', grs[:, jj, :], scrA[20 + j][:, wtok], reads=[("scrA", 20 + j)], writes=[N_("grs", wp, jj)], key=N_("rg", jj))
                col = lambda base: pv[:, base + j:base + j + 1]
                yield S.op('pe', lambda e: e.matmul(ps[:, P0, LO], lora[0:64, j * 128:(j + 1) * 128], tw[0:64, :], start=True, stop=True),
                           reads=["lora", N_("tw")], writes=PSU(P0, 0, 256))
                yield S.op('pe', lambda e: e.matmul(ps[:, P1, LO], lora[64:128, j * 128:(j + 1) * 128], tw[64:128, :], start=True, stop=True),
                           reads=["lora", N_("tw1")], writes=PSU(P1, 0, 256))
                yield S.op('act', lambda e: e.activation(out=sw, in_=ps[:, P0, LO], func=AF.Sigmoid, bias=col(PV_W0)),
                           reads=PSU(P0, 0, 256) + ["pv"], writes=[N_("sw")])
                yield S.op('act', lambda e: e.activation(out=a_, in_=ps[:, P1, LO], func=AF.Sigmoid, bias=col(PV_A0)),
                           reads=PSU(P1, 0, 256) + ["pv"], writes=[N_("a_")])
                yield S.op('dve', lambda e: e.tensor_tensor_scan(out=cs, data0=resetm, data1=sw, initial=0.0, op0=ALU.mult, op1=ALU.add),
                           reads=[N_("sw"), "cf"], writes=[N_("cs")])
                yield S.op('act', lambda e: e.activation(out=sq, in_=kl, func=AF.Square, scale=col(PV_KK)),
                           reads=[RKV, "pv"], writes=[N_("sq")])
                yield S.op('pe', lambda e: e.matmul(ps[:, P0, HI], bones, sq, start=True, stop=True), reads=[N_("sq"), "cf"], writes=PSU(P0, 256, 512))
                yield S.op('act', lambda e: e.activation(out=W_, in_=cs, func=AF.Exp, scale=-C0), reads=[N_("cs")], writes=[N_("W_")])
                yield S.op('act', lambda e: e.activation(out=Wi, in_=cs, func=AF.Exp, scale=C0), reads=[N_("cs")], writes=[N_("Wi")])
                yield S.op('dve', lambda e: e.tensor_tensor(out=Wp, in0=cs, in1=sw, op=ALU.subtract), reads=[N_("cs"), N_("sw")], writes=[N_("Wp")])
                yield S.op('act', lambda e: e.activation(out=Wp, in_=Wp, func=AF.Exp, scale=-C0), reads=[N_("Wp")], writes=[N_("Wp")])
                cs3 = cs.rearrange("p (a b) -> p a b", a=CW)
                yield S.op('act', lambda e: e.activation(out=Wend[:, jj, :], in_=cs3[:, :, 63], func=AF.Exp, scale=-C0),
                           reads=[N_("cs")], writes=[N_("Wend", wp, jj)])
                yield S.op('dve', lambda e: e.tensor_scalar(out=sq, in0=ps[:, P0, HI], scalar1=1e-24, scalar2=None, op0=ALU.max),
                           reads=PSU(P0, 256, 512), writes=[N_("sq")])
                yield S.op('act', lambda e: e.activation(out=sq, in_=sq, func=AF.Ln), reads=[N_("sq")], writes=[N_("sq")])
                yield S.op('act', lambda e: e.activation(out=sq, in_=sq, func=AF.Exp, scale=-0.5), reads=[N_("sq")], writes=[N_("sq")])
                yield S.op('dve', lambda e: e.scalar_tensor_tensor(out=kkn, in0=kl, scalar=col(PV_KK), in1=sq, op0=ALU.mult, op1=ALU.mult),
                           reads=[RKV, "pv", N_("sq")], writes=[N_("kkn")])
                yield S.op('act', lambda e: e.activation(out=kmod, in_=a_, func=AF.Identity, scale=col(PV_KA), bias=omka[:, j:j + 1]),
                           reads=[N_("a_"), "pv", "omka"], writes=[N_("kmod")])
                yield S.op('pool', lambda e: e.tensor_tensor(out=kmod, in0=kmod, in1=kl, op=ALU.mult),
                           reads=[N_("kmod"), RKV], writes=[N_("kmod")])
                yield S.op('pool', lambda e: e.tensor_tensor(out=RT[:, jj, :], in0=rl, in1=W_, op=ALU.mult),
                           reads=[RKV, N_("W_")], writes=[N_("RT", wp, jj)])
                yield S.op('dve', lambda e: e.scalar_tensor_tensor(out=AT[:, jj, :], in0=kkn, scalar=-1.0, in1=Wp, op0=ALU.mult, op1=ALU.mult),
                           reads=[N_("kkn"), N_("Wp")], writes=[N_("AT", wp, jj)])
                yield S.op('pool', lambda e: e.tensor_tensor(out=ba, in0=kkn, in1=a_, op=ALU.mult), reads=[N_("kkn"), N_("a_")], writes=[N_("ba")])
                yield S.op('pool', lambda e: e.tensor_tensor(out=BT[:, jj, :], in0=ba, in1=Wi, op=ALU.mult), reads=[N_("ba"), N_("Wi")], writes=[N_("BT", wp, jj)])
                wend_bc = Wend[:, jj, :].unsqueeze(2).broadcast_to([128, CW, 64])
                r3 = lambda a: a.rearrange("p (a b) -> p a b", a=CW)
                yield S.op('pool', lambda e: e.tensor_tensor(out=r3(bCb), in0=r3(BT[:, jj, :]), in1=wend_bc, op=ALU.mult),
                           reads=[N_("BT", wp, jj), N_("Wend", wp, jj)], writes=[N_("bCb")])
                yield S.op('pool', lambda e: e.tensor_tensor(out=KT[:, jj, :], in0=kmod, in1=Wi, op=ALU.mult), reads=[N_("kmod"), N_("Wi")], writes=[N_("KT", wp, jj)])
                yield S.op('pool', lambda e: e.tensor_tensor(out=r3(kCb), in0=r3(KT[:, jj, :]), in1=wend_bc, op=ALU.mult),
                           reads=[N_("KT", wp, jj), N_("Wend", wp, jj)], writes=[N_("kCb")])
                yield S.op('dve', lambda e: e.scalar_tensor_tensor(out=kkr, in0=rl, scalar=col(PV_RK), in1=kmod, op0=ALU.mult, op1=ALU.mult),
                           reads=[RKV, "pv", N_("kmod")], writes=[N_("kkr")])
                yield S.op('pe', lambda e: e.matmul(ps[:, P1, HI], bones, kkr, start=True, stop=True), reads=[N_("kkr"), "cf"], writes=PSU(P1, 256, 512))
                yield S.op('dve', lambda e: e.tensor_tensor(out=BON[:, jj, :], in0=ps[:, P1, HI], in1=vl_, op=ALU.mult),
                           reads=PSU(P1, 256, 512) + [RKV], writes=[N_("BON", wp, jj)])
                if POOL_COPY:
                    yield S.op('pool', lambda e: e.tensor_copy(out=vlb, in_=vl_), reads=[RKV], writes=[N_("vlb")])
                else:
                    yield S.op('act', lambda e: e.activation(out=vlb, in_=vl_, func=AF.Copy), reads=[RKV], writes=[N_("vlb")])
                for si, (src, sname, dst, dname, bank, hh) in enumerate([(vlb, "vlb", VT, "VT", P0, 0), (bCb, "bCb", BCt, "BCt", P1, 0),
                                                                          (kCb, "kCb", KCt, "KCt", P0, 256)]):
                    fns = []
                    for c in range(CW):
                        for e_ in range(2):
                            rows = slice(64 * e_, 64 * e_ + 64)
                            fns.append(lambda e, src=src, bank=bank, rows=rows, c=c, hh=hh: e.matmul(
                                ps[rows, bank, hh + c * 64:hh + (c + 1) * 64], src[rows, c * 64:(c + 1) * 64], I64[rows, :], start=True, stop=True))
                    yield S.multi('pe', fns, reads=[N_(sname), "cb"], writes=PSU(bank, hh, hh + 256))
                    pv3 = ps[:, bank, hh:hh + 256].rearrange("p (a b) -> p a b", a=CW)
                    if si == 1:
                        yield S.op('dve', lambda e, dst=dst, pv3=pv3: e.tensor_copy(out=dst[:, :, jj, :], in_=pv3),
                                   reads=PSU(bank, hh, hh + 256), writes=[N_(dname, wp, jj)])
                    else:
                        yield S.op('act', lambda e, dst=dst, pv3=pv3: e.activation(out=dst[:, :, jj, :], in_=pv3, func=AF.Copy),
                                   reads=PSU(bank, hh, hh + 256), writes=[N_(dname, wp, jj)])
            prep_done[grp] = wn + 1

    def chain_stream(grp):
        L = lanes[grp]
        N_ = lambda n, *a: (n, grp) + a
        C0, C1 = 4 * grp + 2, 4 * grp + 3
        Yw, mixr = L.Yw, L.mixr
        NNs, PTm, G2, G3, Xb, Ub = L.NNs, L.PTm, L.G2, L.G3, L.Xb, L.Ub
        hbpar = 0
        for wn in range(NWR):
            wp = wn % 2
            grs, RT, AT, BT, KT, VT, BCt, KCt, BON, Wend = (L.grs[wp], L.RT[wp], L.AT[wp], L.BT[wp], L.KT[wp], L.VT[wp],
                                                            L.BCt[wp], L.KCt[wp], L.BON[wp], L.Wend[wp])
            allj = lambda n: [N_(n, wp, jj) for jj in range(4)]
            wtok = slice(wn * WR, (wn + 1) * WR)
            while prep_done[grp] < wn + 1:
                yield None
            for c in range(CW):
                cu = c % 2
                tokc = slice(c * 64, (c + 1) * 64)
                fa, fb, fc = [], [], []
                for jj in range(4):
                    for e_ in range(2):
                        rows = slice(64 * e_, 64 * e_ + 64)
                        cs_ = slice(jj * 64, (jj + 1) * 64)
                        cs2 = slice(256 + jj * 64, 256 + (jj + 1) * 64)
                        fa.append(lambda e, rows=rows, cs_=cs_, jj=jj: e.matmul(ps[rows, C0, cs_], BT[rows, jj, tokc], AT[rows, jj, tokc], start=True, stop=True))
                        fa.append(lambda e, rows=rows, cs2=cs2, jj=jj: e.matmul(ps[rows, C0, cs2], AT[rows, jj, tokc], BT[rows, jj, tokc], start=True, stop=True))
                        fb.append(lambda e, rows=rows, cs_=cs_, jj=jj: e.matmul(ps[rows, C1, cs_], KT[rows, jj, tokc], AT[rows, jj, tokc], start=True, stop=True))
                        fb.append(lambda e, rows=rows, cs2=cs2, jj=jj: e.matmul(ps[rows, C1, cs2], BT[rows, jj, tokc], RT[rows, jj, tokc], start=True, stop=True))
                        fc.append(lambda e, rows=rows, cs_=cs_, jj=jj: e.matmul(ps[rows, C0, cs_], KT[rows, jj, tokc], RT[rows, jj, tokc], start=True, stop=True))
                yield S.multi('pe', fa, reads=allj("BT") + allj("AT"), writes=PSU(C0, 0, 512))
                yield S.multi('pe', fb, reads=allj("BT") + allj("AT") + allj("KT") + allj("RT"), writes=PSU(C1, 0, 512))
                N0 = NNs[0]
                yield S.op('dve', lambda e: e.tensor_tensor(out=v4(N0[:, 0, :]), in0=v4(ps[:, C0, LO]), in1=bc4(mST), op=ALU.mult),
                           reads=PSU(C0, 0, 256) + ["cb"], writes=[N_("NN", 0)])
                yield S.op('dve', lambda e: e.tensor_tensor(out=v4(N0[:, 1, :]), in0=v4(ps[:, C0, HI]), in1=bc4(mS), op=ALU.mult),
                           reads=PSU(C0, 256, 512) + ["cb"], writes=[N_("NN", 0)])
                yield S.multi('pe', fc, reads=allj("KT") + allj("RT"), writes=PSU(C0, 0, 256))
                yield S.op('pool', lambda e: e.tensor_tensor(out=v4(PTm[0]), in0=v4(N0[:, 0, :]), in1=bc4(I64), op=ALU.add),
                           reads=[N_("NN", 0), "cb"], writes=[N_("PTm", 0)])
                yield S.op('dve', lambda e: e.tensor_tensor(out=v4(G2[cu][:, 0, :]), in0=v4(ps[:, C1, LO]), in1=bc4(mST), op=ALU.mult),
                           reads=PSU(C1, 0, 256) + ["cb"], writes=[N_("G2", cu)])
                yield S.op('dve', lambda e: e.tensor_tensor(out=v4(G2[cu][:, 1, :]), in0=v4(ps[:, C1, HI]), in1=bc4(mIT), op=ALU.mult),
                           reads=PSU(C1, 256, 512) + ["cb"], writes=[N_("G2", cu)])
                yield S.op('dve', lambda e: e.tensor_tensor(out=v4(G3[cu]), in0=v4(ps[:, C0, LO]), in1=bc4(mIT), op=ALU.mult),
                           reads=PSU(C0, 0, 256) + ["cb"], writes=[N_("G3", cu)])
                for l in range(5):
                    a0_, a1_ = NNs[l % 2], NNs[(l + 1) % 2]
                    fns = []
                    for jj in range(4):
                        for e_ in range(2):
                            rows = slice(64 * e_, 64 * e_ + 64)
                            cs_ = slice(jj * 64, (jj + 1) * 64)
                            cs2 = slice(256 + jj * 64, 256 + (jj + 1) * 64)
                            fns.append(lambda e, rows=rows, cs_=cs_, cs2=cs2, a0_=a0_: e.matmul(
                                ps[rows, C1, cs2], a0_[rows, 0, cs_], a0_[rows, 1, cs_], start=True, stop=True))
                            fns.append(lambda e, rows=rows, cs_=cs_, a0_=a0_: e.matmul(
                                ps[rows, C1, cs_], a0_[rows, 1, cs_], a0_[rows, 0, cs_], start=True, stop=True))
                    yield S.multi('pe', fns, reads=[N_("NN", l % 2)], writes=PSU(C1, 0, 512))
                    if l % 2 == 0 or not DVE_EVAC:
                        yield S.op('act', lambda e, a1_=a1_: e.activation(out=a1_.rearrange("p a b -> p (a b)"), in_=ps[:, C1, :], func=AF.Copy),
                                   reads=PSU(C1, 0, 512), writes=[N_("NN", (l + 1) % 2)])
                    else:
                        yield S.op('dve', lambda e, a1_=a1_: e.tensor_copy(out=a1_.rearrange("p a b -> p (a b)"), in_=ps[:, C1, :]),
                                   reads=PSU(C1, 0, 512), writes=[N_("NN", (l + 1) % 2)])
                    fns = []
                    for jj in range(4):
                        for e_ in range(2):
                            rows = slice(64 * e_, 64 * e_ + 64)
                            cs_ = slice(jj * 64, (jj + 1) * 64)
                            cs2 = slice(256 + jj * 64, 256 + (jj + 1) * 64)
                            fns.append(lambda e, rows=rows, cs_=cs_, cs2=cs2, a1_=a1_, l=l: e.matmul(
                                ps[rows, C0, cs2], a1_[rows, 1, cs_], PTm[l % 2][rows, cs_], start=True, stop=True))
                    yield S.multi('pe', fns, reads=[N_("NN", (l + 1) % 2), N_("PTm", l % 2)], writes=PSU(C0, 256, 512))
                    yield S.op('dve', lambda e, l=l: e.tensor_tensor(out=PTm[(l + 1) % 2], in0=ps[:, C0, HI], in1=PTm[l % 2], op=ALU.add),
                               reads=PSU(C0, 256, 512) + [N_("PTm", l % 2)], writes=[N_("PTm", (l + 1) % 2)])
                TT = PTm[1]
                hb = Hb[hbpar]
                hbn = Hb[1 - hbpar]
                fns = []
                for jj in range(4):
                    j = grp * 4 + jj
                    for e_ in range(2):
                        rows = slice(64 * e_, 64 * e_ + 64)
                        cs_ = slice(jj * 64, (jj + 1) * 64)
                        fns.append(lambda e, rows=rows, cs_=cs_, jj=jj, j=j: e.matmul(ps[rows, C1, cs_], AT[rows, jj, tokc], hb[rows, j, :], start=True, stop=False))
                        fns.append(lambda e, rows=rows, cs_=cs_, jj=jj: e.matmul(ps[rows, C1, cs_], G2[cu][rows, 0, cs_], VT[rows, c, jj, :], start=False, stop=True))
                yield S.multi('pe', fns, reads=allj("AT") + [("Hb", hbpar, grp), N_("G2", cu)] + allj("VT"), writes=PSU(C1, 0, 256))
                yield S.op('act', lambda e: e.activation(out=Xb, in_=ps[:, C1, LO], func=AF.Copy), reads=PSU(C1, 0, 256), writes=[N_("Xb")])
                fns = []
                for jj in range(4):
                    for e_ in range(2):
                        rows = slice(64 * e_, 64 * e_ + 64)
                        cs_ = slice(jj * 64, (jj + 1) * 64)
                        cs2 = slice(256 + jj * 64, 256 + (jj + 1) * 64)
                        fns.append(lambda e, rows=rows, cs_=cs_, cs2=cs2: e.matmul(ps[rows, C1, cs2], TT[rows, cs_], Xb[rows, cs_], start=True, stop=True))
                yield S.multi('pe', fns, reads=[N_("PTm", 1), N_("Xb")], writes=PSU(C1, 256, 512))
                yield S.op('act', lambda e: e.activation(out=Ub, in_=ps[:, C1, HI], func=AF.Copy), reads=PSU(C1, 256, 512), writes=[N_("Ub")])
                fns = []
                for jj in range(4):
                    j = grp * 4 + jj
                    for e_ in range(2):
                        rows = slice(64 * e_, 64 * e_ + 64)
                        cs_ = slice(jj * 64, (jj + 1) * 64)
                        cs2 = slice(256 + jj * 64, 256 + (jj + 1) * 64)
                        fns.append(lambda e, rows=rows, cs2=cs2, jj=jj, j=j: e.matmul(ps[rows, C0, cs2], hb[rows, j, :], RT[rows, jj, tokc], start=True, stop=False))
                        fns.append(lambda e, rows=rows, cs_=cs_, cs2=cs2: e.matmul(ps[rows, C0, cs2], Ub[rows, cs_], G2[cu][rows, 1, cs_], start=False, stop=False))
                        fns.append(lambda e, rows=rows, cs_=cs_, cs2=cs2, jj=jj: e.matmul(ps[rows, C0, cs2], VT[rows, c, jj, :], G3[cu][rows, cs_], start=False, stop=True))
                        fns.append(lambda e, rows=rows, cs_=cs_, jj=jj: e.matmul(ps[rows, C0, cs_], BCt[rows, c, jj, :], Ub[rows, cs_], start=True, stop=False))
                        fns.append(lambda e, rows=rows, cs_=cs_, jj=jj: e.matmul(ps[rows, C0, cs_], KCt[rows, c, jj, :], VT[rows, c, jj, :], start=False, stop=True))
                yield S.multi('pe', fns, reads=[("Hb", hbpar, grp), N_("Ub"), N_("G2", cu), N_("G3", cu)] + allj("RT") + allj("VT") + allj("BCt") + allj("KCt"),
                              writes=PSU(C0, 0, 512))
                H4 = Hst[:, grp * 4:(grp + 1) * 4, :]
                yield S.op('dve', lambda e: e.tensor_tensor(out=H4, in0=H4, in1=Wend[:, :, c:c + 1].broadcast_to([128, 4, 64]), op=ALU.mult),
                           reads=[N_("H")] + allj("Wend"), writes=[N_("H")])
                yield S.op('dve', lambda e: e.tensor_tensor(out=H4, in0=v4(ps[:, C0, LO]), in1=H4, op=ALU.add),
                           reads=PSU(C0, 0, 256) + [N_("H")], writes=[N_("H")])
                if POOL_COPY:
                    yield S.op('pool', lambda e: e.tensor_copy(out=hbn[:, grp * 4:(grp + 1) * 4, :], in_=H4),
                               reads=[N_("H")], writes=[("Hb", 1 - hbpar, grp)])
                else:
                    yield S.op('act', lambda e: e.activation(out=hbn[:, grp * 4:(grp + 1) * 4, :], in_=H4, func=AF.Copy),
                               reads=[N_("H")], writes=[("Hb", 1 - hbpar, grp)])
                yield S.op('act', lambda e: e.activation(out=Yw[:, :, tokc], in_=v4(ps[:, C0, HI]), func=AF.Copy),
                           reads=PSU(C0, 256, 512), writes=[N_("Yw")])
                hbpar = 1 - hbpar
            for jj in range(4):
                j = grp * 4 + jj
                col = lambda base, j=j: pv[:, base + j:base + j + 1]
                yc, sq_, yn = L.ptmp
                yield S.op('pe', lambda e: e.matmul(ps[:, C1, LO], bones64, Yw[:, jj, :], start=True, stop=True), reads=[N_("Yw"), "cf"], writes=PSU(C1, 0, 256))
                yield S.op('dve', lambda e: e.tensor_tensor(out=yc, in0=Yw[:, jj, :], in1=ps[:, C1, LO], op=ALU.subtract),
                           reads=[N_("Yw")] + PSU(C1, 0, 256), writes=[N_("yc")])
                yield S.op('pool', lambda e: e.tensor_tensor(out=sq_, in0=yc, in1=yc, op=ALU.mult), reads=[N_("yc")], writes=[N_("sq_")])
                yield S.op('pe', lambda e: e.matmul(ps[:, C1, HI], bones64, sq_, start=True, stop=True), reads=[N_("sq_"), "cf"], writes=PSU(C1, 256, 512))
                yield S.op('act', lambda e: e.activation(out=sq_, in_=ps[:, C1, HI], func=AF.Ln, bias=epsc[:, 1:2]),
                           reads=PSU(C1, 256, 512) + ["epsc"], writes=[N_("sq_")])
                yield S.op('act', lambda e: e.activation(out=sq_, in_=sq_, func=AF.Exp, scale=-0.5), reads=[N_("sq_")], writes=[N_("sq_")])
                yield S.op('pool', lambda e: e.tensor_tensor(out=yn, in0=yc, in1=sq_, op=ALU.mult), reads=[N_("yc"), N_("sq_")], writes=[N_("yn")])
                yield S.op('act', lambda e: e.activation(out=yn, in_=yn, func=AF.Identity, bias=col(PV_LNB), scale=col(PV_LNW)),
                           reads=[N_("yn"), "pv"], writes=[N_("yn")])
                yield S.op('pool', lambda e: e.tensor_tensor(out=yn, in0=yn, in1=BON[:, jj, :], op=ALU.add), reads=[N_("yn"), N_("BON", wp, jj)], writes=[N_("yn")])
                yield S.op('pool', lambda e: e.tensor_tensor(out=mixr[:, jj, :], in0=yn, in1=grs[:, jj, :], op=ALU.mult),
                           reads=[N_("yn"), N_("grs", wp, jj)], writes=[N_("mixr", jj)])
                yield S.dma('pool', scrM[8 + j][:, wtok], mixr[:, jj, :], reads=[N_("mixr", jj)], writes=[("scrM", 8 + j, wn)], key=N_("rm", jj))
            chain_done[grp] = wn + 1

    prep_done = [0, 0]
    chain_done = [0, 0]
    gens = [prep_stream(0), prep_stream(1), chain_stream(0), chain_stream(1)]
    while gens:
        for gen in list(gens):
            try:
                next(gen)
            except StopIteration:
                gens.remove(gen)
    S.barrier()

    o = 0
    wo, o = carve(o, [128, 16, 2048], BF16)
    mt, o = carve(o, [128, 2, 16, 128], BF16)
    xt2, o = carve(o, [128, 2, D], F32)
    ot, o = carve(o, [128, 2, D], F32)
    assert o <= ARENA_W, o
    for nb in range(4):
        S.dma('pool', wo[:, :, nb * 512:(nb + 1) * 512], wout_d[nb], writes=[("wo", nb)], key=("wo", nb))
    allM = []
    def o_load(tt):
        u = tt % 2
        tsl = slice(tt * 128, (tt + 1) * 128)
        S.dma('sp', mt[:, u], scrM[:, :, tsl].rearrange("c p t -> p c t"), reads=allM, writes=[("mt", u)], key=("om", u))
        S.dma('sp', xt2[:, u, :], x_d[tsl, :], writes=[("xt2", u)], key=("ox", u))

    o_load(0)
    for tt in range(NT):
        u = tt % 2
        tsl = slice(tt * 128, (tt + 1) * 128)
        if tt + 1 < NT:
            o_load(tt + 1)
        for nb in range(4):
            bank = 4 * u + nb
            fns = [lambda e, kc=kc, bank=bank, nb=nb: e.matmul(ps[:, bank, :], mt[:, u, kc, :], wo[:, kc, nb * 512:(nb + 1) * 512],
                                                               start=(kc == 0), stop=(kc == 15)) for kc in range(16)]
            S.multi('pe', fns, reads=[("mt", u), ("wo", nb)], writes=PSU(bank, 0, 512))
        pso = ps[:, 4 * u:4 * u + 4, :]
        allb = [r for b in range(4 * u, 4 * u + 4) for r in PSU(b, 0, 512)]
        S.op('act', lambda e: e.activation(out=ot[:, u, :].rearrange("p (a b) -> p a b", a=4), in_=pso, func=AF.Square,
                                           scale=float(D) ** -0.5, accum_out=small[:, 32 + tt:33 + tt]),
             reads=allb, writes=[("ot", u), ("ss2", tt)])
        S.op('act', lambda e: e.activation(out=small[:, 48 + tt:49 + tt], in_=small[:, 32 + tt:33 + tt], func=AF.Sqrt, bias=epsc[:, 0:1]),
             reads=[("ss2", tt), "epsc"], writes=[("rs2", tt)])
        S.op('dve', lambda e: e.reciprocal(out=small[:, 48 + tt:49 + tt], in_=small[:, 48 + tt:49 + tt]),
             reads=[("rs2", tt)], writes=[("rs2", tt)])
        S.op('dve', lambda e: e.scalar_tensor_tensor(out=ot[:, u, :].rearrange("p (a b) -> p a b", a=4), in0=pso,
                                                     scalar=small[:, 48 + tt:49 + tt], in1=gpb[:].rearrange("p (a b) -> p a b", a=4),
                                                     op0=ALU.mult, op1=ALU.mult),
             reads=allb + [("rs2", tt), "gpb"], writes=[("ot", u)])
        S.op('pool', lambda e: e.tensor_tensor(out=ot[:, u, :], in0=ot[:, u, :], in1=xt2[:, u, :], op=ALU.add),
             reads=[("ot", u), ("xt2", u)], writes=[("ot", u)])
        S.dma('pool', out_d[tsl, :], ot[:, u, :], reads=[("ot", u)], writes=[("out", tt)], key=("oo", u))
    S.barrier()
    es.close()
    return nc, S


_CACHE = {}


def _host_consts():
    cbm = np.zeros((128, CB_N), np.float32)
    cbm[:, CB_IDENT:CB_IDENT + 128] = np.eye(128, dtype=np.float32)
    s = np.arange(128)[:, None]
    q = np.arange(128)[None, :]
    prev = (s > q).astype(np.float32)
    cur = (s <= q).astype(np.float32)
    cbm[:, CB_MASKA:CB_MASKA + 128] = prev
    cbm[:, CB_MASKA + 128:CB_MASKA + 256] = cur
    cbm[:, CB_MASKA0 + 128:CB_MASKA0 + 256] = cur
    p = (np.arange(128) % 64)[:, None]
    f = np.arange(64)[None, :]
    cbm[:, CB_MST:CB_MST + 64] = (p < f)
    cbm[:, CB_MS:CB_MS + 64] = (f < p)
    cbm[:, CB_MIT:CB_MIT + 64] = (p <= f)
    cbm[:, CB_I64:CB_I64 + 64] = (p == f)
    cbm[:, CB_OL0:CB_OL0 + 64] = 1.0
    cbm[:, CB_OL1 + 64:CB_OL1 + 128] = 1.0
    cfm = np.zeros((128, CF_N), np.float32)
    blk = (np.arange(128)[:, None] // 64 == np.arange(128)[None, :] // 64).astype(np.float32)
    cfm[:, CF_BONES:CF_BONES + 128] = blk
    cfm[:, CF_BONES64:CF_BONES64 + 128] = blk / 64.0
    rm = np.ones(512, np.float32)
    rm[::64] = 0.0
    cfm[:, CF_RESET:CF_RESET + 512] = rm[None, :]
    cfm[:, CF_ONES:CF_ONES + 128] = 1.0
    return cbm, cfm


def _colidx():
    idx = []
    idx += [np.arange(1024, 1152), np.arange(1152, 1280), np.arange(1024, 1152), np.arange(1152, 1280)]
    idx.append(np.arange(1280, 1536))
    idx.append(np.arange(0, 1024))
    idx.append(np.arange(1536, 2560))
    R0 = 2560
    idx.append(np.arange(R0 + 3072, R0 + 3200))
    for j in range(8):
        for base in (0, 1024, 2048, 3200):
            idx.append(np.arange(R0 + base + j * 128, R0 + base + (j + 1) * 128))
    idx = np.concatenate(idx)
    assert idx.size == NBLK * 128
    return idx


def kernel(x, c, w_ada, b_ada, pre_norm_g, post_norm_g, w_in, w_out, attn_sinks,
           rwkv_mu, rwkv_w0, rwkv_w_up, rwkv_a0, rwkv_a_up, rwkv_k_k, rwkv_k_a,
           rwkv_r_k, rwkv_ln_w, rwkv_ln_b):
    f = lambda a: np.ascontiguousarray(np.asarray(a, dtype=np.float32))
    x = f(x); c = f(c)
    B = x.shape[0]
    if 'nc' not in _CACHE:
        _CACHE['nc'] = build_program()
    nc, S = _CACHE['nc']
    wada_h = f(np.asarray(w_ada[0]).reshape(16, 128, 12, 512).transpose(2, 1, 0, 3))
    win_h = f(np.asarray(w_in[0])[:, _colidx()].reshape(16, 128, NBLK, 128).transpose(2, 1, 0, 3))
    wout_h = f(np.asarray(w_out[0]).reshape(16, 128, 4, 512).transpose(2, 1, 0, 3))
    col = lambda v: np.asarray(v, np.float32).reshape(-1, 128).T
    pvec = np.zeros((128, PV_N), np.float32)
    pvec[:, PV_PREG:PV_PREG + 16] = col(pre_norm_g[0])
    mu = np.asarray(rwkv_mu[0], np.float32)
    pvec[:, PV_MU] = mu[3072:3200]
    for j in range(8):
        for r_ in range(3):
            pvec[:, PV_MU + 1 + 3 * j + r_] = mu[r_ * 1024 + j * 128:r_ * 1024 + (j + 1) * 128]
    pvec[:, PV_W0:PV_W0 + 8] = col(rwkv_w0[0])
    pvec[:, PV_A0:PV_A0 + 8] = col(rwkv_a0[0])
    pvec[:, PV_KK:PV_KK + 8] = col(rwkv_k_k[0])
    pvec[:, PV_KA:PV_KA + 8] = col(rwkv_k_a[0])
    pvec[:, PV_RK:PV_RK + 8] = col(np.asarray(rwkv_r_k[0]).reshape(-1))
    pvec[:, PV_LNW:PV_LNW + 8] = col(rwkv_ln_w[0])
    pvec[:, PV_LNB:PV_LNB + 8] = col(rwkv_ln_b[0])
    lora_h = f(np.concatenate([np.asarray(rwkv_w_up[0]), np.asarray(rwkv_a_up[0])], axis=0))
    sinks_h = f(np.repeat(np.asarray(attn_sinks[0], np.float32), 64)[None, :])
    sinkc_h = f(np.asarray(attn_sinks[0], np.float32).reshape(8, 2)[:, np.arange(128) // 64].T)
    cbm, cfm = _host_consts()
    bada_h = f(np.asarray(b_ada[0])[None, :])
    postg_h = f(np.asarray(post_norm_g[0])[None, :])
    in_maps = []
    for b in range(B):
        in_maps.append({
            "x": x[b], "cvec": f(c[b].reshape(16, 128).T), "wada": wada_h, "bada": bada_h, "win": win_h,
            "wout": wout_h, "pvec": pvec, "lora": lora_h, "postg": postg_h, "sinksrep": sinks_h, "sinkcol": sinkc_h,
            "constb": cbm, "constf": cfm,
        })
    res = run_bass_kernel_spmd(nc, in_maps, core_ids=list(range(B)))
    return np.stack([np.asarray(r["out"], dtype=np.float32) for r in res.results], axis=0)
```
